# Optimizing a Trainium2 kernel written in Bass

```python
import math
import jax, jax.numpy as jnp
from jax import lax
import numpy as np

D_MODEL = 2048
BATCH = 16
SEQ = 2048
DEPTH = 4

D_MIX = D_MODEL
SSD_INNER = D_MIX // 2
SSD_HEAD_DIM = 64
SSD_HEADS = SSD_INNER // SSD_HEAD_DIM
SSD_GROUPS = 2
SSD_HEADS_PER_GROUP = SSD_HEADS // SSD_GROUPS
SSD_STATE = 128
SSD_CONV = 4
SSD_CHUNK = 128
SSD_CONV_DIM = SSD_INNER + 2 * SSD_GROUPS * SSD_STATE
DT_MIN = 0.001
DT_MAX = 0.1
MLA_V_HEAD = 128
MLA_HEADS = (D_MIX - SSD_INNER) // MLA_V_HEAD
MLA_NOPE = 128
MLA_ROPE = 64
MLA_QK_HEAD = MLA_NOPE + MLA_ROPE
Q_LORA = D_MODEL // 4
KV_LORA = D_MODEL // 4
ROPE_THETA = 10000.0
Q_BLOCK = 128
MEM_LEN = 256
X_HEADS = 4
X_HEAD_DIM = 128
X_INNER = X_HEADS * X_HEAD_DIM
FFN_HIDDEN = ((8 * D_MODEL + 3 * 256 - 1) // (3 * 256)) * 256
IN_COLS = SSD_INNER + SSD_CONV_DIM + SSD_HEADS + Q_LORA + KV_LORA + MLA_ROPE
RMS_EPS = 1e-6

kernel_name = 'hymba_ssd_mla_memory_hybrid'


def rms_norm(x, g):
    xf = x.astype(jnp.float32)
    y = xf * lax.rsqrt(jnp.mean(xf * xf, axis=-1, keepdims=True) + RMS_EPS)
    return (y * g.astype(jnp.float32)).astype(x.dtype)


def rope_tables(positions):
    inv_freq = 1.0 / (ROPE_THETA ** (jnp.arange(0, MLA_ROPE, 2, dtype=jnp.float32) / MLA_ROPE))
    ang = positions.astype(jnp.float32)[..., None] * inv_freq
    return jnp.cos(ang), jnp.sin(ang)


def apply_rope(x, cos, sin):
    half = x.shape[-1] // 2
    x1 = x[..., :half].astype(jnp.float32)
    x2 = x[..., half:].astype(jnp.float32)
    c = cos[:, :, None, :]
    s = sin[:, :, None, :]
    return jnp.concatenate([x1 * c - x2 * s, x2 * c + x1 * s], axis=-1).astype(x.dtype)


def causal_depthwise_conv(u, w, b):
    k = w.shape[0]
    y = lax.conv_general_dilated(u, w[:, None, :].astype(u.dtype), window_strides=(1,),
                                 padding=[(k - 1, 0)],
                                 dimension_numbers=('NWC', 'WIO', 'NWC'),
                                 feature_group_count=u.shape[-1])
    return y + b.astype(u.dtype)


def _swap(t):
    return jnp.transpose(t, (0, 3, 4, 1, 2))


def ssd_scan(xdt, a, b_in, c_in):
    bsz, seq = xdt.shape[:2]
    nc = seq // SSD_CHUNK
    xc = xdt.reshape(bsz, nc, SSD_CHUNK, SSD_GROUPS, SSD_HEADS_PER_GROUP, SSD_HEAD_DIM)
    bc = b_in.reshape(bsz, nc, SSD_CHUNK, SSD_GROUPS, SSD_STATE)
    cc = c_in.reshape(bsz, nc, SSD_CHUNK, SSD_GROUPS, SSD_STATE)
    ac = _swap(a.reshape(bsz, nc, SSD_CHUNK, SSD_GROUPS, SSD_HEADS_PER_GROUP))
    a_cs = jnp.cumsum(ac, axis=-1)
    causal = jnp.tril(jnp.ones((SSD_CHUNK, SSD_CHUNK), dtype=bool))
    decay_in = jnp.exp(jnp.where(causal, a_cs[..., :, None] - a_cs[..., None, :], -jnp.inf))
    cb = jnp.einsum('bclgn,bcsgn->bgcls', cc, bc)
    y_diag = jnp.einsum('bgecls,bcsgep->bclgep', cb[:, :, None] * decay_in, xc)
    decay_to_end = jnp.exp(a_cs[..., -1:] - a_cs)
    chunk_states = jnp.einsum('bclgn,bclgep->bcgepn', bc, xc * _swap(decay_to_end)[..., None])
    chunk_states = jnp.concatenate([jnp.zeros_like(chunk_states[:, :1]), chunk_states], axis=1)
    tot = jnp.pad(a_cs[..., -1], ((0, 0), (0, 0), (0, 0), (1, 0)))
    tot_cs = jnp.cumsum(tot, axis=-1)
    causal_c = jnp.tril(jnp.ones((nc + 1, nc + 1), dtype=bool))
    decay_chunk = jnp.exp(jnp.where(causal_c, tot_cs[..., :, None] - tot_cs[..., None, :], -jnp.inf))
    states_in = jnp.einsum('bgezc,bcgepn->bzgepn', decay_chunk, chunk_states)[:, :-1]
    y_off = jnp.einsum('bclgn,bcgepn->bclgep', cc, states_in) * _swap(jnp.exp(a_cs))[..., None]
    return (y_diag + y_off).reshape(bsz, seq, SSD_GROUPS, SSD_HEADS_PER_GROUP, SSD_HEAD_DIM)


def ssd_mixer(z, xbc, dt_raw, conv_w, conv_b, dt_bias, a_log, d_skip, norm_g):
    bsz, seq, _ = z.shape
    gn = SSD_GROUPS * SSD_STATE
    xbc = jax.nn.silu(causal_depthwise_conv(xbc, conv_w, conv_b))
    xs = xbc[..., :SSD_INNER].reshape(bsz, seq, SSD_GROUPS, SSD_HEADS_PER_GROUP, SSD_HEAD_DIM)
    b_in = xbc[..., SSD_INNER:SSD_INNER + gn].reshape(bsz, seq, SSD_GROUPS, SSD_STATE)
    c_in = xbc[..., SSD_INNER + gn:].reshape(bsz, seq, SSD_GROUPS, SSD_STATE)
    dt = jax.nn.softplus(dt_raw.astype(jnp.float32) + dt_bias.astype(jnp.float32))
    dt = dt.reshape(bsz, seq, SSD_GROUPS, SSD_HEADS_PER_GROUP)
    a = dt * (-jnp.exp(a_log.astype(jnp.float32))).reshape(SSD_GROUPS, SSD_HEADS_PER_GROUP)
    y = ssd_scan(xs * dt[..., None], a, b_in, c_in)
    y = y + xs * d_skip.reshape(SSD_GROUPS, SSD_HEADS_PER_GROUP, 1)
    y = (y.reshape(bsz, seq, SSD_INNER) * jax.nn.silu(z)).reshape(bsz, seq, SSD_GROUPS, SSD_INNER // SSD_GROUPS)
    y = rms_norm(y, norm_g.reshape(SSD_GROUPS, SSD_INNER // SSD_GROUPS))
    return y.reshape(bsz, seq, SSD_INNER).astype(z.dtype)


def causal_block_attention(q, k, v, scale):
    seq = q.shape[1]
    outs = []
    for blk in range(seq // Q_BLOCK):
        q0 = blk * Q_BLOCK
        kend = q0 + Q_BLOCK
        s = jnp.einsum('bqhd,bkhd->bhqk', q[:, q0:kend], k[:, :kend]).astype(jnp.float32) * scale
        mask = (q0 + jnp.arange(Q_BLOCK))[:, None] >= jnp.arange(kend)[None, :]
        p = jax.nn.softmax(jnp.where(mask, s, -jnp.inf), axis=-1)
        outs.append(jnp.einsum('bhqk,bkhd->bqhd', p.astype(v.dtype), v[:, :kend]))
    return jnp.concatenate(outs, axis=1)


def mla_mixer(q_a, kv_a, k_rope, cos, sin, q_a_norm_g, w_q_b, kv_a_norm_g, w_kv_b, q_norm_g, k_norm_g):
    bsz, seq, _ = q_a.shape
    q = (rms_norm(q_a, q_a_norm_g) @ w_q_b).reshape(bsz, seq, MLA_HEADS, MLA_QK_HEAD)
    kv = (rms_norm(kv_a, kv_a_norm_g) @ w_kv_b).reshape(bsz, seq, MLA_HEADS, MLA_NOPE + MLA_V_HEAD)
    k_nope = kv[..., :MLA_NOPE]
    v = kv[..., MLA_NOPE:]
    k_pe = jnp.broadcast_to(k_rope[:, :, None, :], (bsz, seq, MLA_HEADS, MLA_ROPE))
    k = jnp.concatenate([k_nope, k_pe], axis=-1)
    q = rms_norm(q, q_norm_g)
    k = rms_norm(k, k_norm_g)
    q = jnp.concatenate([q[..., :MLA_NOPE], apply_rope(q[..., MLA_NOPE:], cos, sin)], axis=-1)
    k = jnp.concatenate([k[..., :MLA_NOPE], apply_rope(k[..., MLA_NOPE:], cos, sin)], axis=-1)
    o = causal_block_attention(q, k, v, MLA_QK_HEAD ** -0.5)
    return o.reshape(bsz, seq, MLA_HEADS * MLA_V_HEAD)


def memory_cross_attention(h, m, w_q, w_k, w_v, q_norm_g, k_norm_g, w_o):
    bsz, seq, _ = h.shape
    mlen = m.shape[1]
    q = rms_norm((h @ w_q).reshape(bsz, seq, X_HEADS, X_HEAD_DIM), q_norm_g)
    k = rms_norm((m @ w_k).reshape(bsz, mlen, X_HEADS, X_HEAD_DIM), k_norm_g)
    v = (m @ w_v).reshape(bsz, mlen, X_HEADS, X_HEAD_DIM)
    s = jnp.einsum('bshd,bmhd->bhsm', q, k).astype(jnp.float32) * (X_HEAD_DIM ** -0.5)
    p = jax.nn.softmax(s, axis=-1)
    o = jnp.einsum('bhsm,bmhd->bshd', p.astype(v.dtype), v)
    return o.reshape(bsz, seq, X_INNER) @ w_o


def swiglu(h, w_gate, w_up, w_down):
    return (jax.nn.silu(h @ w_gate) * (h @ w_up)) @ w_down


def setup_inputs(seed: int = 0) -> dict:
    key = jax.random.key(seed)
    ks = jax.random.split(key, 32)
    f32 = jnp.float32
    L = DEPTH

    def nrm(k, shape, fan_in):
        return jax.random.normal(k, shape, f32) * (fan_in ** -0.5)

    def gain(k, shape):
        return 1.0 + 0.02 * jax.random.normal(k, shape, f32)

    x = jax.random.normal(ks[0], (BATCH, SEQ, D_MODEL), f32)
    mem = jax.random.normal(ks[1], (BATCH, MEM_LEN, D_MODEL), f32)
    positions = (jnp.arange(SEQ, dtype=jnp.int32)[None, :]
                 + jax.random.randint(ks[2], (BATCH, 1), 0, 4096, dtype=jnp.int32))
    dt0 = jnp.exp(jax.random.uniform(ks[6], (L, SSD_HEADS), f32, math.log(DT_MIN), math.log(DT_MAX)))
    dt_bias = dt0 + jnp.log(-jnp.expm1(-dt0))
    a_log = jnp.log(jax.random.uniform(ks[7], (L, SSD_HEADS), f32, 1.0, 16.0))
    return {
        'x': x,
        'mem': mem,
        'positions': positions,
        'attn_norm_g': gain(ks[3], (L, D_MODEL)),
        'w_in': nrm(ks[4], (L, D_MODEL, IN_COLS), D_MODEL),
        'conv_w': nrm(ks[5], (L, SSD_CONV, SSD_CONV_DIM), SSD_CONV),
        'conv_b': 0.01 * jax.random.normal(ks[8], (L, SSD_CONV_DIM), f32),
        'dt_bias': dt_bias,
        'a_log': a_log,
        'd_skip': gain(ks[9], (L, SSD_HEADS)),
        'ssd_norm_g': gain(ks[10], (L, SSD_INNER)),
        'q_a_norm_g': gain(ks[11], (L, Q_LORA)),
        'w_q_b': nrm(ks[12], (L, Q_LORA, MLA_HEADS * MLA_QK_HEAD), Q_LORA),
        'kv_a_norm_g': gain(ks[13], (L, KV_LORA)),
        'w_kv_b': nrm(ks[14], (L, KV_LORA, MLA_HEADS * (MLA_NOPE + MLA_V_HEAD)), KV_LORA),
        'mla_q_norm_g': gain(ks[15], (L, MLA_QK_HEAD)),
        'mla_k_norm_g': gain(ks[16], (L, MLA_QK_HEAD)),
        'w_out': nrm(ks[17], (L, D_MIX, D_MODEL), D_MIX),
        'xattn_norm_g': gain(ks[18], (L, D_MODEL)),
        'mem_norm_g': gain(ks[19], (L, D_MODEL)),
        'w_xq': nrm(ks[20], (L, D_MODEL, X_INNER), D_MODEL),
        'w_xk': nrm(ks[21], (L, D_MODEL, X_INNER), D_MODEL),
        'w_xv': nrm(ks[22], (L, D_MODEL, X_INNER), D_MODEL),
        'xq_norm_g': gain(ks[23], (L, X_HEAD_DIM)),
        'xk_norm_g': gain(ks[24], (L, X_HEAD_DIM)),
        'w_xo': nrm(ks[25], (L, X_INNER, D_MODEL), X_INNER),
        'ffn_norm_g': gain(ks[26], (L, D_MODEL)),
        'w_gate': nrm(ks[27], (L, D_MODEL, FFN_HIDDEN), D_MODEL),
        'w_up': nrm(ks[28], (L, D_MODEL, FFN_HIDDEN), D_MODEL),
        'w_down': nrm(ks[29], (L, FFN_HIDDEN, D_MODEL), FFN_HIDDEN),
    }


def reference(x, mem, positions, attn_norm_g, w_in, conv_w, conv_b, dt_bias, a_log, d_skip, ssd_norm_g,
              q_a_norm_g, w_q_b, kv_a_norm_g, w_kv_b, mla_q_norm_g, mla_k_norm_g, w_out,
              xattn_norm_g, mem_norm_g, w_xq, w_xk, w_xv, xq_norm_g, xk_norm_g, w_xo,
              ffn_norm_g, w_gate, w_up, w_down):
    cos, sin = rope_tables(positions)
    c0 = SSD_INNER
    c1 = c0 + SSD_CONV_DIM
    c2 = c1 + SSD_HEADS
    c3 = c2 + Q_LORA
    c4 = c3 + KV_LORA
    for l in range(DEPTH):
        h = rms_norm(x, attn_norm_g[l])
        proj = h @ w_in[l]
        y_ssd = ssd_mixer(proj[..., :c0], proj[..., c0:c1], proj[..., c1:c2],
                          conv_w[l], conv_b[l], dt_bias[l], a_log[l], d_skip[l], ssd_norm_g[l])
        y_mla = mla_mixer(proj[..., c2:c3], proj[..., c3:c4], proj[..., c4:], cos, sin,
                          q_a_norm_g[l], w_q_b[l], kv_a_norm_g[l], w_kv_b[l],
                          mla_q_norm_g[l], mla_k_norm_g[l])
        mixed = jnp.concatenate([y_ssd, y_mla.astype(y_ssd.dtype)], axis=-1) @ w_out[l]
        x = x + mixed.astype(x.dtype)
        h = rms_norm(x, xattn_norm_g[l])
        m = rms_norm(mem, mem_norm_g[l])
        x = x + memory_cross_attention(h, m, w_xq[l], w_xk[l], w_xv[l],
                                       xq_norm_g[l], xk_norm_g[l], w_xo[l]).astype(x.dtype)
        h = rms_norm(x, ffn_norm_g[l])
        x = x + swiglu(h, w_gate[l], w_up[l], w_down[l]).astype(x.dtype)
    return x
```

```python
import numpy as np
from contextlib import ExitStack
import concourse.bass as bass
import concourse.mybir as mybir
from concourse.bass_utils import run_bass_kernel_spmd

F32 = mybir.dt.float32
BF16 = mybir.dt.bfloat16
I32 = mybir.dt.int32
AF = mybir.ActivationFunctionType
ALU = mybir.AluOpType
P = 128
EPS = 1e-6


def _dsize(dt):
    return 2 if dt == BF16 else 4


class Op:
    __slots__ = ("eng", "fn", "deps", "dma", "seq", "sem", "semval", "signal", "gid", "seg", "dur", "nbytes",
                 "bg", "prev", "rt", "fin", "nd", "users")


class Tile:
    _n = 0

    def __init__(self, ap, name):
        self.ap = ap
        Tile._n += 1
        self.key = (name, Tile._n)

    def __getitem__(self, idx):
        return self.ap[idx]


def _key(r):
    if isinstance(r, Tile):
        return r.key
    if isinstance(r, tuple) and isinstance(r[0], Tile):
        return (r[0].key,) + tuple(r[1:])
    return r


class Sched:
    ENGS = ("pe", "act", "dve", "pool", "sp")
    QSEMS = (("sp", 32), ("act", 8), ("pool", 8))
    HOP = 0.4
    DMA_BW = 120e3
    DMA_LAT = 1.8

    def __init__(self):
        self.all = []
        self.last_w = {}
        self.readers = {}
        self.seg = 0
        import os
        self.reorder = os.environ.get("KREORDER", "1") == "1"

    def add(self, eng, fn, r=(), w=(), dma=False, dur=0.2, nbytes=0, bg=False):
        op = Op()
        op.eng, op.fn, op.dma, op.dur, op.nbytes, op.bg = eng, fn, dma, dur, nbytes, bg
        op.signal = False
        op.seq = op.sem = op.semval = op.prev = None
        deps = set()
        for x in r:
            k = _key(x)
            for lw in self.last_w.get(k, ()):
                deps.add(lw)
        cowrite = set()
        for x in w:
            k = _key(x)
            lws = self.last_w.get(k, ())
            rds = self.readers.get(k, ())
            if dma and lws and not rds and all(o.dma for o in lws):
                cowrite.add(k)
                for lw in lws:
                    deps |= lw.deps
                continue
            for lw in lws:
                deps.add(lw)
            for rd in rds:
                deps.add(rd)
        deps.discard(op)
        op.deps = deps
        for x in r:
            self.readers.setdefault(_key(x), []).append(op)
        for x in w:
            k = _key(x)
            if k in cowrite:
                self.last_w[k] = list(self.last_w[k]) + [op]
            else:
                self.last_w[k] = [op]
            self.readers[k] = []
        op.gid = len(self.all)
        op.seg = self.seg
        self.all.append(op)
        return op

    def barrier(self):
        self.seg += 1
        self.last_w = {}
        self.readers = {}

    def pe(self, fn, r=(), w=(), **kw):
        return self.add("pe", fn, r, w, **kw)

    def act(self, fn, r=(), w=(), **kw):
        return self.add("act", fn, r, w, **kw)

    def dve(self, fn, r=(), w=(), **kw):
        return self.add("dve", fn, r, w, **kw)

    def pool(self, fn, r=(), w=(), **kw):
        return self.add("pool", fn, r, w, **kw)

    def dma(self, fn, r=(), w=(), q="sp", **kw):
        return self.add(q, fn, r, w, dma=True, **kw)

    def _schedule(self, ops):
        import heapq
        order = {e: [] for e in self.ENGS}
        import os
        lo, hi = int(os.environ.get("KR_LO", "0")), int(os.environ.get("KR_HI", "100000"))
        if not self.reorder or not (lo <= ops[0].seg <= hi):
            for op in ops:
                order[op.eng].append(op)
            return order
        for op in ops:
            op.users = []
            op.nd = 0
            op.rt = 0.0
        seg = ops[0].seg
        for op in ops:
            for d in op.deps:
                if d.seg == seg:
                    d.users.append(op)
                    op.nd += 1
        NC = 4
        heap = {e: [] for e in self.ENGS}
        front = {e: [] for e in self.ENGS}

        def push(op):
            ent = (op.gid + (10 ** 9 if op.bg else 0), op.gid, op)
            f = front[op.eng]
            if len(f) < NC:
                f.append(ent)
            else:
                m = max(f)
                if ent < m:
                    f.remove(m)
                    f.append(ent)
                    heapq.heappush(heap[op.eng], m)
                else:
                    heapq.heappush(heap[op.eng], ent)

        for op in ops:
            if op.nd == 0:
                push(op)
        t_eng = {e: 0.0 for e in self.ENGS}
        pipe = 0.0
        remaining = len(ops)
        while remaining:
            best = None
            for e in self.ENGS:
                for ent in front[e]:
                    st = max(t_eng[e], ent[2].rt)
                    key = (st, ent[0])
                    if best is None or key < best[0]:
                        best = (key, e, ent)
            (st, _), e, ent = best
            op = ent[2]
            front[e].remove(ent)
            if heap[e]:
                front[e].append(heapq.heappop(heap[e]))
            if op.dma:
                t_eng[e] = st + 0.06
                p0 = max(st, pipe)
                xfer = op.nbytes / self.DMA_BW
                pipe = p0 + xfer
                op.fin = p0 + xfer + self.DMA_LAT
            else:
                op.fin = st + op.dur
                t_eng[e] = op.fin
            order[e].append(op)
            remaining -= 1
            for u in op.users:
                u.nd -= 1
                hop = 0.0 if (u.eng == "pe" and op.eng == "pe" and not u.dma) else self.HOP
                if op.fin + hop > u.rt:
                    u.rt = op.fin + hop
                if u.nd == 0:
                    push(u)
        return order

    def emit(self, nc, es):
        ENGS = self.ENGS
        nseg = self.seg + 1
        segs = [[] for _ in range(nseg)]
        for op in self.all:
            segs[op.seg].append(op)
        final = {e: [] for e in ENGS}
        segpos = {e: [] for e in ENGS}
        for si, ops in enumerate(segs):
            if not ops:
                continue
            order = self._schedule(ops)
            for e in ENGS:
                if order[e]:
                    segpos[e].append((len(final[e]), si))
                    final[e].extend(order[e])
        qbase = {}
        nd = 0
        for q, n in self.QSEMS:
            qbase[q] = (nd, n)
            nd += n
        dma_total = [0] * nd
        for q, n in self.QSEMS:
            base = qbase[q][0]
            dl = [op for op in final[q] if op.dma]
            for i, op in enumerate(dl):
                op.sem = base + (i % n)
                op.semval = 16 * (i // n + 1)
                op.prev = dl[i - n] if i >= n else None
                dma_total[op.sem] = op.semval
        for op in self.all:
            for d in op.deps:
                if not d.dma and not (d.eng == "pe" and op.eng == "pe" and not op.dma):
                    d.signal = True
        for e in ENGS:
            for (pos, si), nxt in zip(segpos[e], segpos[e][1:] + [(len(final[e]), None)]):
                for op in reversed(final[e][pos:nxt[0]]):
                    if not op.dma:
                        op.signal = True
                        break
        for e in ENGS:
            n = 0
            for op in final[e]:
                if (not op.dma) and op.signal:
                    n += 1
                    op.seq = n
        bar_e = [dict() for _ in range(nseg + 1)]
        bar_d = [dict() for _ in range(nseg + 1)]
        cur_e, cur_d = {}, {}
        byseg_e = [dict() for _ in range(nseg)]
        byseg_d = [dict() for _ in range(nseg)]
        for e in ENGS:
            for op in final[e]:
                if op.dma:
                    if byseg_d[op.seg].get(op.sem, 0) < op.semval:
                        byseg_d[op.seg][op.sem] = op.semval
                elif op.signal:
                    if byseg_e[op.seg].get(e, 0) < op.seq:
                        byseg_e[op.seg][e] = op.seq
        for si in range(nseg):
            bar_e[si] = dict(cur_e)
            bar_d[si] = dict(cur_d)
            for k, v in byseg_e[si].items():
                cur_e[k] = max(cur_e.get(k, 0), v)
            for k, v in byseg_d[si].items():
                cur_d[k] = max(cur_d.get(k, 0), v)
        esem = {e: es.enter_context(nc.semaphore("sem_" + e)) for e in ENGS}
        dsem = [es.enter_context(nc.semaphore("dsem%d" % i)) for i in range(nd)]
        block = es.enter_context(nc.Block())
        sections = {"pe": block.tensor, "act": block.scalar, "dve": block.vector,
                    "pool": block.gpsimd, "sp": block.sync}

        def make(ename):
            def body(eng):
                seen_e = {}
                seen_d = {}

                def wait_e(en, v):
                    if seen_e.get(en, 0) < v:
                        eng.wait_ge(esem[en], v)
                        seen_e[en] = v

                def wait_d(s_, v):
                    if seen_d.get(s_, 0) < v:
                        eng.wait_ge(dsem[s_], v)
                        seen_d[s_] = v

                starts = dict(segpos[ename])
                for i, op in enumerate(final[ename]):
                    if i in starts and starts[i] > 0:
                        si = starts[i]
                        for en, v in bar_e[si].items():
                            wait_e(en, v)
                        for s_, v in bar_d[si].items():
                            wait_d(s_, v)
                    need_e = {}
                    need_d = {}
                    for d in op.deps:
                        if d.seg != op.seg:
                            continue
                        if d.dma:
                            if need_d.get(d.sem, 0) < d.semval:
                                need_d[d.sem] = d.semval
                        else:
                            if d.eng == "pe" and ename == "pe" and not op.dma:
                                continue
                            if need_e.get(d.eng, 0) < d.seq:
                                need_e[d.eng] = d.seq
                    if op.dma and op.prev is not None:
                        if need_d.get(op.prev.sem, 0) < op.prev.semval:
                            need_d[op.prev.sem] = op.prev.semval
                    for en, v in need_e.items():
                        wait_e(en, v)
                    for s_, v in need_d.items():
                        wait_d(s_, v)
                    inst = op.fn(eng)
                    if op.dma:
                        inst.then_inc(dsem[op.sem], 16)
                    elif op.signal:
                        inst.then_inc(esem[ename], 1)
                if ename == "sp":
                    for s_ in range(nd):
                        if dma_total[s_]:
                            eng.wait_ge(dsem[s_], dma_total[s_])
            return body

        for e in ENGS:
            sections[e](make(e))


class Arena:
    def __init__(self, ap, ncol):
        self.ap = ap
        self.ncol = ncol
        self.off = 0
        self.floor = 0

    def reset(self):
        self.off = self.floor

    def freeze(self):
        self.floor = self.off

    def alloc(self, name, free_shape, dt=F32, parts=P):
        n = int(np.prod(free_shape))
        words = (n * _dsize(dt) + 3) // 4
        assert self.off + words <= self.ncol, ("SBUF arena overflow", name, self.off, words, self.ncol)
        ap = self.ap[0:parts, self.off:self.off + words]
        self.off += words
        if dt != F32:
            ap = ap.bitcast(dt)[:, 0:n]
        if len(free_shape) == 2:
            ap = ap.rearrange("p (a b) -> p a b", a=free_shape[0])
        elif len(free_shape) == 3:
            ap = ap.rearrange("p (a b c) -> p a b c", a=free_shape[0], b=free_shape[1])
        return Tile(ap, name)


D = 2048
KD = 16
SSD_INNER = 1024
NHS = 16
HD = 64
NST = 128
CONV_DIM = 1536
MLA_H = 8
Q_LORA = 512
KV_LORA = 512
XH = 4
MEM = 256
FF = 5632
KF = 44
IN_COLS = 3664
C0, C1, C2, C3, C4 = 1024, 2560, 2576, 3088, 3600
TT = 512


class Cfg:
    def __init__(self, NS=2, S=2048, L=4, phases=("mix", "xattn", "ffn")):
        self.NS, self.S, self.L = NS, S, L
        self.phases = tuple(phases)


PI = float(np.pi)


class Builder:
    def __init__(self, cfg):
        self.cfg = cfg
        self.nc = bass.Bass("TRN2", target_bir_lowering=False)
        self.S = Sched()
        self.dram = {}
        self.ev_i = 0

    def din(self, name, shape, dt=F32):
        t = self.nc.dram_tensor(name, list(shape), dt, kind="ExternalInput").ap()
        self.dram[name] = t
        return t

    def dscr(self, name, shape, dt=F32):
        return self.nc.dram_tensor(name, list(shape), dt, kind="Internal").ap()

    def ps(self, b):
        return self.psum[:, b, :]

    def psb(self, b):
        return self.psum[:, b, :].bitcast(BF16)

    PSK = staticmethod(lambda b: ("psum", b))

    @staticmethod
    def _fs(ap):
        n = 1
        for d in ap.shape[1:]:
            n *= d
        return n

    def mm(self, out, lhsT, rhs, st, sp, r, w):
        dur = self._fs(out) * (4.0 if lhsT.dtype == F32 else 1.0) / 2400.0 + 0.02
        self.S.pe(lambda e: e.matmul(out, lhsT=lhsT, rhs=rhs, start=st, stop=sp), r=r, w=w, dur=dur)

    def tr(self, out, in_, ident, r, w):
        self.S.pe(lambda e: e.transpose(out=out, in_=in_, identity=ident), r=r, w=w,
                  dur=0.12 if in_.dtype == F32 else 0.07)

    def actf(self, out, in_, func, r, w, scale=None, bias=None, accum=None):
        kw = {}
        if scale is not None:
            kw["scale"] = scale
        if bias is not None:
            kw["bias"] = bias
        if accum is not None:
            kw["accum_out"] = accum
        self.S.act(lambda e: e.activation(out=out, in_=in_, func=func, **kw), r=r, w=w, dur=self._fs(out) / 960.0 + 0.22)

    def _edur(self, eng, out):
        return self._fs(out) / (480.0 if eng == "pool" else 960.0) + 0.2

    def tt(self, eng, out, in0, in1, op, r, w):
        self.S.add(eng, lambda e: e.tensor_tensor(out=out, in0=in0, in1=in1, op=op), r=r, w=w, dur=self._edur(eng, out))

    def ts(self, eng, out, in0, s1, s2, op0, op1, r, w):
        if op1 is None:
            self.S.add(eng, lambda e: e.tensor_scalar(out=out, in0=in0, scalar1=s1, scalar2=None, op0=op0), r=r, w=w,
                       dur=self._edur(eng, out))
        else:
            self.S.add(eng, lambda e: e.tensor_scalar(out=out, in0=in0, scalar1=s1, scalar2=s2, op0=op0, op1=op1), r=r, w=w,
                       dur=self._edur(eng, out))

    def stt(self, out, in0, scalar, in1, op0, op1, r, w):
        self.S.dve(lambda e: e.scalar_tensor_tensor(out=out, in0=in0, scalar=scalar, in1=in1, op0=op0, op1=op1), r=r, w=w,
                   dur=self._fs(out) / 960.0 + 0.2)

    def cp(self, eng, out, in_, r, w, bg=False):
        if eng == "act":
            self.S.act(lambda e: e.copy(out=out, in_=in_), r=r, w=w, dur=self._fs(out) / 960.0 + 0.22, bg=bg)
        else:
            self.S.add(eng, lambda e: e.tensor_copy(out=out, in_=in_), r=r, w=w, dur=self._edur(eng, out), bg=bg)

    def evac(self, out, in_, r, w):
        self.ev_i += 1
        self.cp("act" if self.ev_i % 2 else "dve", out, in_, r, w)

    def dm(self, out, in_, r=(), w=(), q="sp", bg=False):
        nb = 1
        for d in out.shape:
            nb *= d
        nb *= _dsize(out.dtype)
        self.S.dma(lambda e: e.dma_start(out=out, in_=in_), r=r, w=w, q=q, nbytes=nb, bg=bg)

    def recip(self, out, in_, r, w):
        self.S.dve(lambda e: e.reciprocal(out=out, in_=in_), r=r, w=w, dur=self._fs(out) / 960.0 + 0.2)

    def convert(self, src, K, C, CW, name):
        S = self.S
        ncb = C // CW
        dst = self.dscr(name, [ncb, P, K, CW], BF16)
        pieces = src if isinstance(src, list) else [(src, lambda st: st[:, 0:C])]
        for k in range(K):
            b = self.cv_i % 3
            self.cv_i += 1
            st, bt = self.cv_f32[b], self.cv_bf[b]
            for (ap, dfn) in pieces:
                self.dm(dfn(st), ap[k * P:(k + 1) * P], r=[], w=[st])
            eng = ("act", "dve", "pool")[self.cv_i % 3]
            self.cp(eng, bt[:, 0:C], st[:, 0:C], r=[st], w=[bt])
            d = dst[:, :, k, :].rearrange("cb p c -> p cb c")
            self.cv_pending.append((d, bt, C, CW, name))
            if len(self.cv_pending) > 2:
                self.cv_flush(1)
        return dst

    def cv_flush(self, n):
        for _ in range(n):
            if not self.cv_pending:
                return
            d, bt, C, CW, name = self.cv_pending.pop(0)
            self.dm(d, bt[:, 0:C].rearrange("p (cb c) -> p cb c", c=CW), r=[bt], w=[("dram", name)])

    def convert_jobs(self, src, K, C, CW, name):
        assert C % 512 == 0
        dst = self.dscr(name, [C // CW, P, K, CW], BF16)
        jobs = []
        for k in range(K):
            for j in range(C // 512):
                jobs.append((src, dst, k, j, CW, name))
        return dst, jobs

    def run_job(self, job, bg=True):
        src, dst, k, j, CW, name = job
        i = self.bg_i
        self.bg_i += 1
        st, bt = self.bg_f32[i % 4], self.bg_bf[i % 4]
        self.dm(st[:, :], src[k * P:(k + 1) * P, j * 512:(j + 1) * 512], w=[st], bg=bg)
        self.cp("pool", bt[:, :], st[:, :], r=[st], w=[bt], bg=bg)
        if CW >= 512:
            c0 = j * 512
            d = dst[c0 // CW, :, k, c0 % CW:c0 % CW + 512]
            self.dm(d, bt[:, :], r=[bt], w=[("dram", name)], bg=bg)
        else:
            n = 512 // CW
            d = dst[j * n:(j + 1) * n, :, k, :].rearrange("cb p c -> p cb c")
            self.dm(d, bt[:, :].rearrange("p (cb c) -> p cb c", c=CW), r=[bt], w=[("dram", name)], bg=bg)

    def bg_take(self, key, n=None, bg=True):
        lst = self.bg_jobs.get(key)
        if not lst:
            return
        if n is None or n > len(lst):
            n = len(lst)
        for job in lst[:n]:
            self.run_job(job, bg=bg)
        del lst[:n]

    def bg_need(self, key):
        if self.bg_jobs.get(key):
            self.bg_take(key, None, bg=False)
            self.S.barrier()

    def gain_T(self, dst2d, src_rows, rows, parts):
        stg = self.gstg[self.g_i % 2]
        bank = self.g_i % 4
        self.g_i += 1
        srcs = src_rows if isinstance(src_rows, list) else [(src_rows, 0, parts)]
        for ap, c0, w in srcs:
            self.dm(stg[0:rows, c0:c0 + w], ap, w=[stg])
        self.tr(self.ps(bank)[0:parts, 0:rows], stg[0:rows, 0:parts], self.ident_f[0:rows, 0:rows],
                r=[stg, self.ident_f], w=[self.PSK(bank)])
        self.cp("dve", dst2d, self.ps(bank)[0:parts, 0:rows], r=[self.PSK(bank)], w=[("gain", self.g_i)])

    def gain_fm(self, name, src, L, nk, parts=P, lo=0):
        t = self.arena.alloc(name, [L, nk], parts=parts)
        self.gain_T(t.ap.rearrange("p l k -> p (l k)"),
                    src[:, lo:lo + nk * parts].rearrange("l (k p) -> (l k) p", p=parts), L * nk, parts)
        return t

    def gain_rep(self, name, src, L, n):
        t = self.arena.alloc(name, [L, n])
        self.dm(t.ap.rearrange("p l n -> p (l n)")[:, None, :],
                src.rearrange("(a l) n -> a (l n)", a=1).partition_broadcast(P), w=[t])
        return t

    def rms_stats(self, chunks, ncols, scale, bank, rstd, sq_t):
        n = len(chunks)
        for i, (ap, key, parts) in enumerate(chunks):
            sq = sq_t[i % 2]
            self.actf(sq[0:parts, 0:ncols], ap, AF.Square, r=[key], w=[sq])
            self.mm(self.ps(bank)[:, 0:ncols], self.ones_b[0:parts, :], sq[0:parts, 0:ncols], i == 0, i == n - 1,
                    r=[sq, self.ones_b], w=[self.PSK(bank)])
        self.ts("dve", rstd[:, 0:ncols], self.ps(bank)[:, 0:ncols], scale, EPS, ALU.mult, ALU.add,
                r=[self.PSK(bank)], w=[rstd])
        self.actf(rstd[:, 0:ncols], rstd[:, 0:ncols], AF.Sqrt, r=[rstd], w=[rstd])
        self.recip(rstd[:, 0:ncols], rstd[:, 0:ncols], r=[rstd], w=[rstd])

    def load_xt(self, xT, s, t, xt):
        tsl = slice(t * TT, (t + 1) * TT)
        for h in range(2):
            self.dm(xt[:, h * 8:(h + 1) * 8, :], xT[s, :, h * 8:(h + 1) * 8, tsl],
                    r=[("xT", s, t, h)], w=[(xt, k) for k in range(h * 8, h * 8 + 8)])

    def store_xt(self, xT, s, t, xt):
        tsl = slice(t * TT, (t + 1) * TT)
        for h in range(2):
            self.dm(xT[s, :, h * 8:(h + 1) * 8, tsl], xt[:, h * 8:(h + 1) * 8, :],
                    r=[(xt, k) for k in range(h * 8, h * 8 + 8)], w=[("xT", s, t, h)])

    def norm_h(self, xt, hT, g, l, rstd, sq_t, bank=0):
        self.rms_stats([(xt[:, k, :], (xt, k), P) for k in range(KD)], TT, 1.0 / D, bank, rstd, sq_t)
        for k in range(KD):
            self.stt(hT[:, k, :], xt[:, k, :], g[:, l, k:k + 1], rstd[:, :], ALU.mult, ALU.mult,
                     r=[(xt, k), rstd, g], w=[(hT, k)])
    def build(self):
        cfg, nc, S = self.cfg, self.nc, self.S
        NS, SQ, L = cfg.NS, cfg.S, cfg.L
        NT = SQ // TT
        ph = cfg.phases
        es = ExitStack()
        self.es = es
        es.enter_context(nc.allow_non_contiguous_dma(reason="tiny per-layer vectors / small tiles"))
        I = {}
        I["x"] = self.din("x", [NS, SQ, D])
        if "mix" in ph:
            I["positions"] = self.din("positions", [NS, SQ], I32)
            for n, sh in (("attn_norm_g", [L, D]), ("w_in", [L, D, IN_COLS]), ("conv_w", [L, 4, CONV_DIM]),
                          ("conv_b", [L, CONV_DIM]), ("dt_bias", [L, NHS]), ("a_log", [L, NHS]), ("d_skip", [L, NHS]),
                          ("ssd_norm_g", [L, SSD_INNER]), ("q_a_norm_g", [L, Q_LORA]), ("w_q_b", [L, Q_LORA, MLA_H * 192]),
                          ("kv_a_norm_g", [L, KV_LORA]), ("w_kv_b", [L, KV_LORA, MLA_H * 256]), ("mla_q_norm_g", [L, 192]),
                          ("mla_k_norm_g", [L, 192]), ("w_out", [L, D, D])):
                I[n] = self.din(n, sh)
        if "xattn" in ph:
            I["mem"] = self.din("mem", [NS, MEM, D])
            for n, sh in (("xattn_norm_g", [L, D]), ("mem_norm_g", [L, D]), ("w_xq", [L, D, 512]), ("w_xk", [L, D, 512]),
                          ("w_xv", [L, D, 512]), ("xq_norm_g", [L, 128]), ("xk_norm_g", [L, 128]), ("w_xo", [L, 512, D])):
                I[n] = self.din(n, sh)
        if "ffn" in ph:
            for n, sh in (("ffn_norm_g", [L, D]), ("w_gate", [L, D, FF]), ("w_up", [L, D, FF]), ("w_down", [L, FF, D])):
                I[n] = self.din(n, sh)
        out = nc.dram_tensor("out", [NS, SQ, D], F32, kind="ExternalOutput").ap()
        self.I = I
        xT = self.dscr("xT", [NS, P, KD, SQ])
        self.xT = xT
        ACOLS = 53000
        arena_t = es.enter_context(nc.sbuf_tensor("arena", [P, ACOLS], F32))
        self.arena = A = Arena(arena_t[:, :], ACOLS)
        self.psum = es.enter_context(nc.psum_tensor("psum", [P, 8, 512], F32))
        PSK = self.PSK

        ones_f = A.alloc("ones_f", [P])
        ident_f = A.alloc("ident_f", [P])
        mge_f = A.alloc("mge_f", [P])
        mgt_f = A.alloc("mgt_f", [P])
        ident_b = A.alloc("ident_b", [P], BF16)
        ones_b = A.alloc("ones_b", [P], BF16)
        mge_b = A.alloc("mge_b", [P], BF16)
        S.pool(lambda e: e.memset(ones_f[:, :], 1.0), w=[ones_f])
        S.pool(lambda e: e.affine_select(out=ident_f[:, :], in_=ones_f[:, :], pattern=[[1, P]],
                                         compare_op=ALU.is_equal, fill=0.0, base=0, channel_multiplier=-1),
               r=[ones_f], w=[ident_f])
        S.pool(lambda e: e.affine_select(out=mge_f[:, :], in_=ones_f[:, :], pattern=[[1, P]],
                                         compare_op=ALU.is_ge, fill=0.0, base=0, channel_multiplier=-1),
               r=[ones_f], w=[mge_f])
        S.pool(lambda e: e.affine_select(out=mgt_f[:, :], in_=ones_f[:, :], pattern=[[-1, P]],
                                         compare_op=ALU.is_ge, fill=0.0, base=-1, channel_multiplier=1),
               r=[ones_f], w=[mgt_f])
        self.cp("dve", ident_b[:, :], ident_f[:, :], r=[ident_f], w=[ident_b])
        self.cp("dve", ones_b[:, :], ones_f[:, :], r=[ones_f], w=[ones_b])
        self.cp("dve", mge_b[:, :], mge_f[:, :], r=[mge_f], w=[mge_b])
        self.ones_b, self.ident_f, self.ident_b, self.ones_f = ones_b, ident_f, ident_b, ones_f
        self.mge_f, self.mgt_f, self.mge_b = mge_f, mgt_f, mge_b
        G = {}
        self.gstg = [A.alloc("gstg%d" % i, [P]) for i in range(2)]
        self.g_i = 0
        if "mix" in ph:
            G["attn"] = self.gain_fm("g_attn", I["attn_norm_g"], L, KD)
            cw = A.alloc("cw", [L, 4, 12])
            for l in range(L):
                self.gain_T(cw[:, l, :, :].rearrange("p j c -> p (j c)"),
                            I["conv_w"][l].rearrange("j (c p) -> (j c) p", p=P), 48, P)
            G["cw"] = cw
            G["cb"] = self.gain_fm("cb", I["conv_b"], L, 12)
            G["dtb"] = self.gain_rep("dtb", I["dt_bias"], L, NHS)
            G["dsk"] = self.gain_rep("dsk", I["d_skip"], L, NHS)
            alog = self.gain_rep("alog", I["a_log"], L, NHS)
            Aneg = A.alloc("Aneg", [L, NHS])
            self.actf(Aneg[:, :, :], alog[:, :, :], AF.Exp, r=[alog], w=[Aneg])
            self.ts("dve", Aneg[:, :, :], Aneg[:, :, :], -1.0, None, ALU.mult, None, r=[Aneg], w=[Aneg])
            G["Aneg"] = Aneg
            G["qa"] = self.gain_fm("g_qa", I["q_a_norm_g"], L, 4)
            G["kva"] = self.gain_fm("g_kva", I["kv_a_norm_g"], L, 4)
            for nm, src in (("q", I["mla_q_norm_g"]), ("k", I["mla_k_norm_g"])):
                G[nm + "n"] = self.gain_fm("g_%sn" % nm, src, L, 1)
                G[nm + "r"] = self.gain_fm("g_%sr" % nm, src, L, 1, parts=64, lo=128)
                sw = A.alloc("g_%srs" % nm, [L, 1], parts=64)
                self.gain_T(sw.ap.rearrange("p l k -> p (l k)"), [(src[:, 160:192], 0, 32), (src[:, 128:160], 32, 32)], L, 64)
                G[nm + "rs"] = sw
        if "xattn" in ph:
            G["xattn"] = self.gain_fm("g_xattn", I["xattn_norm_g"], L, KD)
            G["mem"] = self.gain_fm("g_mem", I["mem_norm_g"], L, KD)
            G["xq"] = self.gain_fm("g_xq", I["xq_norm_g"], L, 1)
            G["xk"] = self.gain_fm("g_xk", I["xk_norm_g"], L, 1)
        if "ffn" in ph:
            G["ffn"] = self.gain_fm("g_ffn", I["ffn_norm_g"], L, KD)
        self.G = G
        self.bg_f32 = [A.alloc("bgf%d" % i, [512]) for i in range(4)]
        self.bg_bf = [A.alloc("bgb%d" % i, [512], BF16) for i in range(4)]
        self.bg_i = 0
        self.bg_jobs = {}
        A.freeze()

        self.cv_i = 0
        self.cv_pending = []
        self.cv_f32 = [A.alloc("cvf%d" % i, [FF]) for i in range(3)]
        self.cv_bf = [A.alloc("cvb%d" % i, [FF], BF16) for i in range(3)]
        W = [dict() for _ in range(L)]

        def plain(l, key, grp, src, K, C, CW):
            nm = "W%s%d" % (key, l)
            if l == 0 and grp == "mx":
                W[l][key] = self.convert(src, K, C, CW, nm)
            else:
                W[l][key], jobs = self.convert_jobs(src, K, C, CW, nm)
                self.bg_jobs.setdefault((grp, l), []).extend(jobs)

        for l in range(L):
            if "mix" in ph:
                wi = I["w_in"][l]
                plain(l, "z", "mx", wi[:, 0:C0], KD, 1024, 512)
                plain(l, "xbc", "mx", wi[:, C0:C1], KD, 1536, 512)
                W[l]["sm"] = self.convert([(wi[:, C1:C2], lambda st: st[:, 0:16]),
                                           (wi[:, C4:C4 + 64], lambda st: st[:, 16:80]),
                                           (wi[:, C4 + 32:C4 + 64], lambda st: st[:, 80:112]),
                                           (wi[:, C4:C4 + 32], lambda st: st[:, 112:144])], KD, 144, 144, "Wsm%d" % l)
                plain(l, "qa", "mx", wi[:, C2:C3], KD, 512, 512)
                plain(l, "kva", "mx", wi[:, C3:C4], KD, 512, 512)
                wq3 = I["w_q_b"][l].rearrange("r (h c) -> r h c", c=192)
                v8 = lambda st, n, c: st[:, 0:n].rearrange("p (h c) -> p h c", c=c)
                W[l]["qbn"] = self.convert([(wq3[:, :, 0:128], lambda st: v8(st, 1024, 128))], 4, 1024, 1024, "Wqbn%d" % l)
                W[l]["qbr"] = self.convert([(wq3[:, :, 128:192], lambda st: v8(st, 512, 64))], 4, 512, 512, "Wqbr%d" % l)
                W[l]["qbrs"] = self.convert([(wq3[:, :, 160:192], lambda st: v8(st, 512, 64)[:, :, 0:32]),
                                             (wq3[:, :, 128:160], lambda st: v8(st, 512, 64)[:, :, 32:64])],
                                            4, 512, 512, "Wqbrs%d" % l)
                wk3 = I["w_kv_b"][l].rearrange("r (h c) -> r h c", c=256)
                W[l]["kvn"] = self.convert([(wk3[:, :, 0:128], lambda st: v8(st, 1024, 128))], 4, 1024, 1024, "Wkvn%d" % l)
                W[l]["kvv"] = self.convert([(wk3[:, :, 128:256], lambda st: v8(st, 1024, 128))], 4, 1024, 1024, "Wkvv%d" % l)
                plain(l, "out", "mx", I["w_out"][l], KD, D, 512)
            if "xattn" in ph:
                plain(l, "xq", "mx", I["w_xq"][l], KD, 512, 512)
                plain(l, "xk", "mx", I["w_xk"][l], KD, 512, 512)
                plain(l, "xv", "mx", I["w_xv"][l], KD, 512, 512)
                plain(l, "xo", "mx", I["w_xo"][l], 4, D, D)
            if "ffn" in ph:
                plain(l, "g", "ffn", I["w_gate"][l], KD, FF, 256)
                plain(l, "u", "ffn", I["w_up"][l], KD, FF, 256)
                plain(l, "d", "ffn", I["w_down"][l], KF, D, 128)
        self.cv_flush(100)
        S.barrier()
        A.reset()

        xin_t = [A.alloc("xin%d" % i, [4, D]) for i in range(2)]
        xo_t = [A.alloc("xo%d" % i, [KD, TT]) for i in range(2)]
        it = 0
        for s in range(NS):
            for t in range(NT):
                b = it % 2
                it += 1
                xi, xo = xin_t[b], xo_t[b]
                for tb in range(4):
                    self.dm(xi[:, tb, :], I["x"][s, t * TT + tb * P:t * TT + (tb + 1) * P, :], w=[(xi, tb)])
                for k in range(KD):
                    bank = k % 4
                    for tb in range(4):
                        self.tr(self.ps(bank)[:, tb * P:(tb + 1) * P], xi[:, tb, k * P:(k + 1) * P], ident_f[:, :],
                                r=[(xi, tb), ident_f], w=[PSK(bank)])
                    self.evac(xo[:, k, :], self.ps(bank), r=[PSK(bank)], w=[(xo, k)])
                self.store_xt(xT, s, t, xo)
        S.barrier()
        A.reset()

        if "mix" in ph:
            self.szt = self.dscr("szt", [NS, SQ, SSD_INNER])
            self.xbcT = self.dscr("xbcT", [NS, P, 12, SQ])
            self.dtt = self.dscr("dtt", [NS, SQ // TT, P, 4 * NHS])
            self.qnT = self.dscr("qnT", [NS, MLA_H, P, SQ], BF16)
            self.qrT = self.dscr("qrT", [NS, MLA_H, 64, SQ], BF16)
            self.knT = self.dscr("knT", [NS, MLA_H, P, SQ], BF16)
            self.krT = self.dscr("krT", [NS, MLA_H, 64, SQ], BF16)
            self.vt = self.dscr("vt", [NS, SQ, MLA_H * P], BF16)
            self.mixT = self.dscr("mixT", [NS, P, KD, SQ], BF16)
            self.phase_rope()
            S.barrier()
            A.reset()
        if "xattn" in ph:
            self.phase_mem()
            S.barrier()
            A.reset()

        HOST = {"assd": 0.17, "amla": 0.32, "ssd": 0.0, "attn": 0.15, "oproj": 0.11, "xattn": 0.25}
        for l in range(L):
            nffn = len(self.bg_jobs.get(("ffn", l), ()))
            if "mix" in ph:
                self.bg_need(("mx", l))
                for f in (self.phase_assd, self.phase_amla, self.phase_ssd, self.phase_attn, self.phase_oproj):
                    nm = f.__name__[6:]
                    if any(p.startswith("only_") for p in ph) and ("only_" + nm) not in ph:
                        continue
                    self.bg_take(("ffn", l), int(nffn * HOST[nm]) + 1)
                    f(l, W[l])
                    S.barrier()
                    A.reset()
            if "xattn" in ph and "noxattn" not in ph:
                self.bg_need(("mx", l))
                self.bg_take(("ffn", l), int(nffn * HOST["xattn"]) + 1)
                self.phase_xattn(l, W[l])
                S.barrier()
                A.reset()
            if "ffn" in ph:
                self.bg_need(("ffn", l))
                if l + 1 < L:
                    self.bg_take(("mx", l + 1))
                self.phase_ffn(l, W[l])
                S.barrier()
                A.reset()

        xi_t = [A.alloc("xfi%d" % i, [KD, TT]) for i in range(2)]
        ot_t = [A.alloc("xfo%d" % i, [4, D]) for i in range(2)]
        it = 0
        for s in range(NS):
            for t in range(NT):
                b = it % 2
                it += 1
                xi, ot = xi_t[b], ot_t[b]
                self.load_xt(xT, s, t, xi)
                n = 0
                for tb in range(4):
                    for kg in range(4):
                        bank = n % 4
                        n += 1
                        for kk in range(4):
                            k = kg * 4 + kk
                            self.tr(self.ps(bank)[:, kk * P:(kk + 1) * P], xi[:, k, tb * P:(tb + 1) * P], ident_f[:, :],
                                    r=[(xi, k), ident_f], w=[PSK(bank)])
                        self.evac(ot[:, tb, kg * 512:(kg + 1) * 512], self.ps(bank), r=[PSK(bank)], w=[(ot, tb)])
                for tb in range(4):
                    self.dm(out[s, t * TT + tb * P:t * TT + (tb + 1) * P, :], ot[:, tb, :], r=[(ot, tb)], w=[("out", s, t, tb)])
        S.emit(nc, es)
        es.close()
        return nc

    def phase_ffn(self, l, W):
        cfg, S, A = self.cfg, self.S, self.arena
        PSK = self.PSK
        NS, NT = cfg.NS, cfg.S // TT
        xT = self.xT
        Wg, Wu, Wd = W["g"], W["u"], W["d"]
        xt_t = [A.alloc("fx%d" % i, [KD, TT]) for i in range(2)]
        sq_t = [A.alloc("fsq%d" % i, [TT], BF16) for i in range(2)]
        rstd = A.alloc("frstd", [TT])
        hT = A.alloc("fh", [KD, TT], BF16)
        actT = A.alloc("fact", [KF, TT], BF16)
        wg_t = [A.alloc("fwg%d" % i, [KD, 256], BF16) for i in range(2)]
        wu_t = [A.alloc("fwu%d" % i, [KD, 256], BF16) for i in range(2)]
        wd_t = [A.alloc("fwd%d" % i, [KF, 128], BF16) for i in range(2)]
        sg_t = [A.alloc("fsg%d" % i, [TT]) for i in range(2)]
        it = 0
        wi = 0
        di = 0
        for s in range(NS):
            for t in range(NT):
                xt = xt_t[it % 2]
                it += 1
                self.load_xt(xT, s, t, xt)
                self.norm_h(xt, hT, self.G["ffn"], l, rstd, sq_t)
                for cg in range(FF // 256):
                    wg, wu = wg_t[wi % 2], wu_t[wi % 2]
                    wi += 1
                    self.dm(wg[:, :, :], Wg[cg], w=[wg])
                    self.dm(wu[:, :, :], Wu[cg], w=[wu])
                    for c in range(2):
                        j = cg * 2 + c
                        bg, bu = 1 + (j % 2), 3 + (j % 2)
                        for k in range(KD):
                            self.mm(self.ps(bg), wg[:, k, c * P:(c + 1) * P], hT[:, k, :], k == 0, k == KD - 1,
                                    r=[wg, (hT, k)], w=[PSK(bg)])
                        for k in range(KD):
                            self.mm(self.ps(bu), wu[:, k, c * P:(c + 1) * P], hT[:, k, :], k == 0, k == KD - 1,
                                    r=[wu, (hT, k)], w=[PSK(bu)])
                        sg = sg_t[j % 2]
                        self.actf(sg[:, :], self.ps(bg), AF.Silu, r=[PSK(bg)], w=[sg])
                        self.tt("dve", actT[:, j, :], sg[:, :], self.ps(bu), ALU.mult, r=[sg, PSK(bu)], w=[(actT, j)])
                for dk in range(KD):
                    wd = wd_t[di % 2]
                    bo = 5 + (di % 2)
                    di += 1
                    for h in range(2):
                        self.dm(wd[:, h * 22:(h + 1) * 22, :], Wd[dk][:, h * 22:(h + 1) * 22, :], w=[(wd, h)])
                    for j in range(KF):
                        self.mm(self.ps(bo), wd[:, j, :], actT[:, j, :], j == 0, j == KF - 1,
                                r=[(wd, j // 22), (actT, j)], w=[PSK(bo)])
                    self.tt("dve", xt[:, dk, :], self.ps(bo), xt[:, dk, :], ALU.add, r=[PSK(bo), (xt, dk)], w=[(xt, dk)])
                self.store_xt(xT, s, t, xt)
    def phase_mem(self):
        cfg, A = self.cfg, self.arena
        PSK = self.PSK
        NS = cfg.NS
        mem = self.I["mem"]
        self.memT = self.dscr("memT", [NS, P, KD, MEM], BF16)
        mt_t = [A.alloc("mt%d" % i, [D]) for i in range(2)]
        junk = A.alloc("mjunk", [D], BF16)
        ss = A.alloc("mss", [2])
        mh_t = [A.alloc("mh%d" % i, [D], BF16) for i in range(2)]
        mo_t = [A.alloc("mo%d" % i, [KD, P], BF16) for i in range(2)]
        it = 0
        for s in range(NS):
            for mb in range(MEM // P):
                b = it % 2
                it += 1
                mt, mh, mo = mt_t[b], mh_t[b], mo_t[b]
                sc = ss[:, b:b + 1]
                self.dm(mt[:, :], mem[s, mb * P:(mb + 1) * P, :], w=[mt])
                self.actf(junk[:, :], mt[:, :], AF.Square, r=[mt], w=[junk, (ss, b)], accum=sc)
                self.ts("dve", sc, sc, 1.0 / D, EPS, ALU.mult, ALU.add, r=[(ss, b)], w=[(ss, b)])
                self.actf(sc, sc, AF.Sqrt, r=[(ss, b)], w=[(ss, b)])
                self.recip(sc, sc, r=[(ss, b)], w=[(ss, b)])
                self.ts("dve", mh[:, :], mt[:, :], sc, None, ALU.mult, None, r=[mt, (ss, b)], w=[mh])
                for k in range(KD):
                    bank = 2 * b + k // 8
                    self.tr(self.psb(bank)[:, (k % 8) * P:(k % 8 + 1) * P], mh[:, k * P:(k + 1) * P], self.ident_b[:, :],
                            r=[mh, self.ident_b], w=[PSK(bank)])
                for hf in range(2):
                    bank = 2 * b + hf
                    self.evac(mo[:, hf * 8:(hf + 1) * 8, :], self.psb(bank).rearrange("p (a b) -> p a b", a=8),
                              r=[PSK(bank)], w=[(mo, hf)])
                self.dm(self.memT[s, :, :, mb * P:(mb + 1) * P], mo[:, :, :], r=[(mo, 0), (mo, 1)], w=[("memT", s, mb)])

    def phase_xattn(self, l, W):
        cfg, A, G = self.cfg, self.arena, self.G
        PSK = self.PSK
        NS, NT = cfg.NS, cfg.S // TT
        xT = self.xT
        Wxq = A.alloc("Wxq", [KD, 512], BF16)
        Wxk = A.alloc("Wxk", [KD, 512], BF16)
        Wxv = A.alloc("Wxv", [KD, 512], BF16)
        Wxo = A.alloc("Wxo", [4, D], BF16)
        self.dm(Wxq[:, :, :], W["xq"][0], w=[Wxq])
        self.dm(Wxk[:, :, :], W["xk"][0], w=[Wxk])
        self.dm(Wxv[:, :, :], W["xv"][0], w=[Wxv])
        self.dm(Wxo[:, :, :], W["xo"][0], w=[Wxo])
        kT = A.alloc("kT", [NS, XH, MEM], BF16)
        V = A.alloc("V", [NS, 2, 512], BF16)
        mT = A.alloc("mT", [KD, MEM], BF16)
        sq_t = [A.alloc("xsq%d" % i, [TT], BF16) for i in range(2)]
        rk = A.alloc("xrk", [TT])
        for s in range(NS):
            self.dm(mT[:, :, :], self.memT[s], r=[("memT", s)], w=[mT])
            for k in range(KD):
                self.ts("dve" if k % 2 else "pool", mT[:, k, :], mT[:, k, :], G["mem"][:, l, k:k + 1], None, ALU.mult, None,
                        r=[mT, G["mem"]], w=[(mT, k)])
            for h in range(XH):
                for k in range(KD):
                    self.mm(self.ps(1)[:, 0:MEM], Wxk[:, k, h * P:(h + 1) * P], mT[:, k, :], k == 0, k == KD - 1,
                            r=[Wxk, (mT, k), mT], w=[PSK(1)])
                self.rms_stats([(self.ps(1)[:, 0:MEM], PSK(1), P)], MEM, 1.0 / 128, 2, rk, sq_t)
                self.stt(kT[:, s, h, :], self.ps(1)[:, 0:MEM], G["xk"][:, l, 0:1], rk[:, 0:MEM], ALU.mult, ALU.mult,
                         r=[PSK(1), rk, G["xk"]], w=[(kT, s, h)])
            for mb in range(2):
                for k in range(KD):
                    self.mm(self.ps(3), mT[:, k, mb * P:(mb + 1) * P], Wxv[:, k, :], k == 0, k == KD - 1,
                            r=[Wxv, (mT, k), mT], w=[PSK(3)])
                self.evac(V[:, s, mb, :], self.ps(3), r=[PSK(3)], w=[(V, s, mb)])
        if "xe0" in cfg.phases:
            return
        xt_t = [A.alloc("xx%d" % i, [KD, TT]) for i in range(2)]
        rstd = A.alloc("xrstd", [TT])
        hT = A.alloc("xh", [KD, TT], BF16)
        qn = A.alloc("xqn", [TT], BF16)
        pT = [A.alloc("xp%d" % i, [TT], BF16) for i in range(2)]
        rden = A.alloc("xrden", [TT])
        oTn = A.alloc("xo", [XH, TT], BF16)
        it = 0
        for s in range(NS):
            for t in range(NT):
                xt = xt_t[it % 2]
                it += 1
                import os
                cut = int(os.environ.get("KCUT", "99"))
                self.load_xt(xT, s, t, xt)
                if cut == 0:
                    return
                self.norm_h(xt, hT, G["xattn"], l, rstd, sq_t)
                if cut == 1:
                    return
                for h in range(XH):
                    for k in range(KD):
                        self.mm(self.ps(1), Wxq[:, k, h * P:(h + 1) * P], hT[:, k, :], k == 0, k == KD - 1,
                                r=[Wxq, (hT, k)], w=[PSK(1)])
                    if cut == 2:
                        return
                    self.rms_stats([(self.ps(1), PSK(1), P)], TT, 1.0 / 128, 2, rk, sq_t)
                    self.stt(qn[:, :], self.ps(1), G["xq"][:, l, 0:1], rk[:, :], ALU.mult, ALU.mult,
                             r=[PSK(1), rk, G["xq"]], w=[qn])
                    if cut == 3:
                        return
                    for mb in range(2):
                        self.mm(self.ps(3 + mb), kT[:, s, h, mb * P:(mb + 1) * P], qn[:, :], True, True,
                                r=[(kT, s, h), qn], w=[PSK(3 + mb)])
                        self.actf(pT[mb][:, :], self.ps(3 + mb), AF.Exp, r=[PSK(3 + mb)], w=[pT[mb]], scale=float(128 ** -0.5))
                    if cut == 4:
                        return
                    for mb in range(2):
                        self.mm(self.ps(5), V[:, s, mb, h * P:(h + 1) * P], pT[mb][:, :], mb == 0, mb == 1,
                                r=[(V, s, mb), pT[mb]], w=[PSK(5)])
                    for mb in range(2):
                        self.mm(self.ps(6), self.ones_b[:, :], pT[mb][:, :], mb == 0, mb == 1,
                                r=[self.ones_b, pT[mb]], w=[PSK(6)])
                    if cut == 5:
                        return
                    self.cp("dve", rden[:, :], self.ps(6), r=[PSK(6)], w=[rden])
                    self.recip(rden[:, :], rden[:, :], r=[rden], w=[rden])
                    self.tt("dve", oTn[:, h, :], self.ps(5), rden[:, :], ALU.mult, r=[PSK(5), rden], w=[(oTn, h)])
                if cut == 6:
                    return
                for dk in range(KD):
                    bo = 5 if dk % 2 else 0
                    for h in range(XH):
                        self.mm(self.ps(bo), Wxo[:, h, dk * P:(dk + 1) * P], oTn[:, h, :], h == 0, h == XH - 1,
                                r=[Wxo, (oTn, h)], w=[PSK(bo)])
                    self.tt("dve", xt[:, dk, :], self.ps(bo), xt[:, dk, :], ALU.add, r=[PSK(bo), (xt, dk)], w=[(xt, dk)])
                self.store_xt(xT, s, t, xt)
    def phase_rope(self):
        cfg, A, S = self.cfg, self.arena, self.S
        NS, SQ = cfg.NS, cfg.S
        self.ropeC = self.dscr("ropeC", [NS, 64, SQ])
        self.ropeS = self.dscr("ropeS", [NS, 64, SQ])
        ji = A.alloc("ji", [1], I32, parts=64)
        jf = A.alloc("jf", [1], parts=64)
        invf = A.alloc("invf", [1], parts=64)
        S.pool(lambda e: e.iota(out=ji[0:32, :], pattern=[[0, 1]], base=0, channel_multiplier=1), w=[ji])
        S.pool(lambda e: e.iota(out=ji[32:64, :], pattern=[[0, 1]], base=0, channel_multiplier=1), w=[ji])
        self.cp("dve", jf[:, :], ji[:, :], r=[ji], w=[jf])
        self.actf(invf[:, :], jf[:, :], AF.Exp, r=[jf], w=[invf], scale=float(-np.log(10000.0) / 32.0))
        pos_i = A.alloc("pos_i", [SQ], I32, parts=64)
        ang = A.alloc("ang", [SQ], parts=64)
        a2 = A.alloc("a2", [SQ], parts=64)
        ni = A.alloc("ni", [SQ], I32, parts=64)
        nf = A.alloc("nf", [SQ], parts=64)
        rr = A.alloc("rr", [SQ], parts=64)
        m_ = A.alloc("m_", [SQ], parts=64)
        tabs = [A.alloc("tabS", [SQ], parts=64), A.alloc("tabC", [SQ], parts=64)]
        TWO_PI = 2.0 * PI
        for s in range(NS):
            self.dm(pos_i[:, :].rearrange("p (a s) -> p a s", a=1), self.I["positions"][s:s + 1, :].partition_broadcast(64), w=[pos_i])
            self.cp("dve", ang[:, :], pos_i[:, :], r=[pos_i], w=[ang])
            self.ts("dve", ang[:, :], ang[:, :], invf[:, 0:1], None, ALU.mult, None, r=[ang, invf], w=[ang])
            for tab, shift in ((tabs[0], 0.0), (tabs[1], PI / 2)):
                self.ts("dve", a2[:, :], ang[:, :], shift, 1.0 / TWO_PI, ALU.add, ALU.mult, r=[ang], w=[a2])
                self.cp("dve", ni[:, :], a2[:, :], r=[a2], w=[ni])
                self.cp("dve", nf[:, :], ni[:, :], r=[ni], w=[nf])
                self.stt(rr[:, :], nf[:, :], -TWO_PI, ang[:, :], ALU.mult, ALU.add, r=[nf, ang], w=[rr])
                if shift:
                    self.ts("dve", rr[:, :], rr[:, :], shift, None, ALU.add, None, r=[rr], w=[rr])
                self.ts("dve", m_[:, :], rr[:, :], PI, TWO_PI, ALU.is_gt, ALU.mult, r=[rr], w=[m_])
                self.tt("dve", rr[:, :], rr[:, :], m_[:, :], ALU.subtract, r=[rr, m_], w=[rr])
                self.ts("dve", m_[:, :], rr[:, :], -PI, TWO_PI, ALU.is_lt, ALU.mult, r=[rr], w=[m_])
                self.tt("dve", rr[:, :], rr[:, :], m_[:, :], ALU.add, r=[rr, m_], w=[rr])
                self.ts("dve", rr[:, :], rr[:, :], -3.1415925, 3.1415925, ALU.max, ALU.min, r=[rr], w=[rr])
                self.actf(tab[:, :], rr[:, :], AF.Sin, r=[rr], w=[tab])
            self.ts("dve", tabs[0][0:32, :], tabs[0][0:32, :], -1.0, None, ALU.mult, None, r=[tabs[0]], w=[tabs[0]])
            self.dm(self.ropeS[s], tabs[0][:, :], r=[tabs[0]], w=[("ropeS", s)])
            self.dm(self.ropeC[s], tabs[1][:, :], r=[tabs[1]], w=[("ropeC", s)])

    def phase_assd(self, l, W):
        cfg, A, G = self.cfg, self.arena, self.G
        PSK = self.PSK
        NS, SQ, NT = cfg.NS, cfg.S, cfg.S // TT
        xt_t = [A.alloc("ax%d" % i, [KD, TT]) for i in range(2)]
        sq_t = [A.alloc("asq%d" % i, [TT], BF16) for i in range(2)]
        rstd = A.alloc("arstd", [TT])
        hT = A.alloc("ah", [KD, TT], BF16)
        slab_t = [A.alloc("aslab%d" % i, [KD, 512], BF16) for i in range(3)]
        wsm = A.alloc("awsm", [KD, 144], BF16)
        stz = A.alloc("astz", [4, SSD_INNER])
        stx = A.alloc("astx", [12, TT])
        dts = A.alloc("adts", [4, NHS])
        self.dm(wsm[:, :, :], W["sm"][0], w=[wsm])
        it = 0
        si = 0
        n = 0
        for s in range(NS):
            for t in range(NT):
                xt = xt_t[it % 2]
                it += 1
                tsl = slice(t * TT, (t + 1) * TT)
                self.load_xt(self.xT, s, t, xt)
                self.norm_h(xt, hT, G["attn"], l, rstd, sq_t)
                import os
                skip = os.environ.get("KSKIP", "").split(",")
                for half in range(2):
                    if "z" in skip:
                        break
                    slab = slab_t[si % 3]
                    si += 1
                    self.dm(slab[:, :, :], W["z"][half], w=[slab])
                    for tb in range(4):
                        bank = 1 + n % 2
                        n += 1
                        for k in range(KD):
                            self.mm(self.ps(bank), hT[:, k, tb * P:(tb + 1) * P], slab[:, k, :], k == 0, k == KD - 1,
                                    r=[slab, (hT, k)], w=[PSK(bank)])
                        self.actf(stz[:, tb, half * 512:(half + 1) * 512], self.ps(bank), AF.Silu, r=[PSK(bank)], w=[(stz, tb)])
                for tb in range(4):
                    if "z" in skip:
                        break
                    self.dm(self.szt[s, t * TT + tb * P:t * TT + (tb + 1) * P, :], stz[:, tb, :], r=[(stz, tb)], w=[("szt", s, t, tb)])
                for tb in range(4):
                    if "dt" in skip:
                        break
                    for k in range(KD):
                        self.mm(self.ps(3)[:, tb * NHS:(tb + 1) * NHS], hT[:, k, tb * P:(tb + 1) * P], wsm[:, k, 0:NHS],
                                k == 0, k == KD - 1, r=[wsm, (hT, k)], w=[PSK(3)])
                if "dt" not in skip:
                    self.evac(dts[:, :, :], self.ps(3)[:, 0:4 * NHS].rearrange("p (a b) -> p a b", a=4), r=[PSK(3)], w=[dts])
                    if "dtdma" not in skip:
                        self.dm(self.dtt[s, t], dts.ap.rearrange("p a b -> p (a b)"), r=[dts], w=[("dtt", s, t)])
                for sl in range(3):
                    if "xbc" in skip:
                        break
                    slab = slab_t[si % 3]
                    si += 1
                    self.dm(slab[:, :, :], W["xbc"][sl], w=[slab])
                    for c in range(4):
                        ch = sl * 4 + c
                        bank = 4 + n % 2
                        n += 1
                        for k in range(KD):
                            self.mm(self.ps(bank), slab[:, k, c * P:(c + 1) * P], hT[:, k, :], k == 0, k == KD - 1,
                                    r=[slab, (hT, k)], w=[PSK(bank)])
                        self.evac(stx[:, ch, :], self.ps(bank), r=[PSK(bank)], w=[(stx, ch)])
                for h in range(2):
                    if "xbc" in skip:
                        break
                    self.dm(self.xbcT[s, :, h * 6:(h + 1) * 6, tsl], stx[:, h * 6:(h + 1) * 6, :],
                            r=[(stx, c) for c in range(h * 6, h * 6 + 6)], w=[("xbcT", s, t, h)])

    def phase_amla(self, l, W):
        cfg, A, G = self.cfg, self.arena, self.G
        PSK = self.PSK
        NS, SQ, NT = cfg.NS, cfg.S, cfg.S // TT
        xt = A.alloc("mx", [KD, TT])
        sq_t = [A.alloc("msq%d" % i, [TT], BF16) for i in range(2)]
        rstd = A.alloc("mrstd", [TT])
        rh = A.alloc("mrh", [TT])
        hT = A.alloc("mh", [KD, TT], BF16)
        slab = A.alloc("mslab", [KD, 512], BF16)
        wsm = A.alloc("mwsm", [KD, 144], BF16)
        Wqbn = A.alloc("Wqbn", [4, 1024], BF16)
        Wqbr = A.alloc("Wqbr", [4, 512], BF16)
        Wqbrs = A.alloc("Wqbrs", [4, 512], BF16)
        Wkvn = A.alloc("Wkvn", [4, 1024], BF16)
        Wkvv = A.alloc("Wkvv", [4, 1024], BF16)
        for tl, nm in ((wsm, "sm"), (Wqbn, "qbn"), (Wqbr, "qbr"), (Wqbrs, "qbrs"), (Wkvn, "kvn"), (Wkvv, "kvv")):
            self.dm(tl[:, :, :], W[nm][0], w=[tl])
        la = A.alloc("mla", [4, TT])
        lan = A.alloc("mlan", [4, TT], BF16)
        stn = A.alloc("mstn", [MLA_H, TT], BF16)
        str_ = A.alloc("mstr", [MLA_H, TT], BF16, parts=64)
        vts = A.alloc("mvts", [4, MLA_H * P], BF16)
        c2 = A.alloc("mc2", [TT], parts=64)
        s2 = A.alloc("ms2", [TT], parts=64)
        t1 = A.alloc("mt1", [TT], parts=64)
        t2 = A.alloc("mt2", [TT], parts=64)
        kr0 = A.alloc("mkr0", [TT], parts=64)
        sqk = A.alloc("msqk", [TT], BF16, parts=64)
        sqn = A.alloc("msqn", [TT], BF16)
        n = 0
        for s in range(NS):
            for t in range(NT):
                tsl = slice(t * TT, (t + 1) * TT)
                self.load_xt(self.xT, s, t, xt)
                self.norm_h(xt, hT, G["attn"], l, rstd, sq_t)
                self.dm(c2[:, :], self.ropeC[s, :, tsl], w=[c2])
                self.dm(s2[:, :], self.ropeS[s, :, tsl], w=[s2])

                def lora(wname, gname):
                    self.dm(slab[:, :, :], W[wname][0], w=[slab])
                    for c in range(4):
                        bank = 1 + c % 2
                        for k in range(KD):
                            self.mm(self.ps(bank), slab[:, k, c * P:(c + 1) * P], hT[:, k, :], k == 0, k == KD - 1,
                                    r=[slab, (hT, k)], w=[PSK(bank)])
                        self.evac(la[:, c, :], self.ps(bank), r=[PSK(bank)], w=[(la, c)])
                    self.rms_stats([(la[:, c, :], (la, c), P) for c in range(4)], TT, 1.0 / 512, 0, rstd, sq_t)
                    for c in range(4):
                        self.stt(lan[:, c, :], la[:, c, :], G[gname][:, l, c:c + 1], rstd[:, :], ALU.mult, ALU.mult,
                                 r=[(la, c), rstd, G[gname]], w=[(lan, c)])

                lora("qa", "qa")
                for h in range(MLA_H):
                    bn, br, bs = (1, 2, 3) if h % 2 == 0 else (4, 5, 6)
                    for kk in range(4):
                        self.mm(self.ps(bn), Wqbn[:, kk, h * P:(h + 1) * P], lan[:, kk, :], kk == 0, kk == 3,
                                r=[Wqbn, (lan, kk)], w=[PSK(bn)])
                    for kk in range(4):
                        self.mm(self.ps(br)[0:64, :], Wqbr[:, kk, h * 64:(h + 1) * 64], lan[:, kk, :], kk == 0, kk == 3,
                                r=[Wqbr, (lan, kk)], w=[PSK(br)])
                    for kk in range(4):
                        self.mm(self.ps(bs)[0:64, :], Wqbrs[:, kk, h * 64:(h + 1) * 64], lan[:, kk, :], kk == 0, kk == 3,
                                r=[Wqbrs, (lan, kk)], w=[PSK(bs)])
                    self.rms_stats([(self.ps(bn), PSK(bn), P), (self.ps(br)[0:64, :], PSK(br), 64)], TT, 1.0 / 192, 7, rh, sq_t)
                    self.stt(stn[:, h, :], self.ps(bn), G["qn"][:, l, 0:1], rh[:, :], ALU.mult, ALU.mult,
                             r=[PSK(bn), rh, G["qn"]], w=[(stn, h)])
                    self.stt(t1[:, :], self.ps(br)[0:64, :], G["qr"][:, l, 0:1], rh[0:64, :], ALU.mult, ALU.mult,
                             r=[PSK(br), rh, G["qr"]], w=[t1])
                    self.stt(t2[:, :], self.ps(bs)[0:64, :], G["qrs"][:, l, 0:1], rh[0:64, :], ALU.mult, ALU.mult,
                             r=[PSK(bs), rh, G["qrs"]], w=[t2])
                    self.tt("pool", t1[:, :], t1[:, :], c2[:, :], ALU.mult, r=[t1, c2], w=[t1])
                    self.tt("pool", t2[:, :], t2[:, :], s2[:, :], ALU.mult, r=[t2, s2], w=[t2])
                    self.tt("pool", str_[:, h, :], t1[:, :], t2[:, :], ALU.add, r=[t1, t2], w=[(str_, h)])
                self.dm(self.qnT[s, :, :, tsl].rearrange("h p t -> p h t"), stn[:, :, :],
                        r=[(stn, h) for h in range(MLA_H)], w=[("qnT", s, t)])
                self.dm(self.qrT[s, :, :, tsl].rearrange("h p t -> p h t"), str_[:, :, :],
                        r=[(str_, h) for h in range(MLA_H)], w=[("qrT", s, t)])
                lora("kva", "kva")
                for k in range(KD):
                    self.mm(self.ps(2)[0:64, :], wsm[:, k, 16:80], hT[:, k, :], k == 0, k == KD - 1,
                            r=[wsm, (hT, k)], w=[PSK(2)])
                for k in range(KD):
                    self.mm(self.ps(3)[0:64, :], wsm[:, k, 80:144], hT[:, k, :], k == 0, k == KD - 1,
                            r=[wsm, (hT, k)], w=[PSK(3)])
                self.actf(sqk[:, :], self.ps(2)[0:64, :], AF.Square, r=[PSK(2)], w=[sqk])
                self.ts("dve", t1[:, :], self.ps(2)[0:64, :], G["kr"][:, l, 0:1], None, ALU.mult, None, r=[PSK(2), G["kr"]], w=[t1])
                self.ts("dve", t2[:, :], self.ps(3)[0:64, :], G["krs"][:, l, 0:1], None, ALU.mult, None, r=[PSK(3), G["krs"]], w=[t2])
                self.tt("pool", t1[:, :], t1[:, :], c2[:, :], ALU.mult, r=[t1, c2], w=[t1])
                self.tt("pool", t2[:, :], t2[:, :], s2[:, :], ALU.mult, r=[t2, s2], w=[t2])
                self.tt("pool", kr0[:, :], t1[:, :], t2[:, :], ALU.add, r=[t1, t2], w=[kr0])
                for h in range(MLA_H):
                    bn = 4 + h % 2
                    for kk in range(4):
                        self.mm(self.ps(bn), Wkvn[:, kk, h * P:(h + 1) * P], lan[:, kk, :], kk == 0, kk == 3,
                                r=[Wkvn, (lan, kk)], w=[PSK(bn)])
                    self.actf(sqn[:, :], self.ps(bn), AF.Square, r=[PSK(bn)], w=[sqn])
                    self.mm(self.ps(7), self.ones_b[:, :], sqn[:, :], True, False, r=[sqn, self.ones_b], w=[PSK(7)])
                    self.mm(self.ps(7), self.ones_b[0:64, :], sqk[:, :], False, True, r=[sqk, self.ones_b], w=[PSK(7)])
                    self.ts("dve", rh[:, :], self.ps(7), 1.0 / 192, EPS, ALU.mult, ALU.add, r=[PSK(7)], w=[rh])
                    self.actf(rh[:, :], rh[:, :], AF.Sqrt, r=[rh], w=[rh])
                    self.recip(rh[:, :], rh[:, :], r=[rh], w=[rh])
                    self.stt(stn[:, h, :], self.ps(bn), G["kn"][:, l, 0:1], rh[:, :], ALU.mult, ALU.mult,
                             r=[PSK(bn), rh, G["kn"]], w=[(stn, h)])
                    self.tt("pool", str_[:, h, :], kr0[:, :], rh[0:64, :], ALU.mult, r=[kr0, rh], w=[(str_, h)])
                self.dm(self.knT[s, :, :, tsl].rearrange("h p t -> p h t"), stn[:, :, :],
                        r=[(stn, h) for h in range(MLA_H)], w=[("knT", s, t)])
                self.dm(self.krT[s, :, :, tsl].rearrange("h p t -> p h t"), str_[:, :, :],
                        r=[(str_, h) for h in range(MLA_H)], w=[("krT", s, t)])
                for tb in range(4):
                    for half in range(2):
                        bank = 1 + n % 2
                        n += 1
                        for kk in range(4):
                            self.mm(self.ps(bank), lan[:, kk, tb * P:(tb + 1) * P], Wkvv[:, kk, half * 512:(half + 1) * 512],
                                    kk == 0, kk == 3, r=[Wkvv, (lan, kk)], w=[PSK(bank)])
                        self.evac(vts[:, tb, half * 512:(half + 1) * 512], self.ps(bank), r=[PSK(bank)], w=[(vts, tb)])
                self.dm(self.vt[s, tsl, :].rearrange("(tb p) c -> p tb c", p=P), vts[:, :, :],
                        r=[(vts, tb) for tb in range(4)], w=[("vt", s, t)])

    def phase_attn(self, l, W):
        cfg, A = self.cfg, self.arena
        PSK = self.PSK
        NS, SQ, NT = cfg.NS, cfg.S, cfg.S // TT
        NJ = SQ // P
        kn_t = [A.alloc("ckn%d" % i, [SQ], BF16) for i in range(2)]
        kr_t = [A.alloc("ckr%d" % i, [SQ], BF16, parts=64) for i in range(2)]
        v_t = [A.alloc("cv%d" % i, [NJ, P], BF16) for i in range(2)]
        qn_t = [A.alloc("cqn%d" % i, [TT], BF16) for i in range(2)]
        qr_t = [A.alloc("cqr%d" % i, [TT], BF16, parts=64) for i in range(2)]
        p_t = [A.alloc("cp%d" % i, [TT], BF16) for i in range(3)]
        rden = A.alloc("crden", [TT])
        o_t = [A.alloc("co%d" % i, [TT], BF16) for i in range(2)]
        scale = float(192 ** -0.5)
        hi = 0
        qi = 0
        pj = 0
        for s in range(NS):
            for h in range(MLA_H):
                kn, kr, v = kn_t[hi % 2], kr_t[hi % 2], v_t[hi % 2]
                hi += 1
                self.dm(kn[:, :], self.knT[s, h], w=[kn])
                self.dm(kr[:, :], self.krT[s, h], w=[kr])
                self.dm(v[:, :, :], self.vt[s].rearrange("(j p) c -> p j c", p=P)[:, :, h * P:(h + 1) * P], w=[v])
                for Q in range(NT):
                    qn, qr, ot = qn_t[qi % 2], qr_t[qi % 2], o_t[qi % 2]
                    bo, bd = (4, 5) if qi % 2 == 0 else (6, 7)
                    qi += 1
                    qsl = slice(Q * TT, (Q + 1) * TT)
                    self.dm(qn[:, :], self.qnT[s, h, :, qsl], w=[qn])
                    self.dm(qr[:, :], self.qrT[s, h, :, qsl], w=[qr])
                    nj = 4 * Q + 4
                    for j in range(nj):
                        r_ = j - 4 * Q
                        q0 = P * max(r_, 0)
                        bs_ = pj % 3
                        p = p_t[pj % 3]
                        pj += 1
                        self.mm(self.ps(bs_)[:, q0:TT], kn[:, j * P:(j + 1) * P], qn[:, q0:TT], True, False,
                                r=[kn, qn], w=[PSK(bs_)])
                        self.mm(self.ps(bs_)[:, q0:TT], kr[:, j * P:(j + 1) * P], qr[:, q0:TT], False, True,
                                r=[kr, qr], w=[PSK(bs_)])
                        self.actf(p[:, q0:TT], self.ps(bs_)[:, q0:TT], AF.Exp, r=[PSK(bs_)], w=[p], scale=scale)
                        if r_ >= 0:
                            self.tt("pool", p[:, q0:q0 + P], p[:, q0:q0 + P], self.mge_b[:, :], ALU.mult, r=[p, self.mge_b], w=[p])
                        self.mm(self.ps(bo)[:, q0:TT], v[:, j, :], p[:, q0:TT], j == 0, j == nj - 1, r=[v, p], w=[PSK(bo)])
                        self.mm(self.ps(bd)[:, q0:TT], self.ones_b[:, :], p[:, q0:TT], j == 0, j == nj - 1,
                                r=[self.ones_b, p], w=[PSK(bd)])
                    self.cp("dve", rden[:, :], self.ps(bd), r=[PSK(bd)], w=[rden])
                    self.recip(rden[:, :], rden[:, :], r=[rden], w=[rden])
                    self.tt("dve", ot[:, :], self.ps(bo), rden[:, :], ALU.mult, r=[PSK(bo), rden], w=[ot])
                    self.dm(self.mixT[s, :, 8 + h, qsl], ot[:, :], r=[ot], w=[("mixT", s, h, Q)])

    def phase_oproj(self, l, W):
        cfg, A = self.cfg, self.arena
        PSK = self.PSK
        NS, NT = cfg.NS, cfg.S // TT
        xt_t = [A.alloc("ox%d" % i, [KD, TT]) for i in range(2)]
        mx_t = [A.alloc("om%d" % i, [KD, TT], BF16) for i in range(2)]
        slab_t = [A.alloc("oslab%d" % i, [KD, 512], BF16) for i in range(2)]
        it = 0
        si = 0
        for s in range(NS):
            for t in range(NT):
                xt, mx = xt_t[it % 2], mx_t[it % 2]
                it += 1
                tsl = slice(t * TT, (t + 1) * TT)
                self.load_xt(self.xT, s, t, xt)
                for h in range(2):
                    self.dm(mx[:, h * 8:(h + 1) * 8, :], self.mixT[s, :, h * 8:(h + 1) * 8, tsl], w=[(mx, h)])
                for sl in range(4):
                    slab = slab_t[si % 2]
                    si += 1
                    self.dm(slab[:, :, :], W["out"][sl], w=[slab])
                    for c in range(4):
                        dk = sl * 4 + c
                        bank = 1 + dk % 2
                        for k in range(KD):
                            self.mm(self.ps(bank), slab[:, k, c * P:(c + 1) * P], mx[:, k, :], k == 0, k == KD - 1,
                                    r=[slab, (mx, k // 8)], w=[PSK(bank)])
                        self.tt("dve", xt[:, dk, :], self.ps(bank), xt[:, dk, :], ALU.add, r=[PSK(bank), (xt, dk)], w=[(xt, dk)])
                self.store_xt(self.xT, s, t, xt)
    def phase_ssd(self, l, W):
        cfg, A, G, S = self.cfg, self.arena, self.G, self.S
        PSK = self.PSK
        NS, SQ, NT = cfg.NS, cfg.S, cfg.S // TT
        ident_b, mge_f, mgt_f, ones_f = self.ident_b, self.mge_f, self.mgt_f, self.ones_f
        gn = A.alloc("sgn", [SSD_INNER])
        self.dm(gn[:, :].rearrange("p (a c) -> p a c", a=1), self.I["ssd_norm_g"][l:l + 1, :].partition_broadcast(P), w=[gn])
        xb = A.alloc("sxb", [12, TT + 3])
        acc_t = [A.alloc("sacc%d" % i, [TT]) for i in range(2)]
        cv = A.alloc("scv", [12, TT], BF16)
        xs_tok = A.alloc("sxs", [4, SSD_INNER], BF16)
        B_tok = A.alloc("sBt", [4, 256], BF16)
        sz = A.alloc("ssz", [4, SSD_INNER])
        sm = {}
        for nm in ("dtr", "dtv", "av", "acs", "tot", "E", "Wd", "dec", "dtw"):
            sm[nm] = A.alloc("s" + nm, [4, NHS])
        flat = lambda tl: tl.ap.rearrange("p a b -> p (a b)")
        xdt = A.alloc("sxdt", [4, SSD_INNER], BF16)
        xdtw = A.alloc("sxdtw", [4, SSD_INNER], BF16)
        st_f = A.alloc("sstf", [SSD_INNER])
        st_b = A.alloc("sstb", [SSD_INNER], BF16)
        cbm_t = [A.alloc("scbm%d" % i, [P], BF16) for i in range(2)]
        ra_t = [A.alloc("sra%d" % i, [8, P]) for i in range(2)]
        ed_t = [A.alloc("sed%d" % i, [TT]) for i in range(2)]
        MT_t = [A.alloc("sMT%d" % i, [4, P], BF16) for i in range(4)]
        y_t = [A.alloc("sy%d" % i, [TT]) for i in range(2)]
        xd_t = [A.alloc("sxd%d" % i, [TT]) for i in range(2)]
        junk = A.alloc("sjunk", [TT], BF16)
        ssq = A.alloc("sssq", [8])
        yn_t = [A.alloc("syn%d" % i, [TT], BF16) for i in range(2)]
        mixs = A.alloc("smixs", [8, TT], BF16)
        v3 = lambda ap: ap.rearrange("p (e d) -> p e d", d=HD)
        bc = lambda ap, n: ap[:, :, None].broadcast_to([P, ap.shape[1], n])
        ci = 0
        mi = 0
        for s in range(NS):
            S.pool(lambda e: e.memset(st_f[:, :], 0.0), w=[(st_f, 0), (st_f, 1)])
            S.pool(lambda e: e.memset(st_b[:, :], 0.0), w=[(st_b, 0), (st_b, 1)])
            for t in range(NT):
                tsl = slice(t * TT, (t + 1) * TT)
                if t == 0:
                    S.pool(lambda e: e.memset(xb[:, :, 0:3], 0.0), w=[xb])
                    for h in range(2):
                        self.dm(xb[:, h * 6:(h + 1) * 6, 3:TT + 3], self.xbcT[s, :, h * 6:(h + 1) * 6, 0:TT], w=[xb])
                else:
                    for h in range(2):
                        self.dm(xb[:, h * 6:(h + 1) * 6, :], self.xbcT[s, :, h * 6:(h + 1) * 6, t * TT - 3:(t + 1) * TT], w=[xb])
                self.dm(sz[:, :, :], self.szt[s, tsl, :].rearrange("(tb p) c -> p tb c", p=P), w=[sz])
                self.dm(flat(sm["dtr"]), self.dtt[s, t], w=[sm["dtr"]])
                for c in range(12):
                    acc = acc_t[c % 2]
                    self.ts("dve", acc[:, :], xb[:, c, 0:TT], G["cw"][:, l, 0, c:c + 1], G["cb"][:, l, c:c + 1], ALU.mult, ALU.add,
                            r=[xb, G["cw"], G["cb"]], w=[acc])
                    for j in range(1, 4):
                        self.stt(acc[:, :], xb[:, c, j:j + TT], G["cw"][:, l, j, c:c + 1], acc[:, :], ALU.mult, ALU.add,
                                 r=[xb, acc, G["cw"]], w=[acc])
                    self.actf(cv[:, c, :], acc[:, :], AF.Silu, r=[acc], w=[(cv, c)])
                for tb in range(4):
                    bank = 4 + tb % 2
                    for c in range(8):
                        self.tr(self.psb(bank)[:, c * P:(c + 1) * P], cv[:, c, tb * P:(tb + 1) * P], ident_b[:, :],
                                r=[(cv, c), ident_b], w=[PSK(bank)])
                    self.evac(xs_tok[:, tb, :], self.psb(bank), r=[PSK(bank)], w=[(xs_tok, tb)])
                    for g in range(2):
                        self.tr(self.psb(6)[:, (tb * 2 + g) * P:(tb * 2 + g + 1) * P], cv[:, 8 + g, tb * P:(tb + 1) * P], ident_b[:, :],
                                r=[(cv, 8 + g), ident_b], w=[PSK(6)])
                self.evac(B_tok.ap.rearrange("p a b -> p (a b)"), self.psb(6), r=[PSK(6)], w=[B_tok])
                dtv, av = sm["dtv"], sm["av"]
                self.tt("dve", dtv[:, :, :], sm["dtr"][:, :, :], G["dtb"][:, l:l + 1, :].broadcast_to([P, 4, NHS]), ALU.add,
                        r=[sm["dtr"], G["dtb"]], w=[dtv])
                self.actf(dtv[:, :, :], dtv[:, :, :], AF.Exp, r=[dtv], w=[dtv])
                self.ts("dve", dtv[:, :, :], dtv[:, :, :], 1.0, None, ALU.add, None, r=[dtv], w=[dtv])
                self.actf(dtv[:, :, :], dtv[:, :, :], AF.Ln, r=[dtv], w=[dtv])
                self.tt("dve", av[:, :, :], dtv[:, :, :], G["Aneg"][:, l:l + 1, :].broadcast_to([P, 4, NHS]), ALU.mult,
                        r=[dtv, G["Aneg"]], w=[av])
                self.mm(self.ps(3)[:, 0:64], mge_f[:, :], flat(av), True, True, r=[av, mge_f], w=[PSK(3)])
                self.mm(self.ps(3)[:, 64:128], ones_f[:, :], flat(av), True, True, r=[av, ones_f], w=[PSK(3)])
                self.cp("dve", flat(sm["acs"]), self.ps(3)[:, 0:64], r=[PSK(3)], w=[sm["acs"]])
                self.cp("dve", flat(sm["tot"]), self.ps(3)[:, 64:128], r=[PSK(3)], w=[sm["tot"]])
                self.actf(sm["E"][:, :, :], sm["acs"][:, :, :], AF.Exp, r=[sm["acs"]], w=[sm["E"]])
                self.tt("dve", sm["Wd"][:, :, :], sm["tot"][:, :, :], sm["acs"][:, :, :], ALU.subtract, r=[sm["tot"], sm["acs"]], w=[sm["Wd"]])
                self.actf(sm["Wd"][:, :, :], sm["Wd"][:, :, :], AF.Exp, r=[sm["Wd"]], w=[sm["Wd"]])
                self.actf(sm["dec"][:, :, :], sm["tot"][:, :, :], AF.Exp, r=[sm["tot"]], w=[sm["dec"]])
                self.tt("dve", sm["dtw"][:, :, :], dtv[:, :, :], sm["Wd"][:, :, :], ALU.mult, r=[dtv, sm["Wd"]], w=[sm["dtw"]])
                for tb in range(4):
                    self.tt("pool", v3(xdt[:, tb, :]), v3(xs_tok[:, tb, :]), bc(dtv[:, tb, :], HD), ALU.mult,
                            r=[(xs_tok, tb), dtv], w=[(xdt, tb)])
                    self.tt("pool", v3(xdtw[:, tb, :]), v3(xs_tok[:, tb, :]), bc(sm["dtw"][:, tb, :], HD), ALU.mult,
                            r=[(xs_tok, tb), sm["dtw"]], w=[(xdtw, tb)])
                for tb in range(4):
                    csl = slice(tb * P, (tb + 1) * P)
                    for g in range(2):
                        gs = slice(g * 512, (g + 1) * 512)
                        es_ = slice(g * 8, (g + 1) * 8)
                        cbm, ra, yt, xd, yn = cbm_t[ci % 2], ra_t[ci % 2], y_t[ci % 2], xd_t[ci % 2], yn_t[ci % 2]
                        sc = ssq[:, ci % 8:ci % 8 + 1]
                        sck = (ssq, ci % 8)
                        ci += 1
                        Bt, Ct = cv[:, 8 + g, csl], cv[:, 10 + g, csl]
                        self.mm(self.ps(0)[:, 0:P], Bt, Ct, True, True, r=[(cv, 8 + g), (cv, 10 + g)], w=[PSK(0)])
                        self.tt("dve", cbm[:, :], self.ps(0)[:, 0:P], mge_f[:, :], ALU.mult, r=[PSK(0), mge_f], w=[cbm])
                        self.tt("pool", ra[:, :, :], mge_f[:, None, :].broadcast_to([P, 8, P]), bc(av[:, tb, es_], P), ALU.mult,
                                r=[mge_f, av], w=[ra])
                        MTs = []
                        for hh in range(2):
                            ed = ed_t[hh]
                            MT = MT_t[mi % 4]
                            mi += 1
                            MTs.append(MT)
                            self.mm(self.ps(1 + hh), mgt_f[:, :], ra[:, hh * 4:(hh + 1) * 4, :].rearrange("p a b -> p (a b)"), True, True,
                                    r=[ra, mgt_f], w=[PSK(1 + hh)])
                            self.actf(ed[:, :], self.ps(1 + hh), AF.Exp, r=[PSK(1 + hh)], w=[ed])
                            self.tt("dve" if hh else "pool", MT[:, :, :], ed[:, :].rearrange("p (a b) -> p a b", a=4),
                                    cbm[:, None, :].broadcast_to([P, 4, P]), ALU.mult, r=[ed, cbm], w=[MT])
                        for e in range(8):
                            col = (g * 8 + e) * HD
                            self.mm(self.ps(4)[:, e * HD:(e + 1) * HD], MTs[e // 4][:, e % 4, :], xdt[:, tb, col:col + HD], True, True,
                                    r=[MTs[e // 4], (xdt, tb)], w=[PSK(4)])
                        self.mm(self.ps(5), Ct, st_b[:, gs], True, True, r=[(cv, 10 + g), (st_b, g)], w=[PSK(5)])
                        self.mm(self.ps(6), B_tok[:, tb, g * P:(g + 1) * P], xdtw[:, tb, gs], True, True,
                                r=[B_tok, (xdtw, tb)], w=[PSK(6)])
                        self.tt("dve", v3(yt[:, :]), v3(self.ps(5)), bc(sm["E"][:, tb, es_], HD), ALU.mult,
                                r=[PSK(5), sm["E"]], w=[yt])
                        self.tt("dve", yt[:, :], self.ps(4), yt[:, :], ALU.add, r=[PSK(4), yt], w=[yt])
                        self.tt("pool", v3(xd[:, :]), v3(xs_tok[:, tb, gs]), bc(G["dsk"][:, l, es_], HD), ALU.mult,
                                r=[(xs_tok, tb), G["dsk"]], w=[xd])
                        self.tt("pool", yt[:, :], yt[:, :], xd[:, :], ALU.add, r=[yt, xd], w=[yt])
                        self.tt("pool", yt[:, :], yt[:, :], sz[:, tb, gs], ALU.mult, r=[yt, sz], w=[yt])
                        self.actf(junk[:, :], yt[:, :], AF.Square, r=[yt], w=[junk, sck], accum=sc)
                        self.ts("dve", sc, sc, 1.0 / 512, EPS, ALU.mult, ALU.add, r=[sck], w=[sck])
                        self.actf(sc, sc, AF.Sqrt, r=[sck], w=[sck])
                        self.recip(sc, sc, r=[sck], w=[sck])
                        self.stt(yn[:, :], yt[:, :], sc, gn[:, gs], ALU.mult, ALU.mult, r=[yt, sck, gn], w=[yn])
                        for c4 in range(4):
                            self.tr(self.psb(7)[:, c4 * P:(c4 + 1) * P], yn[:, c4 * P:(c4 + 1) * P], ident_b[:, :],
                                    r=[yn, ident_b], w=[PSK(7)])
                        self.evac(mixs[:, g * 4:(g + 1) * 4, csl], self.psb(7)[:, 0:512].rearrange("p (a b) -> p a b", a=4),
                                  r=[PSK(7)], w=[(mixs, g, tb)])
                        self.tt("pool", v3(st_f[:, gs]), v3(st_f[:, gs]), bc(sm["dec"][:, tb, es_], HD), ALU.mult,
                                r=[(st_f, g), sm["dec"]], w=[(st_f, g)])
                        self.tt("dve", st_f[:, gs], self.ps(6), st_f[:, gs], ALU.add, r=[PSK(6), (st_f, g)], w=[(st_f, g)])
                        self.cp("pool", st_b[:, gs], st_f[:, gs], r=[(st_f, g)], w=[(st_b, g)])
                self.dm(self.mixT[s, :, 0:8, tsl], mixs[:, :, :],
                        r=[(mixs, g, tb) for g in range(2) for tb in range(4)], w=[("mixT", s, "ssd", t)])


def run(inputs, cfg, n_cores):
    b = Builder(cfg)
    nc = b.build()
    names = [n for n in b.dram if n in inputs]
    in_maps = []
    for c in range(n_cores):
        m = {}
        for n in names:
            a = inputs[n]
            if n in ("x", "mem", "positions"):
                a = np.ascontiguousarray(a[c * cfg.NS:(c + 1) * cfg.NS])
            m[n] = a
        in_maps.append(m)
    res = run_bass_kernel_spmd(nc, in_maps, core_ids=list(range(n_cores)))
    return np.concatenate([r["out"] for r in res.results], axis=0)


def kernel(**inputs):
    inputs = {k: np.asarray(v) for k, v in inputs.items()}
    cfg = Cfg(NS=2, S=2048, L=4)
    return run(inputs, cfg, 8)
```

```python
import numpy as np
from contextlib import ExitStack
import concourse.bass as bass
import concourse.mybir as mybir
from concourse.bass_utils import run_bass_kernel_spmd

F32 = mybir.dt.float32
BF16 = mybir.dt.bfloat16
I32 = mybir.dt.int32
AF = mybir.ActivationFunctionType
ALU = mybir.AluOpType
P = 128
EPS = 1e-6


def _dsize(dt):
    return 2 if dt == BF16 else 4


class Op:
    __slots__ = ("eng", "fn", "deps", "dma", "seq", "sem", "semval", "signal", "gid", "seg", "dur", "nbytes",
                 "bg", "prev", "rt", "fin", "nd", "users")


class Tile:
    _n = 0

    def __init__(self, ap, name):
        self.ap = ap
        Tile._n += 1
        self.key = (name, Tile._n)

    def __getitem__(self, idx):
        return self.ap[idx]


def _key(r):
    if isinstance(r, Tile):
        return r.key
    if isinstance(r, tuple) and isinstance(r[0], Tile):
        return (r[0].key,) + tuple(r[1:])
    return r


class Sched:
    ENGS = ("pe", "act", "dve", "pool", "sp")
    QSEMS = (("sp", 32), ("act", 8), ("pool", 8))
    HOP = 0.4
    DMA_BW = 120e3
    DMA_LAT = 1.8

    def __init__(self):
        self.all = []
        self.last_w = {}
        self.readers = {}
        self.seg = 0
        import os
        self.reorder = os.environ.get("KREORDER", "1") == "1"

    def add(self, eng, fn, r=(), w=(), dma=False, dur=0.2, nbytes=0, bg=False):
        op = Op()
        op.eng, op.fn, op.dma, op.dur, op.nbytes, op.bg = eng, fn, dma, dur, nbytes, bg
        op.signal = False
        op.seq = op.sem = op.semval = op.prev = None
        deps = set()
        for x in r:
            k = _key(x)
            for lw in self.last_w.get(k, ()):
                deps.add(lw)
        cowrite = set()
        for x in w:
            k = _key(x)
            lws = self.last_w.get(k, ())
            rds = self.readers.get(k, ())
            if dma and lws and not rds and all(o.dma for o in lws):
                cowrite.add(k)
                for lw in lws:
                    deps |= lw.deps
                continue
            for lw in lws:
                deps.add(lw)
            for rd in rds:
                deps.add(rd)
        deps.discard(op)
        op.deps = deps
        for x in r:
            self.readers.setdefault(_key(x), []).append(op)
        for x in w:
            k = _key(x)
            if k in cowrite:
                self.last_w[k] = list(self.last_w[k]) + [op]
            else:
                self.last_w[k] = [op]
            self.readers[k] = []
        op.gid = len(self.all)
        op.seg = self.seg
        self.all.append(op)
        return op

    def barrier(self):
        self.seg += 1
        self.last_w = {}
        self.readers = {}

    def pe(self, fn, r=(), w=(), **kw):
        return self.add("pe", fn, r, w, **kw)

    def act(self, fn, r=(), w=(), **kw):
        return self.add("act", fn, r, w, **kw)

    def dve(self, fn, r=(), w=(), **kw):
        return self.add("dve", fn, r, w, **kw)

    def pool(self, fn, r=(), w=(), **kw):
        return self.add("pool", fn, r, w, **kw)

    def dma(self, fn, r=(), w=(), q="sp", **kw):
        return self.add(q, fn, r, w, dma=True, **kw)

    def _schedule(self, ops):
        import heapq
        order = {e: [] for e in self.ENGS}
        import os
        lo, hi = int(os.environ.get("KR_LO", "0")), int(os.environ.get("KR_HI", "100000"))
        if not self.reorder or not (lo <= ops[0].seg <= hi):
            for op in ops:
                order[op.eng].append(op)
            return order
        for op in ops:
            op.users = []
            op.nd = 0
            op.rt = 0.0
        seg = ops[0].seg
        for op in ops:
            for d in op.deps:
                if d.seg == seg:
                    d.users.append(op)
                    op.nd += 1
        NC = 4
        heap = {e: [] for e in self.ENGS}
        front = {e: [] for e in self.ENGS}

        def push(op):
            ent = (op.gid + (10 ** 9 if op.bg else 0), op.gid, op)
            f = front[op.eng]
            if len(f) < NC:
                f.append(ent)
            else:
                m = max(f)
                if ent < m:
                    f.remove(m)
                    f.append(ent)
                    heapq.heappush(heap[op.eng], m)
                else:
                    heapq.heappush(heap[op.eng], ent)

        for op in ops:
            if op.nd == 0:
                push(op)
        t_eng = {e: 0.0 for e in self.ENGS}
        pipe = 0.0
        remaining = len(ops)
        while remaining:
            best = None
            for e in self.ENGS:
                for ent in front[e]:
                    st = max(t_eng[e], ent[2].rt)
                    key = (st, ent[0])
                    if best is None or key < best[0]:
                        best = (key, e, ent)
            (st, _), e, ent = best
            op = ent[2]
            front[e].remove(ent)
            if heap[e]:
                front[e].append(heapq.heappop(heap[e]))
            if op.dma:
                t_eng[e] = st + 0.06
                p0 = max(st, pipe)
                xfer = op.nbytes / self.DMA_BW
                pipe = p0 + xfer
                op.fin = p0 + xfer + self.DMA_LAT
            else:
                op.fin = st + op.dur
                t_eng[e] = op.fin
            order[e].append(op)
            remaining -= 1
            for u in op.users:
                u.nd -= 1
                hop = 0.0 if (u.eng == "pe" and op.eng == "pe" and not u.dma) else self.HOP
                if op.fin + hop > u.rt:
                    u.rt = op.fin + hop
                if u.nd == 0:
                    push(u)
        return order

    def emit(self, nc, es):
        ENGS = self.ENGS
        nseg = self.seg + 1
        segs = [[] for _ in range(nseg)]
        for op in self.all:
            segs[op.seg].append(op)
        final = {e: [] for e in ENGS}
        segpos = {e: [] for e in ENGS}
        for si, ops in enumerate(segs):
            if not ops:
                continue
            order = self._schedule(ops)
            for e in ENGS:
                if order[e]:
                    segpos[e].append((len(final[e]), si))
                    final[e].extend(order[e])
        qbase = {}
        nd = 0
        for q, n in self.QSEMS:
            qbase[q] = (nd, n)
            nd += n
        dma_total = [0] * nd
        for q, n in self.QSEMS:
            base = qbase[q][0]
            dl = [op for op in final[q] if op.dma]
            for i, op in enumerate(dl):
                op.sem = base + (i % n)
                op.semval = 16 * (i // n + 1)
                op.prev = dl[i - n] if i >= n else None
                dma_total[op.sem] = op.semval
        for op in self.all:
            for d in op.deps:
                if not d.dma and not (d.eng == "pe" and op.eng == "pe" and not op.dma):
                    d.signal = True
        for e in ENGS:
            for (pos, si), nxt in zip(segpos[e], segpos[e][1:] + [(len(final[e]), None)]):
                for op in reversed(final[e][pos:nxt[0]]):
                    if not op.dma:
                        op.signal = True
                        break
        for e in ENGS:
            n = 0
            for op in final[e]:
                if (not op.dma) and op.signal:
                    n += 1
                    op.seq = n
        bar_e = [dict() for _ in range(nseg + 1)]
        bar_d = [dict() for _ in range(nseg + 1)]
        cur_e, cur_d = {}, {}
        byseg_e = [dict() for _ in range(nseg)]
        byseg_d = [dict() for _ in range(nseg)]
        for e in ENGS:
            for op in final[e]:
                if op.dma:
                    if byseg_d[op.seg].get(op.sem, 0) < op.semval:
                        byseg_d[op.seg][op.sem] = op.semval
                elif op.signal:
                    if byseg_e[op.seg].get(e, 0) < op.seq:
                        byseg_e[op.seg][e] = op.seq
        for si in range(nseg):
            bar_e[si] = dict(cur_e)
            bar_d[si] = dict(cur_d)
            for k, v in byseg_e[si].items():
                cur_e[k] = max(cur_e.get(k, 0), v)
            for k, v in byseg_d[si].items():
                cur_d[k] = max(cur_d.get(k, 0), v)
        esem = {e: es.enter_context(nc.semaphore("sem_" + e)) for e in ENGS}
        dsem = [es.enter_context(nc.semaphore("dsem%d" % i)) for i in range(nd)]
        block = es.enter_context(nc.Block())
        sections = {"pe": block.tensor, "act": block.scalar, "dve": block.vector,
                    "pool": block.gpsimd, "sp": block.sync}

        def make(ename):
            def body(eng):
                seen_e = {}
                seen_d = {}

                def wait_e(en, v):
                    if seen_e.get(en, 0) < v:
                        eng.wait_ge(esem[en], v)
                        seen_e[en] = v

                def wait_d(s_, v):
                    if seen_d.get(s_, 0) < v:
                        eng.wait_ge(dsem[s_], v)
                        seen_d[s_] = v

                starts = dict(segpos[ename])
                for i, op in enumerate(final[ename]):
                    if i in starts and starts[i] > 0:
                        si = starts[i]
                        for en, v in bar_e[si].items():
                            wait_e(en, v)
                        for s_, v in bar_d[si].items():
                            wait_d(s_, v)
                    need_e = {}
                    need_d = {}
                    for d in op.deps:
                        if d.seg != op.seg:
                            continue
                        if d.dma:
                            if need_d.get(d.sem, 0) < d.semval:
                                need_d[d.sem] = d.semval
                        else:
                            if d.eng == "pe" and ename == "pe" and not op.dma:
                                continue
                            if need_e.get(d.eng, 0) < d.seq:
                                need_e[d.eng] = d.seq
                    if op.dma and op.prev is not None:
                        if need_d.get(op.prev.sem, 0) < op.prev.semval:
                            need_d[op.prev.sem] = op.prev.semval
                    for en, v in need_e.items():
                        wait_e(en, v)
                    for s_, v in need_d.items():
                        wait_d(s_, v)
                    inst = op.fn(eng)
                    if op.dma:
                        inst.then_inc(dsem[op.sem], 16)
                    elif op.signal:
                        inst.then_inc(esem[ename], 1)
                if ename == "sp":
                    for s_ in range(nd):
                        if dma_total[s_]:
                            eng.wait_ge(dsem[s_], dma_total[s_])
            return body

        for e in ENGS:
            sections[e](make(e))


class Arena:
    def __init__(self, ap, ncol):
        self.ap = ap
        self.ncol = ncol
        self.off = 0
        self.floor = 0

    def reset(self):
        self.off = self.floor

    def freeze(self):
        self.floor = self.off

    def alloc(self, name, free_shape, dt=F32, parts=P):
        n = int(np.prod(free_shape))
        words = (n * _dsize(dt) + 3) // 4
        assert self.off + words <= self.ncol, ("SBUF arena overflow", name, self.off, words, self.ncol)
        ap = self.ap[0:parts, self.off:self.off + words]
        self.off += words
        if dt != F32:
            ap = ap.bitcast(dt)[:, 0:n]
        if len(free_shape) == 2:
            ap = ap.rearrange("p (a b) -> p a b", a=free_shape[0])
        elif len(free_shape) == 3:
            ap = ap.rearrange("p (a b c) -> p a b c", a=free_shape[0], b=free_shape[1])
        return Tile(ap, name)


D = 2048
KD = 16
SSD_INNER = 1024
NHS = 16
HD = 64
NST = 128
CONV_DIM = 1536
MLA_H = 8
Q_LORA = 512
KV_LORA = 512
XH = 4
MEM = 256
FF = 5632
KF = 44
IN_COLS = 3664
C0, C1, C2, C3, C4 = 1024, 2560, 2576, 3088, 3600
TT = 512


class Cfg:
    def __init__(self, NS=2, S=2048, L=4, phases=("mix", "xattn", "ffn")):
        self.NS, self.S, self.L = NS, S, L
        self.phases = tuple(phases)


PI = float(np.pi)


class Builder:
    def __init__(self, cfg):
        self.cfg = cfg
        self.nc = bass.Bass("TRN2", target_bir_lowering=False)
        self.S = Sched()
        self.dram = {}
        self.ev_i = 0

    def din(self, name, shape, dt=F32):
        t = self.nc.dram_tensor(name, list(shape), dt, kind="ExternalInput").ap()
        self.dram[name] = t
        return t

    def dscr(self, name, shape, dt=F32):
        return self.nc.dram_tensor(name, list(shape), dt, kind="Internal").ap()

    def ps(self, b):
        return self.psum[:, b, :]

    def psb(self, b):
        return self.psum[:, b, :].bitcast(BF16)

    PSK = staticmethod(lambda b: ("psum", b))

    @staticmethod
    def _fs(ap):
        n = 1
        for d in ap.shape[1:]:
            n *= d
        return n

    def mm(self, out, lhsT, rhs, st, sp, r, w):
        dur = self._fs(out) * (4.0 if lhsT.dtype == F32 else 1.0) / 2400.0 + 0.02
        self.S.pe(lambda e: e.matmul(out, lhsT=lhsT, rhs=rhs, start=st, stop=sp), r=r, w=w, dur=dur)

    def tr(self, out, in_, ident, r, w):
        self.S.pe(lambda e: e.transpose(out=out, in_=in_, identity=ident), r=r, w=w,
                  dur=0.12 if in_.dtype == F32 else 0.07)

    def actf(self, out, in_, func, r, w, scale=None, bias=None, accum=None):
        kw = {}
        if scale is not None:
            kw["scale"] = scale
        if bias is not None:
            kw["bias"] = bias
        if accum is not None:
            kw["accum_out"] = accum
        self.S.act(lambda e: e.activation(out=out, in_=in_, func=func, **kw), r=r, w=w, dur=self._fs(out) / 960.0 + 0.22)

    def _edur(self, eng, out):
        return self._fs(out) / (480.0 if eng == "pool" else 960.0) + 0.2

    def tt(self, eng, out, in0, in1, op, r, w):
        self.S.add(eng, lambda e: e.tensor_tensor(out=out, in0=in0, in1=in1, op=op), r=r, w=w, dur=self._edur(eng, out))

    def ts(self, eng, out, in0, s1, s2, op0, op1, r, w):
        if op1 is None:
            self.S.add(eng, lambda e: e.tensor_scalar(out=out, in0=in0, scalar1=s1, scalar2=None, op0=op0), r=r, w=w,
                       dur=self._edur(eng, out))
        else:
            self.S.add(eng, lambda e: e.tensor_scalar(out=out, in0=in0, scalar1=s1, scalar2=s2, op0=op0, op1=op1), r=r, w=w,
                       dur=self._edur(eng, out))

    def stt(self, out, in0, scalar, in1, op0, op1, r, w):
        self.S.dve(lambda e: e.scalar_tensor_tensor(out=out, in0=in0, scalar=scalar, in1=in1, op0=op0, op1=op1), r=r, w=w,
                   dur=self._fs(out) / 960.0 + 0.2)

    def cp(self, eng, out, in_, r, w, bg=False):
        if eng == "act":
            self.S.act(lambda e: e.copy(out=out, in_=in_), r=r, w=w, dur=self._fs(out) / 960.0 + 0.22, bg=bg)
        else:
            self.S.add(eng, lambda e: e.tensor_copy(out=out, in_=in_), r=r, w=w, dur=self._edur(eng, out), bg=bg)

    def evac(self, out, in_, r, w):
        self.ev_i += 1
        self.cp("act" if self.ev_i % 2 else "dve", out, in_, r, w)

    def dm(self, out, in_, r=(), w=(), q="sp", bg=False):
        nb = 1
        for d in out.shape:
            nb *= d
        nb *= _dsize(out.dtype)
        self.S.dma(lambda e: e.dma_start(out=out, in_=in_), r=r, w=w, q=q, nbytes=nb, bg=bg)

    def recip(self, out, in_, r, w):
        self.S.dve(lambda e: e.reciprocal(out=out, in_=in_), r=r, w=w, dur=self._fs(out) / 960.0 + 0.2)

    def convert(self, src, K, C, CW, name):
        S = self.S
        ncb = C // CW
        dst = self.dscr(name, [ncb, P, K, CW], BF16)
        pieces = src if isinstance(src, list) else [(src, lambda st: st[:, 0:C])]
        for k in range(K):
            b = self.cv_i % 3
            self.cv_i += 1
            st, bt = self.cv_f32[b], self.cv_bf[b]
            for (ap, dfn) in pieces:
                self.dm(dfn(st), ap[k * P:(k + 1) * P], r=[], w=[st])
            eng = ("act", "dve", "pool")[self.cv_i % 3]
            self.cp(eng, bt[:, 0:C], st[:, 0:C], r=[st], w=[bt])
            d = dst[:, :, k, :].rearrange("cb p c -> p cb c")
            self.cv_pending.append((d, bt, C, CW, name))
            if len(self.cv_pending) > 2:
                self.cv_flush(1)
        return dst

    def cv_flush(self, n):
        for _ in range(n):
            if not self.cv_pending:
                return
            d, bt, C, CW, name = self.cv_pending.pop(0)
            self.dm(d, bt[:, 0:C].rearrange("p (cb c) -> p cb c", c=CW), r=[bt], w=[("dram", name)])

    def convert_jobs(self, src, K, C, CW, name):
        assert C % 512 == 0
        dst = self.dscr(name, [C // CW, P, K, CW], BF16)
        jobs = []
        for k in range(K):
            for j in range(C // 512):
                jobs.append((src, dst, k, j, CW, name))
        return dst, jobs

    def run_job(self, job, bg=True):
        src, dst, k, j, CW, name = job
        i = self.bg_i
        self.bg_i += 1
        st, bt = self.bg_f32[i % 4], self.bg_bf[i % 4]
        self.dm(st[:, :], src[k * P:(k + 1) * P, j * 512:(j + 1) * 512], w=[st], bg=bg)
        self.cp(self.bg_eng, bt[:, :], st[:, :], r=[st], w=[bt], bg=bg)
        if CW >= 512:
            c0 = j * 512
            d = dst[c0 // CW, :, k, c0 % CW:c0 % CW + 512]
            self.dm(d, bt[:, :], r=[bt], w=[("dram", name)], bg=bg)
        else:
            n = 512 // CW
            d = dst[j * n:(j + 1) * n, :, k, :].rearrange("cb p c -> p cb c")
            self.dm(d, bt[:, :].rearrange("p (cb c) -> p cb c", c=CW), r=[bt], w=[("dram", name)], bg=bg)

    def bg_take(self, key, n=None, bg=True):
        import os
        lst = self.bg_jobs.get(key)
        if not lst or (bg and os.environ.get("KBG", "1") == "0"):
            return
        if n is None or n > len(lst):
            n = len(lst)
        for job in lst[:n]:
            self.run_job(job, bg=bg)
        del lst[:n]

    def bg_need(self, key):
        if self.bg_jobs.get(key):
            self.bg_take(key, None, bg=False)
            self.S.barrier()

    def gain_T(self, dst2d, src_rows, rows, parts):
        stg = self.gstg[self.g_i % 2]
        bank = self.g_i % 4
        self.g_i += 1
        srcs = src_rows if isinstance(src_rows, list) else [(src_rows, 0, parts)]
        for ap, c0, w in srcs:
            self.dm(stg[0:rows, c0:c0 + w], ap, w=[stg])
        self.tr(self.ps(bank)[0:parts, 0:rows], stg[0:rows, 0:parts], self.ident_f[0:rows, 0:rows],
                r=[stg, self.ident_f], w=[self.PSK(bank)])
        self.cp("dve", dst2d, self.ps(bank)[0:parts, 0:rows], r=[self.PSK(bank)], w=[("gain", self.g_i)])

    def gain_fm(self, name, src, L, nk, parts=P, lo=0):
        t = self.arena.alloc(name, [L, nk], parts=parts)
        self.gain_T(t.ap.rearrange("p l k -> p (l k)"),
                    src[:, lo:lo + nk * parts].rearrange("l (k p) -> (l k) p", p=parts), L * nk, parts)
        return t

    def gain_rep(self, name, src, L, n):
        t = self.arena.alloc(name, [L, n])
        self.dm(t.ap.rearrange("p l n -> p (l n)")[:, None, :],
                src.rearrange("(a l) n -> a (l n)", a=1).partition_broadcast(P), w=[t])
        return t

    def rms_stats(self, chunks, ncols, scale, bank, rstd, sq_t):
        n = len(chunks)
        for i, (ap, key, parts) in enumerate(chunks):
            sq = sq_t[i % 2]
            self.actf(sq[0:parts, 0:ncols], ap, AF.Square, r=[key], w=[sq])
            self.mm(self.ps(bank)[:, 0:ncols], self.ones_b[0:parts, :], sq[0:parts, 0:ncols], i == 0, i == n - 1,
                    r=[sq, self.ones_b], w=[self.PSK(bank)])
        self.ts("dve", rstd[:, 0:ncols], self.ps(bank)[:, 0:ncols], scale, EPS, ALU.mult, ALU.add,
                r=[self.PSK(bank)], w=[rstd])
        self.actf(rstd[:, 0:ncols], rstd[:, 0:ncols], AF.Sqrt, r=[rstd], w=[rstd])
        self.recip(rstd[:, 0:ncols], rstd[:, 0:ncols], r=[rstd], w=[rstd])

    def load_xt(self, xT, s, t, xt):
        tsl = slice(t * TT, (t + 1) * TT)
        for h in range(2):
            self.dm(xt[:, h * 8:(h + 1) * 8, :], xT[s, :, h * 8:(h + 1) * 8, tsl],
                    r=[("xT", s, t, h)], w=[(xt, k) for k in range(h * 8, h * 8 + 8)])

    def store_xt(self, xT, s, t, xt):
        tsl = slice(t * TT, (t + 1) * TT)
        for h in range(2):
            self.dm(xT[s, :, h * 8:(h + 1) * 8, tsl], xt[:, h * 8:(h + 1) * 8, :],
                    r=[(xt, k) for k in range(h * 8, h * 8 + 8)], w=[("xT", s, t, h)])

    def norm_h(self, xt, hT, g, l, rstd, sq_t, bank=0):
        self.rms_stats([(xt[:, k, :], (xt, k), P) for k in range(KD)], TT, 1.0 / D, bank, rstd, sq_t)
        for k in range(KD):
            self.stt(hT[:, k, :], xt[:, k, :], g[:, l, k:k + 1], rstd[:, :], ALU.mult, ALU.mult,
                     r=[(xt, k), rstd, g], w=[(hT, k)])
    def build(self):
        cfg, nc, S = self.cfg, self.nc, self.S
        NS, SQ, L = cfg.NS, cfg.S, cfg.L
        NT = SQ // TT
        ph = cfg.phases
        es = ExitStack()
        self.es = es
        es.enter_context(nc.allow_non_contiguous_dma(reason="tiny per-layer vectors / small tiles"))
        I = {}
        I["x"] = self.din("x", [NS, SQ, D])
        if "mix" in ph:
            I["positions"] = self.din("positions", [NS, SQ], I32)
            for n, sh in (("attn_norm_g", [L, D]), ("w_in", [L, D, IN_COLS]), ("conv_w", [L, 4, CONV_DIM]),
                          ("conv_b", [L, CONV_DIM]), ("dt_bias", [L, NHS]), ("a_log", [L, NHS]), ("d_skip", [L, NHS]),
                          ("ssd_norm_g", [L, SSD_INNER]), ("q_a_norm_g", [L, Q_LORA]), ("w_q_b", [L, Q_LORA, MLA_H * 192]),
                          ("kv_a_norm_g", [L, KV_LORA]), ("w_kv_b", [L, KV_LORA, MLA_H * 256]), ("mla_q_norm_g", [L, 192]),
                          ("mla_k_norm_g", [L, 192]), ("w_out", [L, D, D])):
                I[n] = self.din(n, sh)
        if "xattn" in ph:
            I["mem"] = self.din("mem", [NS, MEM, D])
            for n, sh in (("xattn_norm_g", [L, D]), ("mem_norm_g", [L, D]), ("w_xq", [L, D, 512]), ("w_xk", [L, D, 512]),
                          ("w_xv", [L, D, 512]), ("xq_norm_g", [L, 128]), ("xk_norm_g", [L, 128]), ("w_xo", [L, 512, D])):
                I[n] = self.din(n, sh)
        if "ffn" in ph:
            for n, sh in (("ffn_norm_g", [L, D]), ("w_gate", [L, D, FF]), ("w_up", [L, D, FF]), ("w_down", [L, FF, D])):
                I[n] = self.din(n, sh)
        out = nc.dram_tensor("out", [NS, SQ, D], F32, kind="ExternalOutput").ap()
        self.I = I
        xT = self.dscr("xT", [NS, P, KD, SQ])
        self.xT = xT
        ACOLS = 53000
        arena_t = es.enter_context(nc.sbuf_tensor("arena", [P, ACOLS], F32))
        self.arena = A = Arena(arena_t[:, :], ACOLS)
        self.psum = es.enter_context(nc.psum_tensor("psum", [P, 8, 512], F32))
        PSK = self.PSK

        ones_f = A.alloc("ones_f", [P])
        ident_f = A.alloc("ident_f", [P])
        mge_f = A.alloc("mge_f", [P])
        mgt_f = A.alloc("mgt_f", [P])
        ident_b = A.alloc("ident_b", [P], BF16)
        ones_b = A.alloc("ones_b", [P], BF16)
        mge_b = A.alloc("mge_b", [P], BF16)
        S.pool(lambda e: e.memset(ones_f[:, :], 1.0), w=[ones_f])
        S.pool(lambda e: e.affine_select(out=ident_f[:, :], in_=ones_f[:, :], pattern=[[1, P]],
                                         compare_op=ALU.is_equal, fill=0.0, base=0, channel_multiplier=-1),
               r=[ones_f], w=[ident_f])
        S.pool(lambda e: e.affine_select(out=mge_f[:, :], in_=ones_f[:, :], pattern=[[1, P]],
                                         compare_op=ALU.is_ge, fill=0.0, base=0, channel_multiplier=-1),
               r=[ones_f], w=[mge_f])
        S.pool(lambda e: e.affine_select(out=mgt_f[:, :], in_=ones_f[:, :], pattern=[[-1, P]],
                                         compare_op=ALU.is_ge, fill=0.0, base=-1, channel_multiplier=1),
               r=[ones_f], w=[mgt_f])
        self.cp("dve", ident_b[:, :], ident_f[:, :], r=[ident_f], w=[ident_b])
        self.cp("dve", ones_b[:, :], ones_f[:, :], r=[ones_f], w=[ones_b])
        self.cp("dve", mge_b[:, :], mge_f[:, :], r=[mge_f], w=[mge_b])
        self.ones_b, self.ident_f, self.ident_b, self.ones_f = ones_b, ident_f, ident_b, ones_f
        self.mge_f, self.mgt_f, self.mge_b = mge_f, mgt_f, mge_b
        G = {}
        self.gstg = [A.alloc("gstg%d" % i, [P]) for i in range(2)]
        self.g_i = 0
        if "mix" in ph:
            G["attn"] = self.gain_fm("g_attn", I["attn_norm_g"], L, KD)
            cw = A.alloc("cw", [L, 4, 12])
            for l in range(L):
                self.gain_T(cw[:, l, :, :].rearrange("p j c -> p (j c)"),
                            I["conv_w"][l].rearrange("j (c p) -> (j c) p", p=P), 48, P)
            G["cw"] = cw
            G["cb"] = self.gain_fm("cb", I["conv_b"], L, 12)
            G["dtb"] = self.gain_rep("dtb", I["dt_bias"], L, NHS)
            G["dsk"] = self.gain_rep("dsk", I["d_skip"], L, NHS)
            alog = self.gain_rep("alog", I["a_log"], L, NHS)
            Aneg = A.alloc("Aneg", [L, NHS])
            self.actf(Aneg[:, :, :], alog[:, :, :], AF.Exp, r=[alog], w=[Aneg])
            self.ts("dve", Aneg[:, :, :], Aneg[:, :, :], -1.0, None, ALU.mult, None, r=[Aneg], w=[Aneg])
            G["Aneg"] = Aneg
            G["qa"] = self.gain_fm("g_qa", I["q_a_norm_g"], L, 4)
            G["kva"] = self.gain_fm("g_kva", I["kv_a_norm_g"], L, 4)
            for nm, src in (("q", I["mla_q_norm_g"]), ("k", I["mla_k_norm_g"])):
                G[nm + "n"] = self.gain_fm("g_%sn" % nm, src, L, 1)
                G[nm + "r"] = self.gain_fm("g_%sr" % nm, src, L, 1, parts=64, lo=128)
                sw = A.alloc("g_%srs" % nm, [L, 1], parts=64)
                self.gain_T(sw.ap.rearrange("p l k -> p (l k)"), [(src[:, 160:192], 0, 32), (src[:, 128:160], 32, 32)], L, 64)
                G[nm + "rs"] = sw
        if "xattn" in ph:
            G["xattn"] = self.gain_fm("g_xattn", I["xattn_norm_g"], L, KD)
            G["mem"] = self.gain_fm("g_mem", I["mem_norm_g"], L, KD)
            G["xq"] = self.gain_fm("g_xq", I["xq_norm_g"], L, 1)
            G["xk"] = self.gain_fm("g_xk", I["xk_norm_g"], L, 1)
        if "ffn" in ph:
            G["ffn"] = self.gain_fm("g_ffn", I["ffn_norm_g"], L, KD)
        self.G = G
        self.bg_f32 = [A.alloc("bgf%d" % i, [512]) for i in range(4)]
        self.bg_bf = [A.alloc("bgb%d" % i, [512], BF16) for i in range(4)]
        self.bg_i = 0
        self.bg_jobs = {}
        A.freeze()

        self.cv_i = 0
        self.cv_pending = []
        self.cv_f32 = [A.alloc("cvf%d" % i, [FF]) for i in range(3)]
        self.cv_bf = [A.alloc("cvb%d" % i, [FF], BF16) for i in range(3)]
        W = [dict() for _ in range(L)]

        def plain(l, key, grp, src, K, C, CW):
            nm = "W%s%d" % (key, l)
            if l == 0 and grp == "mx":
                W[l][key] = self.convert(src, K, C, CW, nm)
            else:
                W[l][key], jobs = self.convert_jobs(src, K, C, CW, nm)
                self.bg_jobs.setdefault((grp, l), []).extend(jobs)

        for l in range(L):
            if "mix" in ph:
                wi = I["w_in"][l]
                plain(l, "z", "mx", wi[:, 0:C0], KD, 1024, 512)
                plain(l, "xbc", "mx", wi[:, C0:C1], KD, 1536, 512)
                W[l]["sm"] = self.convert([(wi[:, C1:C2], lambda st: st[:, 0:16]),
                                           (wi[:, C4:C4 + 64], lambda st: st[:, 16:80]),
                                           (wi[:, C4 + 32:C4 + 64], lambda st: st[:, 80:112]),
                                           (wi[:, C4:C4 + 32], lambda st: st[:, 112:144])], KD, 144, 144, "Wsm%d" % l)
                plain(l, "qa", "mx", wi[:, C2:C3], KD, 512, 512)
                plain(l, "kva", "mx", wi[:, C3:C4], KD, 512, 512)
                wq3 = I["w_q_b"][l].rearrange("r (h c) -> r h c", c=192)
                v8 = lambda st, n, c: st[:, 0:n].rearrange("p (h c) -> p h c", c=c)
                W[l]["qbn"] = self.convert([(wq3[:, :, 0:128], lambda st: v8(st, 1024, 128))], 4, 1024, 1024, "Wqbn%d" % l)
                W[l]["qbr"] = self.convert([(wq3[:, :, 128:192], lambda st: v8(st, 512, 64))], 4, 512, 512, "Wqbr%d" % l)
                W[l]["qbrs"] = self.convert([(wq3[:, :, 160:192], lambda st: v8(st, 512, 64)[:, :, 0:32]),
                                             (wq3[:, :, 128:160], lambda st: v8(st, 512, 64)[:, :, 32:64])],
                                            4, 512, 512, "Wqbrs%d" % l)
                wk3 = I["w_kv_b"][l].rearrange("r (h c) -> r h c", c=256)
                W[l]["kvn"] = self.convert([(wk3[:, :, 0:128], lambda st: v8(st, 1024, 128))], 4, 1024, 1024, "Wkvn%d" % l)
                W[l]["kvv"] = self.convert([(wk3[:, :, 128:256], lambda st: v8(st, 1024, 128))], 4, 1024, 1024, "Wkvv%d" % l)
                plain(l, "out", "mx", I["w_out"][l], KD, D, 512)
            if "xattn" in ph:
                plain(l, "xq", "mx", I["w_xq"][l], KD, 512, 512)
                plain(l, "xk", "mx", I["w_xk"][l], KD, 512, 512)
                plain(l, "xv", "mx", I["w_xv"][l], KD, 512, 512)
                plain(l, "xo", "mx", I["w_xo"][l], 4, D, D)
            if "ffn" in ph:
                plain(l, "g", "ffn", I["w_gate"][l], KD, FF, 256)
                plain(l, "u", "ffn", I["w_up"][l], KD, FF, 256)
                plain(l, "d", "ffn", I["w_down"][l], KF, D, 128)
        self.cv_flush(100)
        S.barrier()
        A.reset()

        xin_t = [A.alloc("xin%d" % i, [4, D]) for i in range(2)]
        xo_t = [A.alloc("xo%d" % i, [KD, TT]) for i in range(2)]
        it = 0
        for s in range(NS):
            for t in range(NT):
                b = it % 2
                it += 1
                xi, xo = xin_t[b], xo_t[b]
                for tb in range(4):
                    self.dm(xi[:, tb, :], I["x"][s, t * TT + tb * P:t * TT + (tb + 1) * P, :], w=[(xi, tb)])
                for k in range(KD):
                    bank = k % 4
                    for tb in range(4):
                        self.tr(self.ps(bank)[:, tb * P:(tb + 1) * P], xi[:, tb, k * P:(k + 1) * P], ident_f[:, :],
                                r=[(xi, tb), ident_f], w=[PSK(bank)])
                    self.evac(xo[:, k, :], self.ps(bank), r=[PSK(bank)], w=[(xo, k)])
                self.store_xt(xT, s, t, xo)
        S.barrier()
        A.reset()

        if "mix" in ph:
            self.szt = self.dscr("szt", [NS, SQ, SSD_INNER])
            self.xbcT = self.dscr("xbcT", [NS, P, 12, SQ])
            self.dtt = self.dscr("dtt", [NS, SQ // TT, P, 4 * NHS])
            self.qnT = self.dscr("qnT", [NS, MLA_H, P, SQ], BF16)
            self.qrT = self.dscr("qrT", [NS, MLA_H, 64, SQ], BF16)
            self.knT = self.dscr("knT", [NS, MLA_H, P, SQ], BF16)
            self.krT = self.dscr("krT", [NS, MLA_H, 64, SQ], BF16)
            self.vt = self.dscr("vt", [NS, SQ, MLA_H * P], BF16)
            self.mixT = self.dscr("mixT", [NS, P, KD, SQ], BF16)
            self.phase_rope()
            S.barrier()
            A.reset()
        if "xattn" in ph:
            self.phase_mem()
            S.barrier()
            A.reset()

        HOST = {"assd": (0.15, 0.15, "pool"), "amla": (0.30, 0.30, "pool"), "ssd": (0.0, 0.0, "act"),
                "attn": (0.18, 0.18, "dve"), "oproj": (0.10, 0.10, "pool"), "xattn": (0.27, 0.27, "pool")}
        self.bg_eng = "pool"
        for l in range(L):
            nffn = len(self.bg_jobs.get(("ffn", l), ()))
            nmx = len(self.bg_jobs.get(("mx", l + 1), ()))

            def host(nm):
                import os
                f1, f2, eng = HOST[nm]
                self.bg_eng = os.environ.get("KENG", eng)
                if f1 > 0:
                    self.bg_take(("ffn", l), int(nffn * f1) + 1)
                    self.bg_take(("mx", l + 1), int(nmx * f2) + 1)
                self.bg_eng = "pool"

            if "mix" in ph:
                self.bg_need(("mx", l))
                for f in (self.phase_assd, self.phase_amla, self.phase_ssd, self.phase_attn, self.phase_oproj):
                    nm = f.__name__[6:]
                    if any(p.startswith("only_") for p in ph) and ("only_" + nm) not in ph:
                        continue
                    host(nm)
                    f(l, W[l])
                    S.barrier()
                    A.reset()
            if "xattn" in ph and "noxattn" not in ph:
                self.bg_need(("mx", l))
                self.host_fn = lambda: host("xattn")
                self.phase_xattn(l, W[l])
                S.barrier()
                A.reset()
            if "ffn" in ph:
                self.bg_need(("ffn", l))
                if "mix" not in ph and l + 1 < L:
                    self.bg_take(("mx", l + 1))
                self.phase_ffn(l, W[l])
                S.barrier()
                A.reset()

        xi_t = [A.alloc("xfi%d" % i, [KD, TT]) for i in range(2)]
        ot_t = [A.alloc("xfo%d" % i, [4, D]) for i in range(2)]
        it = 0
        for s in range(NS):
            for t in range(NT):
                b = it % 2
                it += 1
                xi, ot = xi_t[b], ot_t[b]
                self.load_xt(xT, s, t, xi)
                n = 0
                for tb in range(4):
                    for kg in range(4):
                        bank = n % 4
                        n += 1
                        for kk in range(4):
                            k = kg * 4 + kk
                            self.tr(self.ps(bank)[:, kk * P:(kk + 1) * P], xi[:, k, tb * P:(tb + 1) * P], ident_f[:, :],
                                    r=[(xi, k), ident_f], w=[PSK(bank)])
                        self.evac(ot[:, tb, kg * 512:(kg + 1) * 512], self.ps(bank), r=[PSK(bank)], w=[(ot, tb)])
                for tb in range(4):
                    self.dm(out[s, t * TT + tb * P:t * TT + (tb + 1) * P, :], ot[:, tb, :], r=[(ot, tb)], w=[("out", s, t, tb)])
        S.emit(nc, es)
        es.close()
        return nc

    def phase_ffn(self, l, W):
        cfg, S, A = self.cfg, self.S, self.arena
        PSK = self.PSK
        NS, NT = cfg.NS, cfg.S // TT
        xT = self.xT
        Wg, Wu, Wd = W["g"], W["u"], W["d"]
        xt_t = [A.alloc("fx%d" % i, [KD, TT]) for i in range(2)]
        sq_t = [A.alloc("fsq%d" % i, [TT], BF16) for i in range(2)]
        rstd = A.alloc("frstd", [TT])
        hT = A.alloc("fh", [KD, TT], BF16)
        actT = A.alloc("fact", [KF, TT], BF16)
        wg_t = [A.alloc("fwg%d" % i, [KD, 256], BF16) for i in range(2)]
        wu_t = [A.alloc("fwu%d" % i, [KD, 256], BF16) for i in range(2)]
        wd_t = [A.alloc("fwd%d" % i, [KF, 128], BF16) for i in range(2)]
        sg_t = [A.alloc("fsg%d" % i, [TT]) for i in range(2)]
        it = 0
        wi = 0
        di = 0
        for s in range(NS):
            for t in range(NT):
                xt = xt_t[it % 2]
                it += 1
                self.load_xt(xT, s, t, xt)
                self.norm_h(xt, hT, self.G["ffn"], l, rstd, sq_t)
                for cg in range(FF // 256):
                    wg, wu = wg_t[wi % 2], wu_t[wi % 2]
                    wi += 1
                    self.dm(wg[:, :, :], Wg[cg], w=[wg])
                    self.dm(wu[:, :, :], Wu[cg], w=[wu])
                    for c in range(2):
                        j = cg * 2 + c
                        bg, bu = 1 + (j % 2), 3 + (j % 2)
                        for k in range(KD):
                            self.mm(self.ps(bg), wg[:, k, c * P:(c + 1) * P], hT[:, k, :], k == 0, k == KD - 1,
                                    r=[wg, (hT, k)], w=[PSK(bg)])
                        for k in range(KD):
                            self.mm(self.ps(bu), wu[:, k, c * P:(c + 1) * P], hT[:, k, :], k == 0, k == KD - 1,
                                    r=[wu, (hT, k)], w=[PSK(bu)])
                        sg = sg_t[j % 2]
                        self.actf(sg[:, :], self.ps(bg), AF.Silu, r=[PSK(bg)], w=[sg])
                        self.tt("dve", actT[:, j, :], sg[:, :], self.ps(bu), ALU.mult, r=[sg, PSK(bu)], w=[(actT, j)])
                for dk in range(KD):
                    wd = wd_t[di % 2]
                    bo = 5 + (di % 2)
                    di += 1
                    for h in range(2):
                        self.dm(wd[:, h * 22:(h + 1) * 22, :], Wd[dk][:, h * 22:(h + 1) * 22, :], w=[(wd, h)])
                    for j in range(KF):
                        self.mm(self.ps(bo), wd[:, j, :], actT[:, j, :], j == 0, j == KF - 1,
                                r=[(wd, j // 22), (actT, j)], w=[PSK(bo)])
                    self.tt("dve", xt[:, dk, :], self.ps(bo), xt[:, dk, :], ALU.add, r=[PSK(bo), (xt, dk)], w=[(xt, dk)])
                self.store_xt(xT, s, t, xt)
    def phase_mem(self):
        cfg, A = self.cfg, self.arena
        PSK = self.PSK
        NS = cfg.NS
        mem = self.I["mem"]
        self.memT = self.dscr("memT", [NS, P, KD, MEM], BF16)
        mt_t = [A.alloc("mt%d" % i, [D]) for i in range(2)]
        junk = A.alloc("mjunk", [D], BF16)
        ss = A.alloc("mss", [2])
        mh_t = [A.alloc("mh%d" % i, [D], BF16) for i in range(2)]
        mo_t = [A.alloc("mo%d" % i, [KD, P], BF16) for i in range(2)]
        it = 0
        for s in range(NS):
            for mb in range(MEM // P):
                b = it % 2
                it += 1
                mt, mh, mo = mt_t[b], mh_t[b], mo_t[b]
                sc = ss[:, b:b + 1]
                self.dm(mt[:, :], mem[s, mb * P:(mb + 1) * P, :], w=[mt])
                self.actf(junk[:, :], mt[:, :], AF.Square, r=[mt], w=[junk, (ss, b)], accum=sc)
                self.ts("dve", sc, sc, 1.0 / D, EPS, ALU.mult, ALU.add, r=[(ss, b)], w=[(ss, b)])
                self.actf(sc, sc, AF.Sqrt, r=[(ss, b)], w=[(ss, b)])
                self.recip(sc, sc, r=[(ss, b)], w=[(ss, b)])
                self.ts("dve", mh[:, :], mt[:, :], sc, None, ALU.mult, None, r=[mt, (ss, b)], w=[mh])
                for k in range(KD):
                    bank = 2 * b + k // 8
                    self.tr(self.psb(bank)[:, (k % 8) * P:(k % 8 + 1) * P], mh[:, k * P:(k + 1) * P], self.ident_b[:, :],
                            r=[mh, self.ident_b], w=[PSK(bank)])
                for hf in range(2):
                    bank = 2 * b + hf
                    self.evac(mo[:, hf * 8:(hf + 1) * 8, :], self.psb(bank).rearrange("p (a b) -> p a b", a=8),
                              r=[PSK(bank)], w=[(mo, hf)])
                self.dm(self.memT[s, :, :, mb * P:(mb + 1) * P], mo[:, :, :], r=[(mo, 0), (mo, 1)], w=[("memT", s, mb)])

    def phase_xattn(self, l, W):
        cfg, A, G = self.cfg, self.arena, self.G
        PSK = self.PSK
        NS, NT = cfg.NS, cfg.S // TT
        xT = self.xT
        kT = A.alloc("kT", [NS, XH, MEM], BF16)
        V = A.alloc("V", [NS, 2, 512], BF16)
        sq_t = [A.alloc("xsq%d" % i, [TT], BF16) for i in range(2)]
        rk_t = [A.alloc("xrk%d" % i, [TT]) for i in range(2)]
        mark = A.off
        Wxk = A.alloc("Wxk", [KD, 512], BF16)
        Wxv = A.alloc("Wxv", [KD, 512], BF16)
        self.dm(Wxk[:, :, :], W["xk"][0], w=[Wxk])
        self.dm(Wxv[:, :, :], W["xv"][0], w=[Wxv])
        mT_t = [A.alloc("mT%d" % i, [KD, MEM], BF16) for i in range(2)]
        n = 0
        for s in range(NS):
            mT = mT_t[s % 2]
            self.dm(mT[:, :, :], self.memT[s], r=[("memT", s)], w=[mT])
            for k in range(KD):
                self.ts("dve" if k % 2 else "pool", mT[:, k, :], mT[:, k, :], G["mem"][:, l, k:k + 1], None, ALU.mult, None,
                        r=[mT, G["mem"]], w=[(mT, k)])
            for h in range(XH):
                bq, bs_ = 1 + n % 2, 3 + n % 2
                rk = rk_t[n % 2]
                sqs = [sq_t[n % 2], sq_t[(n + 1) % 2]]
                n += 1
                for k in range(KD):
                    self.mm(self.ps(bq)[:, 0:MEM], Wxk[:, k, h * P:(h + 1) * P], mT[:, k, :], k == 0, k == KD - 1,
                            r=[Wxk, (mT, k), mT], w=[PSK(bq)])
                self.rms_stats([(self.ps(bq)[:, 0:MEM], PSK(bq), P)], MEM, 1.0 / 128, bs_, rk, sqs)
                self.stt(kT[:, s, h, :], self.ps(bq)[:, 0:MEM], G["xk"][:, l, 0:1], rk[:, 0:MEM], ALU.mult, ALU.mult,
                         r=[PSK(bq), rk, G["xk"]], w=[(kT, s, h)])
            for mb in range(2):
                bv = 5 + mb
                for k in range(KD):
                    self.mm(self.ps(bv), mT[:, k, mb * P:(mb + 1) * P], Wxv[:, k, :], k == 0, k == KD - 1,
                            r=[Wxv, (mT, k), mT], w=[PSK(bv)])
                self.evac(V[:, s, mb, :], self.ps(bv), r=[PSK(bv)], w=[(V, s, mb)])
        if "xe0" in cfg.phases:
            return
        self.S.barrier()
        A.off = mark
        self.host_fn()
        Wxq = A.alloc("Wxq", [KD, 512], BF16)
        Wxo = A.alloc("Wxo", [4, D], BF16)
        self.dm(Wxq[:, :, :], W["xq"][0], w=[Wxq])
        self.dm(Wxo[:, :, :], W["xo"][0], w=[Wxo])
        xt_t = [A.alloc("xx%d" % i, [KD, TT]) for i in range(2)]
        rstd = A.alloc("xrstd", [TT])
        hT = A.alloc("xh", [KD, TT], BF16)
        qn_t = [A.alloc("xqn%d" % i, [TT], BF16) for i in range(XH)]
        pT_t = [[A.alloc("xp%d_%d" % (i, mb), [TT], BF16) for mb in range(2)] for i in range(XH)]
        rden_t = [A.alloc("xrden%d" % i, [TT]) for i in range(2)]
        oTn = A.alloc("xo", [XH, TT], BF16)
        it = 0
        for s in range(NS):
            for t in range(NT):
                xt = xt_t[it % 2]
                it += 1
                self.load_xt(xT, s, t, xt)
                self.norm_h(xt, hT, G["xattn"], l, rstd, sq_t)
                for h in range(XH):
                    bq, bs_ = 1 + h, 5 + h % 2
                    rk, qn, pT, rden = rk_t[h % 2], qn_t[h], pT_t[h], rden_t[h % 2]
                    sqs = [sq_t[h % 2], sq_t[(h + 1) % 2]]
                    for k in range(KD):
                        self.mm(self.ps(bq), Wxq[:, k, h * P:(h + 1) * P], hT[:, k, :], k == 0, k == KD - 1,
                                r=[Wxq, (hT, k)], w=[PSK(bq)])
                    self.rms_stats([(self.ps(bq), PSK(bq), P)], TT, 1.0 / 128, bs_, rk, sqs)
                    self.stt(qn[:, :], self.ps(bq), G["xq"][:, l, 0:1], rk[:, :], ALU.mult, ALU.mult,
                             r=[PSK(bq), rk, G["xq"]], w=[qn])
                    for mb in range(2):
                        bsc = bq if mb == 0 else bs_
                        self.mm(self.ps(bsc), kT[:, s, h, mb * P:(mb + 1) * P], qn[:, :], True, True,
                                r=[(kT, s, h), qn], w=[PSK(bsc)])
                        self.actf(pT[mb][:, :], self.ps(bsc), AF.Exp, r=[PSK(bsc)], w=[pT[mb]], scale=float(128 ** -0.5))
                    for mb in range(2):
                        self.mm(self.ps(7), V[:, s, mb, h * P:(h + 1) * P], pT[mb][:, :], mb == 0, mb == 1,
                                r=[(V, s, mb), pT[mb]], w=[PSK(7)])
                    for mb in range(2):
                        self.mm(self.ps(0), self.ones_b[:, :], pT[mb][:, :], mb == 0, mb == 1,
                                r=[self.ones_b, pT[mb]], w=[PSK(0)])
                    self.cp("dve", rden[:, :], self.ps(0), r=[PSK(0)], w=[rden])
                    self.recip(rden[:, :], rden[:, :], r=[rden], w=[rden])
                    self.tt("dve", oTn[:, h, :], self.ps(7), rden[:, :], ALU.mult, r=[PSK(7), rden], w=[(oTn, h)])
                for dk in range(KD):
                    bo = 1 + dk % 4
                    for h in range(XH):
                        self.mm(self.ps(bo), Wxo[:, h, dk * P:(dk + 1) * P], oTn[:, h, :], h == 0, h == XH - 1,
                                r=[Wxo, (oTn, h)], w=[PSK(bo)])
                    self.tt("dve", xt[:, dk, :], self.ps(bo), xt[:, dk, :], ALU.add, r=[PSK(bo), (xt, dk)], w=[(xt, dk)])
                self.store_xt(xT, s, t, xt)
    def phase_rope(self):
        cfg, A, S = self.cfg, self.arena, self.S
        NS, SQ = cfg.NS, cfg.S
        self.ropeC = self.dscr("ropeC", [NS, 64, SQ])
        self.ropeS = self.dscr("ropeS", [NS, 64, SQ])
        ji = A.alloc("ji", [1], I32, parts=64)
        jf = A.alloc("jf", [1], parts=64)
        invf = A.alloc("invf", [1], parts=64)
        S.pool(lambda e: e.iota(out=ji[0:32, :], pattern=[[0, 1]], base=0, channel_multiplier=1), w=[ji])
        S.pool(lambda e: e.iota(out=ji[32:64, :], pattern=[[0, 1]], base=0, channel_multiplier=1), w=[ji])
        self.cp("dve", jf[:, :], ji[:, :], r=[ji], w=[jf])
        self.actf(invf[:, :], jf[:, :], AF.Exp, r=[jf], w=[invf], scale=float(-np.log(10000.0) / 32.0))
        pos_i = A.alloc("pos_i", [SQ], I32, parts=64)
        ang = A.alloc("ang", [SQ], parts=64)
        a2 = A.alloc("a2", [SQ], parts=64)
        ni = A.alloc("ni", [SQ], I32, parts=64)
        nf = A.alloc("nf", [SQ], parts=64)
        rr = A.alloc("rr", [SQ], parts=64)
        m_ = A.alloc("m_", [SQ], parts=64)
        tabs = [A.alloc("tabS", [SQ], parts=64), A.alloc("tabC", [SQ], parts=64)]
        TWO_PI = 2.0 * PI
        for s in range(NS):
            self.dm(pos_i[:, :].rearrange("p (a s) -> p a s", a=1), self.I["positions"][s:s + 1, :].partition_broadcast(64), w=[pos_i])
            self.cp("dve", ang[:, :], pos_i[:, :], r=[pos_i], w=[ang])
            self.ts("dve", ang[:, :], ang[:, :], invf[:, 0:1], None, ALU.mult, None, r=[ang, invf], w=[ang])
            for tab, shift in ((tabs[0], 0.0), (tabs[1], PI / 2)):
                self.ts("dve", a2[:, :], ang[:, :], shift, 1.0 / TWO_PI, ALU.add, ALU.mult, r=[ang], w=[a2])
                self.cp("dve", ni[:, :], a2[:, :], r=[a2], w=[ni])
                self.cp("dve", nf[:, :], ni[:, :], r=[ni], w=[nf])
                self.stt(rr[:, :], nf[:, :], -TWO_PI, ang[:, :], ALU.mult, ALU.add, r=[nf, ang], w=[rr])
                if shift:
                    self.ts("dve", rr[:, :], rr[:, :], shift, None, ALU.add, None, r=[rr], w=[rr])
                self.ts("dve", m_[:, :], rr[:, :], PI, TWO_PI, ALU.is_gt, ALU.mult, r=[rr], w=[m_])
                self.tt("dve", rr[:, :], rr[:, :], m_[:, :], ALU.subtract, r=[rr, m_], w=[rr])
                self.ts("dve", m_[:, :], rr[:, :], -PI, TWO_PI, ALU.is_lt, ALU.mult, r=[rr], w=[m_])
                self.tt("dve", rr[:, :], rr[:, :], m_[:, :], ALU.add, r=[rr, m_], w=[rr])
                self.ts("dve", rr[:, :], rr[:, :], -3.1415925, 3.1415925, ALU.max, ALU.min, r=[rr], w=[rr])
                self.actf(tab[:, :], rr[:, :], AF.Sin, r=[rr], w=[tab])
            self.ts("dve", tabs[0][0:32, :], tabs[0][0:32, :], -1.0, None, ALU.mult, None, r=[tabs[0]], w=[tabs[0]])
            self.dm(self.ropeS[s], tabs[0][:, :], r=[tabs[0]], w=[("ropeS", s)])
            self.dm(self.ropeC[s], tabs[1][:, :], r=[tabs[1]], w=[("ropeC", s)])

    def phase_assd(self, l, W):
        cfg, A, G = self.cfg, self.arena, self.G
        PSK = self.PSK
        NS, SQ, NT = cfg.NS, cfg.S, cfg.S // TT
        xt_t = [A.alloc("ax%d" % i, [KD, TT]) for i in range(2)]
        sq_t = [A.alloc("asq%d" % i, [TT], BF16) for i in range(2)]
        rstd = A.alloc("arstd", [TT])
        hT = A.alloc("ah", [KD, TT], BF16)
        slab_t = [A.alloc("aslab%d" % i, [KD, 512], BF16) for i in range(3)]
        wsm = A.alloc("awsm", [KD, 144], BF16)
        stz = A.alloc("astz", [4, SSD_INNER])
        stx = A.alloc("astx", [12, TT])
        dts = A.alloc("adts", [4, NHS])
        self.dm(wsm[:, :, :], W["sm"][0], w=[wsm])
        it = 0
        si = 0
        n = 0
        for s in range(NS):
            for t in range(NT):
                xt = xt_t[it % 2]
                it += 1
                tsl = slice(t * TT, (t + 1) * TT)
                self.load_xt(self.xT, s, t, xt)
                self.norm_h(xt, hT, G["attn"], l, rstd, sq_t)
                import os
                skip = os.environ.get("KSKIP", "").split(",")
                for half in range(2):
                    if "z" in skip:
                        break
                    slab = slab_t[si % 3]
                    si += 1
                    self.dm(slab[:, :, :], W["z"][half], w=[slab])
                    for tb in range(4):
                        bank = 1 + n % 2
                        n += 1
                        for k in range(KD):
                            self.mm(self.ps(bank), hT[:, k, tb * P:(tb + 1) * P], slab[:, k, :], k == 0, k == KD - 1,
                                    r=[slab, (hT, k)], w=[PSK(bank)])
                        self.actf(stz[:, tb, half * 512:(half + 1) * 512], self.ps(bank), AF.Silu, r=[PSK(bank)], w=[(stz, tb)])
                for tb in range(4):
                    if "z" in skip:
                        break
                    self.dm(self.szt[s, t * TT + tb * P:t * TT + (tb + 1) * P, :], stz[:, tb, :], r=[(stz, tb)], w=[("szt", s, t, tb)])
                for tb in range(4):
                    if "dt" in skip:
                        break
                    for k in range(KD):
                        self.mm(self.ps(3)[:, tb * NHS:(tb + 1) * NHS], hT[:, k, tb * P:(tb + 1) * P], wsm[:, k, 0:NHS],
                                k == 0, k == KD - 1, r=[wsm, (hT, k)], w=[PSK(3)])
                if "dt" not in skip:
                    self.evac(dts[:, :, :], self.ps(3)[:, 0:4 * NHS].rearrange("p (a b) -> p a b", a=4), r=[PSK(3)], w=[dts])
                    if "dtdma" not in skip:
                        self.dm(self.dtt[s, t], dts.ap.rearrange("p a b -> p (a b)"), r=[dts], w=[("dtt", s, t)])
                for sl in range(3):
                    if "xbc" in skip:
                        break
                    slab = slab_t[si % 3]
                    si += 1
                    self.dm(slab[:, :, :], W["xbc"][sl], w=[slab])
                    for c in range(4):
                        ch = sl * 4 + c
                        bank = 4 + n % 2
                        n += 1
                        for k in range(KD):
                            self.mm(self.ps(bank), slab[:, k, c * P:(c + 1) * P], hT[:, k, :], k == 0, k == KD - 1,
                                    r=[slab, (hT, k)], w=[PSK(bank)])
                        self.evac(stx[:, ch, :], self.ps(bank), r=[PSK(bank)], w=[(stx, ch)])
                for h in range(2):
                    if "xbc" in skip:
                        break
                    self.dm(self.xbcT[s, :, h * 6:(h + 1) * 6, tsl], stx[:, h * 6:(h + 1) * 6, :],
                            r=[(stx, c) for c in range(h * 6, h * 6 + 6)], w=[("xbcT", s, t, h)])

    def phase_amla(self, l, W):
        cfg, A, G = self.cfg, self.arena, self.G
        PSK = self.PSK
        NS, SQ, NT = cfg.NS, cfg.S, cfg.S // TT
        xt = A.alloc("mx", [KD, TT])
        sq_t = [A.alloc("msq%d" % i, [TT], BF16) for i in range(2)]
        rstd = A.alloc("mrstd", [TT])
        rh = A.alloc("mrh", [TT])
        hT = A.alloc("mh", [KD, TT], BF16)
        slab = A.alloc("mslab", [KD, 512], BF16)
        wsm = A.alloc("mwsm", [KD, 144], BF16)
        Wqbn = A.alloc("Wqbn", [4, 1024], BF16)
        Wqbr = A.alloc("Wqbr", [4, 512], BF16)
        Wqbrs = A.alloc("Wqbrs", [4, 512], BF16)
        Wkvn = A.alloc("Wkvn", [4, 1024], BF16)
        Wkvv = A.alloc("Wkvv", [4, 1024], BF16)
        for tl, nm in ((wsm, "sm"), (Wqbn, "qbn"), (Wqbr, "qbr"), (Wqbrs, "qbrs"), (Wkvn, "kvn"), (Wkvv, "kvv")):
            self.dm(tl[:, :, :], W[nm][0], w=[tl])
        la = A.alloc("mla", [4, TT])
        lan = A.alloc("mlan", [4, TT], BF16)
        stn = A.alloc("mstn", [MLA_H, TT], BF16)
        str_ = A.alloc("mstr", [MLA_H, TT], BF16, parts=64)
        vts = A.alloc("mvts", [4, MLA_H * P], BF16)
        c2 = A.alloc("mc2", [TT], parts=64)
        s2 = A.alloc("ms2", [TT], parts=64)
        t1 = A.alloc("mt1", [TT], parts=64)
        t2 = A.alloc("mt2", [TT], parts=64)
        kr0 = A.alloc("mkr0", [TT], parts=64)
        sqk = A.alloc("msqk", [TT], BF16, parts=64)
        sqn = A.alloc("msqn", [TT], BF16)
        n = 0
        for s in range(NS):
            for t in range(NT):
                tsl = slice(t * TT, (t + 1) * TT)
                self.load_xt(self.xT, s, t, xt)
                self.norm_h(xt, hT, G["attn"], l, rstd, sq_t)
                self.dm(c2[:, :], self.ropeC[s, :, tsl], w=[c2])
                self.dm(s2[:, :], self.ropeS[s, :, tsl], w=[s2])

                def lora(wname, gname):
                    self.dm(slab[:, :, :], W[wname][0], w=[slab])
                    for c in range(4):
                        bank = 1 + c % 2
                        for k in range(KD):
                            self.mm(self.ps(bank), slab[:, k, c * P:(c + 1) * P], hT[:, k, :], k == 0, k == KD - 1,
                                    r=[slab, (hT, k)], w=[PSK(bank)])
                        self.evac(la[:, c, :], self.ps(bank), r=[PSK(bank)], w=[(la, c)])
                    self.rms_stats([(la[:, c, :], (la, c), P) for c in range(4)], TT, 1.0 / 512, 0, rstd, sq_t)
                    for c in range(4):
                        self.stt(lan[:, c, :], la[:, c, :], G[gname][:, l, c:c + 1], rstd[:, :], ALU.mult, ALU.mult,
                                 r=[(la, c), rstd, G[gname]], w=[(lan, c)])

                lora("qa", "qa")
                for h in range(MLA_H):
                    bn, br, bs = (1, 2, 3) if h % 2 == 0 else (4, 5, 6)
                    for kk in range(4):
                        self.mm(self.ps(bn), Wqbn[:, kk, h * P:(h + 1) * P], lan[:, kk, :], kk == 0, kk == 3,
                                r=[Wqbn, (lan, kk)], w=[PSK(bn)])
                    for kk in range(4):
                        self.mm(self.ps(br)[0:64, :], Wqbr[:, kk, h * 64:(h + 1) * 64], lan[:, kk, :], kk == 0, kk == 3,
                                r=[Wqbr, (lan, kk)], w=[PSK(br)])
                    for kk in range(4):
                        self.mm(self.ps(bs)[0:64, :], Wqbrs[:, kk, h * 64:(h + 1) * 64], lan[:, kk, :], kk == 0, kk == 3,
                                r=[Wqbrs, (lan, kk)], w=[PSK(bs)])
                    self.rms_stats([(self.ps(bn), PSK(bn), P), (self.ps(br)[0:64, :], PSK(br), 64)], TT, 1.0 / 192, 7, rh, sq_t)
                    self.stt(stn[:, h, :], self.ps(bn), G["qn"][:, l, 0:1], rh[:, :], ALU.mult, ALU.mult,
                             r=[PSK(bn), rh, G["qn"]], w=[(stn, h)])
                    self.stt(t1[:, :], self.ps(br)[0:64, :], G["qr"][:, l, 0:1], rh[0:64, :], ALU.mult, ALU.mult,
                             r=[PSK(br), rh, G["qr"]], w=[t1])
                    self.stt(t2[:, :], self.ps(bs)[0:64, :], G["qrs"][:, l, 0:1], rh[0:64, :], ALU.mult, ALU.mult,
                             r=[PSK(bs), rh, G["qrs"]], w=[t2])
                    self.tt("pool", t1[:, :], t1[:, :], c2[:, :], ALU.mult, r=[t1, c2], w=[t1])
                    self.tt("pool", t2[:, :], t2[:, :], s2[:, :], ALU.mult, r=[t2, s2], w=[t2])
                    self.tt("pool", str_[:, h, :], t1[:, :], t2[:, :], ALU.add, r=[t1, t2], w=[(str_, h)])
                self.dm(self.qnT[s, :, :, tsl].rearrange("h p t -> p h t"), stn[:, :, :],
                        r=[(stn, h) for h in range(MLA_H)], w=[("qnT", s, t)])
                self.dm(self.qrT[s, :, :, tsl].rearrange("h p t -> p h t"), str_[:, :, :],
                        r=[(str_, h) for h in range(MLA_H)], w=[("qrT", s, t)])
                lora("kva", "kva")
                for k in range(KD):
                    self.mm(self.ps(2)[0:64, :], wsm[:, k, 16:80], hT[:, k, :], k == 0, k == KD - 1,
                            r=[wsm, (hT, k)], w=[PSK(2)])
                for k in range(KD):
                    self.mm(self.ps(3)[0:64, :], wsm[:, k, 80:144], hT[:, k, :], k == 0, k == KD - 1,
                            r=[wsm, (hT, k)], w=[PSK(3)])
                self.actf(sqk[:, :], self.ps(2)[0:64, :], AF.Square, r=[PSK(2)], w=[sqk])
                self.ts("dve", t1[:, :], self.ps(2)[0:64, :], G["kr"][:, l, 0:1], None, ALU.mult, None, r=[PSK(2), G["kr"]], w=[t1])
                self.ts("dve", t2[:, :], self.ps(3)[0:64, :], G["krs"][:, l, 0:1], None, ALU.mult, None, r=[PSK(3), G["krs"]], w=[t2])
                self.tt("pool", t1[:, :], t1[:, :], c2[:, :], ALU.mult, r=[t1, c2], w=[t1])
                self.tt("pool", t2[:, :], t2[:, :], s2[:, :], ALU.mult, r=[t2, s2], w=[t2])
                self.tt("pool", kr0[:, :], t1[:, :], t2[:, :], ALU.add, r=[t1, t2], w=[kr0])
                for h in range(MLA_H):
                    bn = 4 + h % 2
                    for kk in range(4):
                        self.mm(self.ps(bn), Wkvn[:, kk, h * P:(h + 1) * P], lan[:, kk, :], kk == 0, kk == 3,
                                r=[Wkvn, (lan, kk)], w=[PSK(bn)])
                    self.actf(sqn[:, :], self.ps(bn), AF.Square, r=[PSK(bn)], w=[sqn])
                    self.mm(self.ps(7), self.ones_b[:, :], sqn[:, :], True, False, r=[sqn, self.ones_b], w=[PSK(7)])
                    self.mm(self.ps(7), self.ones_b[0:64, :], sqk[:, :], False, True, r=[sqk, self.ones_b], w=[PSK(7)])
                    self.ts("dve", rh[:, :], self.ps(7), 1.0 / 192, EPS, ALU.mult, ALU.add, r=[PSK(7)], w=[rh])
                    self.actf(rh[:, :], rh[:, :], AF.Sqrt, r=[rh], w=[rh])
                    self.recip(rh[:, :], rh[:, :], r=[rh], w=[rh])
                    self.stt(stn[:, h, :], self.ps(bn), G["kn"][:, l, 0:1], rh[:, :], ALU.mult, ALU.mult,
                             r=[PSK(bn), rh, G["kn"]], w=[(stn, h)])
                    self.tt("pool", str_[:, h, :], kr0[:, :], rh[0:64, :], ALU.mult, r=[kr0, rh], w=[(str_, h)])
                self.dm(self.knT[s, :, :, tsl].rearrange("h p t -> p h t"), stn[:, :, :],
                        r=[(stn, h) for h in range(MLA_H)], w=[("knT", s, t)])
                self.dm(self.krT[s, :, :, tsl].rearrange("h p t -> p h t"), str_[:, :, :],
                        r=[(str_, h) for h in range(MLA_H)], w=[("krT", s, t)])
                for tb in range(4):
                    for half in range(2):
                        bank = 1 + n % 2
                        n += 1
                        for kk in range(4):
                            self.mm(self.ps(bank), lan[:, kk, tb * P:(tb + 1) * P], Wkvv[:, kk, half * 512:(half + 1) * 512],
                                    kk == 0, kk == 3, r=[Wkvv, (lan, kk)], w=[PSK(bank)])
                        self.evac(vts[:, tb, half * 512:(half + 1) * 512], self.ps(bank), r=[PSK(bank)], w=[(vts, tb)])
                self.dm(self.vt[s, tsl, :].rearrange("(tb p) c -> p tb c", p=P), vts[:, :, :],
                        r=[(vts, tb) for tb in range(4)], w=[("vt", s, t)])

    def phase_attn(self, l, W):
        cfg, A = self.cfg, self.arena
        PSK = self.PSK
        NS, SQ, NT = cfg.NS, cfg.S, cfg.S // TT
        NJ = SQ // P
        kn_t = [A.alloc("ckn%d" % i, [SQ], BF16) for i in range(2)]
        kr_t = [A.alloc("ckr%d" % i, [SQ], BF16, parts=64) for i in range(2)]
        v_t = [A.alloc("cv%d" % i, [NJ, P], BF16) for i in range(2)]
        qn_t = [A.alloc("cqn%d" % i, [TT], BF16) for i in range(2)]
        qr_t = [A.alloc("cqr%d" % i, [TT], BF16, parts=64) for i in range(2)]
        p_t = [A.alloc("cp%d" % i, [TT], BF16) for i in range(3)]
        rden = A.alloc("crden", [TT])
        o_t = [A.alloc("co%d" % i, [TT], BF16) for i in range(2)]
        scale = float(192 ** -0.5)
        hi = 0
        qi = 0
        pj = 0
        for s in range(NS):
            for h in range(MLA_H):
                kn, kr, v = kn_t[hi % 2], kr_t[hi % 2], v_t[hi % 2]
                hi += 1
                self.dm(kn[:, :], self.knT[s, h], w=[kn])
                self.dm(kr[:, :], self.krT[s, h], w=[kr])
                self.dm(v[:, :, :], self.vt[s].rearrange("(j p) c -> p j c", p=P)[:, :, h * P:(h + 1) * P], w=[v])
                for Q in range(NT):
                    qn, qr, ot = qn_t[qi % 2], qr_t[qi % 2], o_t[qi % 2]
                    bo, bd = (4, 5) if qi % 2 == 0 else (6, 7)
                    qi += 1
                    qsl = slice(Q * TT, (Q + 1) * TT)
                    self.dm(qn[:, :], self.qnT[s, h, :, qsl], w=[qn])
                    self.dm(qr[:, :], self.qrT[s, h, :, qsl], w=[qr])
                    nj = 4 * Q + 4
                    for j in range(nj):
                        r_ = j - 4 * Q
                        q0 = P * max(r_, 0)
                        bs_ = pj % 3
                        p = p_t[pj % 3]
                        pj += 1
                        self.mm(self.ps(bs_)[:, q0:TT], kn[:, j * P:(j + 1) * P], qn[:, q0:TT], True, False,
                                r=[kn, qn], w=[PSK(bs_)])
                        self.mm(self.ps(bs_)[:, q0:TT], kr[:, j * P:(j + 1) * P], qr[:, q0:TT], False, True,
                                r=[kr, qr], w=[PSK(bs_)])
                        self.actf(p[:, q0:TT], self.ps(bs_)[:, q0:TT], AF.Exp, r=[PSK(bs_)], w=[p], scale=scale)
                        if r_ >= 0:
                            self.tt("pool", p[:, q0:q0 + P], p[:, q0:q0 + P], self.mge_b[:, :], ALU.mult, r=[p, self.mge_b], w=[p])
                        self.mm(self.ps(bo)[:, q0:TT], v[:, j, :], p[:, q0:TT], j == 0, j == nj - 1, r=[v, p], w=[PSK(bo)])
                        self.mm(self.ps(bd)[:, q0:TT], self.ones_b[:, :], p[:, q0:TT], j == 0, j == nj - 1,
                                r=[self.ones_b, p], w=[PSK(bd)])
                    self.cp("dve", rden[:, :], self.ps(bd), r=[PSK(bd)], w=[rden])
                    self.recip(rden[:, :], rden[:, :], r=[rden], w=[rden])
                    self.tt("dve", ot[:, :], self.ps(bo), rden[:, :], ALU.mult, r=[PSK(bo), rden], w=[ot])
                    self.dm(self.mixT[s, :, 8 + h, qsl], ot[:, :], r=[ot], w=[("mixT", s, h, Q)])

    def phase_oproj(self, l, W):
        cfg, A = self.cfg, self.arena
        PSK = self.PSK
        NS, NT = cfg.NS, cfg.S // TT
        xt_t = [A.alloc("ox%d" % i, [KD, TT]) for i in range(2)]
        mx_t = [A.alloc("om%d" % i, [KD, TT], BF16) for i in range(2)]
        slab_t = [A.alloc("oslab%d" % i, [KD, 512], BF16) for i in range(2)]
        it = 0
        si = 0
        for s in range(NS):
            for t in range(NT):
                xt, mx = xt_t[it % 2], mx_t[it % 2]
                it += 1
                tsl = slice(t * TT, (t + 1) * TT)
                self.load_xt(self.xT, s, t, xt)
                for h in range(2):
                    self.dm(mx[:, h * 8:(h + 1) * 8, :], self.mixT[s, :, h * 8:(h + 1) * 8, tsl], w=[(mx, h)])
                for sl in range(4):
                    slab = slab_t[si % 2]
                    si += 1
                    self.dm(slab[:, :, :], W["out"][sl], w=[slab])
                    for c in range(4):
                        dk = sl * 4 + c
                        bank = 1 + dk % 2
                        for k in range(KD):
                            self.mm(self.ps(bank), slab[:, k, c * P:(c + 1) * P], mx[:, k, :], k == 0, k == KD - 1,
                                    r=[slab, (mx, k // 8)], w=[PSK(bank)])
                        self.tt("dve", xt[:, dk, :], self.ps(bank), xt[:, dk, :], ALU.add, r=[PSK(bank), (xt, dk)], w=[(xt, dk)])
                self.store_xt(self.xT, s, t, xt)
    def phase_ssd(self, l, W):
        cfg, A, G, S = self.cfg, self.arena, self.G, self.S
        PSK = self.PSK
        NS, SQ, NT = cfg.NS, cfg.S, cfg.S // TT
        ident_b, mge_f, mgt_f, ones_f = self.ident_b, self.mge_f, self.mgt_f, self.ones_f
        gn = A.alloc("sgn", [SSD_INNER])
        self.dm(gn[:, :].rearrange("p (a c) -> p a c", a=1), self.I["ssd_norm_g"][l:l + 1, :].partition_broadcast(P), w=[gn])
        xb = A.alloc("sxb", [12, TT + 3])
        acc_t = [A.alloc("sacc%d" % i, [TT]) for i in range(2)]
        cv = A.alloc("scv", [12, TT], BF16)
        xs_tok = A.alloc("sxs", [4, SSD_INNER], BF16)
        B_tok = A.alloc("sBt", [4, 256], BF16)
        sz = A.alloc("ssz", [4, SSD_INNER])
        sm = {}
        for nm in ("dtr", "dtv", "av", "acs", "tot", "E", "Wd", "dec", "dtw"):
            sm[nm] = A.alloc("s" + nm, [4, NHS])
        flat = lambda tl: tl.ap.rearrange("p a b -> p (a b)")
        xdt = A.alloc("sxdt", [4, SSD_INNER], BF16)
        xdtw = A.alloc("sxdtw", [4, SSD_INNER], BF16)
        st_f = A.alloc("sstf", [SSD_INNER])
        st_b = A.alloc("sstb", [SSD_INNER], BF16)
        cbm_t = [A.alloc("scbm%d" % i, [P], BF16) for i in range(2)]
        ra_t = [A.alloc("sra%d" % i, [8, P]) for i in range(2)]
        ed_t = [A.alloc("sed%d" % i, [TT]) for i in range(2)]
        MT_t = [A.alloc("sMT%d" % i, [4, P], BF16) for i in range(4)]
        y_t = [A.alloc("sy%d" % i, [TT]) for i in range(2)]
        xd_t = [A.alloc("sxd%d" % i, [TT]) for i in range(2)]
        junk = A.alloc("sjunk", [TT], BF16)
        ssq = A.alloc("sssq", [8])
        yn_t = [A.alloc("syn%d" % i, [TT], BF16) for i in range(2)]
        mixs = A.alloc("smixs", [8, TT], BF16)
        v3 = lambda ap: ap.rearrange("p (e d) -> p e d", d=HD)
        bc = lambda ap, n: ap[:, :, None].broadcast_to([P, ap.shape[1], n])
        ci = 0
        mi = 0
        for s in range(NS):
            S.pool(lambda e: e.memset(st_f[:, :], 0.0), w=[(st_f, 0), (st_f, 1)])
            S.pool(lambda e: e.memset(st_b[:, :], 0.0), w=[(st_b, 0), (st_b, 1)])
            for t in range(NT):
                tsl = slice(t * TT, (t + 1) * TT)
                if t == 0:
                    S.pool(lambda e: e.memset(xb[:, :, 0:3], 0.0), w=[xb])
                    for h in range(2):
                        self.dm(xb[:, h * 6:(h + 1) * 6, 3:TT + 3], self.xbcT[s, :, h * 6:(h + 1) * 6, 0:TT], w=[xb])
                else:
                    for h in range(2):
                        self.dm(xb[:, h * 6:(h + 1) * 6, :], self.xbcT[s, :, h * 6:(h + 1) * 6, t * TT - 3:(t + 1) * TT], w=[xb])
                self.dm(sz[:, :, :], self.szt[s, tsl, :].rearrange("(tb p) c -> p tb c", p=P), w=[sz])
                self.dm(flat(sm["dtr"]), self.dtt[s, t], w=[sm["dtr"]])
                for c in range(12):
                    acc = acc_t[c % 2]
                    self.ts("dve", acc[:, :], xb[:, c, 0:TT], G["cw"][:, l, 0, c:c + 1], G["cb"][:, l, c:c + 1], ALU.mult, ALU.add,
                            r=[xb, G["cw"], G["cb"]], w=[acc])
                    for j in range(1, 4):
                        self.stt(acc[:, :], xb[:, c, j:j + TT], G["cw"][:, l, j, c:c + 1], acc[:, :], ALU.mult, ALU.add,
                                 r=[xb, acc, G["cw"]], w=[acc])
                    self.actf(cv[:, c, :], acc[:, :], AF.Silu, r=[acc], w=[(cv, c)])
                for tb in range(4):
                    bank = 4 + tb % 2
                    for c in range(8):
                        self.tr(self.psb(bank)[:, c * P:(c + 1) * P], cv[:, c, tb * P:(tb + 1) * P], ident_b[:, :],
                                r=[(cv, c), ident_b], w=[PSK(bank)])
                    self.evac(xs_tok[:, tb, :], self.psb(bank), r=[PSK(bank)], w=[(xs_tok, tb)])
                    for g in range(2):
                        self.tr(self.psb(6)[:, (tb * 2 + g) * P:(tb * 2 + g + 1) * P], cv[:, 8 + g, tb * P:(tb + 1) * P], ident_b[:, :],
                                r=[(cv, 8 + g), ident_b], w=[PSK(6)])
                self.evac(B_tok.ap.rearrange("p a b -> p (a b)"), self.psb(6), r=[PSK(6)], w=[B_tok])
                dtv, av = sm["dtv"], sm["av"]
                self.tt("dve", dtv[:, :, :], sm["dtr"][:, :, :], G["dtb"][:, l:l + 1, :].broadcast_to([P, 4, NHS]), ALU.add,
                        r=[sm["dtr"], G["dtb"]], w=[dtv])
                self.actf(dtv[:, :, :], dtv[:, :, :], AF.Exp, r=[dtv], w=[dtv])
                self.ts("dve", dtv[:, :, :], dtv[:, :, :], 1.0, None, ALU.add, None, r=[dtv], w=[dtv])
                self.actf(dtv[:, :, :], dtv[:, :, :], AF.Ln, r=[dtv], w=[dtv])
                self.tt("dve", av[:, :, :], dtv[:, :, :], G["Aneg"][:, l:l + 1, :].broadcast_to([P, 4, NHS]), ALU.mult,
                        r=[dtv, G["Aneg"]], w=[av])
                self.mm(self.ps(3)[:, 0:64], mge_f[:, :], flat(av), True, True, r=[av, mge_f], w=[PSK(3)])
                self.mm(self.ps(3)[:, 64:128], ones_f[:, :], flat(av), True, True, r=[av, ones_f], w=[PSK(3)])
                self.cp("dve", flat(sm["acs"]), self.ps(3)[:, 0:64], r=[PSK(3)], w=[sm["acs"]])
                self.cp("dve", flat(sm["tot"]), self.ps(3)[:, 64:128], r=[PSK(3)], w=[sm["tot"]])
                self.actf(sm["E"][:, :, :], sm["acs"][:, :, :], AF.Exp, r=[sm["acs"]], w=[sm["E"]])
                self.tt("dve", sm["Wd"][:, :, :], sm["tot"][:, :, :], sm["acs"][:, :, :], ALU.subtract, r=[sm["tot"], sm["acs"]], w=[sm["Wd"]])
                self.actf(sm["Wd"][:, :, :], sm["Wd"][:, :, :], AF.Exp, r=[sm["Wd"]], w=[sm["Wd"]])
                self.actf(sm["dec"][:, :, :], sm["tot"][:, :, :], AF.Exp, r=[sm["tot"]], w=[sm["dec"]])
                self.tt("dve", sm["dtw"][:, :, :], dtv[:, :, :], sm["Wd"][:, :, :], ALU.mult, r=[dtv, sm["Wd"]], w=[sm["dtw"]])
                for tb in range(4):
                    self.tt("pool", v3(xdt[:, tb, :]), v3(xs_tok[:, tb, :]), bc(dtv[:, tb, :], HD), ALU.mult,
                            r=[(xs_tok, tb), dtv], w=[(xdt, tb)])
                    self.tt("pool", v3(xdtw[:, tb, :]), v3(xs_tok[:, tb, :]), bc(sm["dtw"][:, tb, :], HD), ALU.mult,
                            r=[(xs_tok, tb), sm["dtw"]], w=[(xdtw, tb)])
                for tb in range(4):
                    csl = slice(tb * P, (tb + 1) * P)
                    for g in range(2):
                        gs = slice(g * 512, (g + 1) * 512)
                        es_ = slice(g * 8, (g + 1) * 8)
                        cbm, ra, yt, xd, yn = cbm_t[ci % 2], ra_t[ci % 2], y_t[ci % 2], xd_t[ci % 2], yn_t[ci % 2]
                        sc = ssq[:, ci % 8:ci % 8 + 1]
                        sck = (ssq, ci % 8)
                        ci += 1
                        Bt, Ct = cv[:, 8 + g, csl], cv[:, 10 + g, csl]
                        self.mm(self.ps(0)[:, 0:P], Bt, Ct, True, True, r=[(cv, 8 + g), (cv, 10 + g)], w=[PSK(0)])
                        self.tt("dve", cbm[:, :], self.ps(0)[:, 0:P], mge_f[:, :], ALU.mult, r=[PSK(0), mge_f], w=[cbm])
                        self.tt("pool", ra[:, :, :], mge_f[:, None, :].broadcast_to([P, 8, P]), bc(av[:, tb, es_], P), ALU.mult,
                                r=[mge_f, av], w=[ra])
                        MTs = []
                        for hh in range(2):
                            ed = ed_t[hh]
                            MT = MT_t[mi % 4]
                            mi += 1
                            MTs.append(MT)
                            self.mm(self.ps(1 + hh), mgt_f[:, :], ra[:, hh * 4:(hh + 1) * 4, :].rearrange("p a b -> p (a b)"), True, True,
                                    r=[ra, mgt_f], w=[PSK(1 + hh)])
                            self.actf(ed[:, :], self.ps(1 + hh), AF.Exp, r=[PSK(1 + hh)], w=[ed])
                            self.tt("dve", MT[:, :, :], ed[:, :].rearrange("p (a b) -> p a b", a=4),
                                    cbm[:, None, :].broadcast_to([P, 4, P]), ALU.mult, r=[ed, cbm], w=[MT])
                        for e in range(8):
                            col = (g * 8 + e) * HD
                            self.mm(self.ps(4)[:, e * HD:(e + 1) * HD], MTs[e // 4][:, e % 4, :], xdt[:, tb, col:col + HD], True, True,
                                    r=[MTs[e // 4], (xdt, tb)], w=[PSK(4)])
                        self.mm(self.ps(5), Ct, st_b[:, gs], True, True, r=[(cv, 10 + g), (st_b, g)], w=[PSK(5)])
                        self.mm(self.ps(6), B_tok[:, tb, g * P:(g + 1) * P], xdtw[:, tb, gs], True, True,
                                r=[B_tok, (xdtw, tb)], w=[PSK(6)])
                        self.tt("dve", v3(yt[:, :]), v3(self.ps(5)), bc(sm["E"][:, tb, es_], HD), ALU.mult,
                                r=[PSK(5), sm["E"]], w=[yt])
                        self.tt("dve", yt[:, :], self.ps(4), yt[:, :], ALU.add, r=[PSK(4), yt], w=[yt])
                        self.tt("pool", v3(xd[:, :]), v3(xs_tok[:, tb, gs]), bc(G["dsk"][:, l, es_], HD), ALU.mult,
                                r=[(xs_tok, tb), G["dsk"]], w=[xd])
                        self.tt("dve", yt[:, :], yt[:, :], xd[:, :], ALU.add, r=[yt, xd], w=[yt])
                        self.tt("pool", yt[:, :], yt[:, :], sz[:, tb, gs], ALU.mult, r=[yt, sz], w=[yt])
                        self.actf(junk[:, :], yt[:, :], AF.Square, r=[yt], w=[junk, sck], accum=sc)
                        self.ts("dve", sc, sc, 1.0 / 512, EPS, ALU.mult, ALU.add, r=[sck], w=[sck])
                        self.actf(sc, sc, AF.Sqrt, r=[sck], w=[sck])
                        self.recip(sc, sc, r=[sck], w=[sck])
                        self.stt(yn[:, :], yt[:, :], sc, gn[:, gs], ALU.mult, ALU.mult, r=[yt, sck, gn], w=[yn])
                        for c4 in range(4):
                            self.tr(self.psb(7)[:, c4 * P:(c4 + 1) * P], yn[:, c4 * P:(c4 + 1) * P], ident_b[:, :],
                                    r=[yn, ident_b], w=[PSK(7)])
                        self.evac(mixs[:, g * 4:(g + 1) * 4, csl], self.psb(7)[:, 0:512].rearrange("p (a b) -> p a b", a=4),
                                  r=[PSK(7)], w=[(mixs, g, tb)])
                        self.tt("pool", v3(st_f[:, gs]), v3(st_f[:, gs]), bc(sm["dec"][:, tb, es_], HD), ALU.mult,
                                r=[(st_f, g), sm["dec"]], w=[(st_f, g)])
                        self.tt("dve", st_f[:, gs], self.ps(6), st_f[:, gs], ALU.add, r=[PSK(6), (st_f, g)], w=[(st_f, g)])
                        self.cp("act", st_b[:, gs], st_f[:, gs], r=[(st_f, g)], w=[(st_b, g)])
                self.dm(self.mixT[s, :, 0:8, tsl], mixs[:, :, :],
                        r=[(mixs, g, tb) for g in range(2) for tb in range(4)], w=[("mixT", s, "ssd", t)])


def run(inputs, cfg, n_cores):
    b = Builder(cfg)
    nc = b.build()
    names = [n for n in b.dram if n in inputs]
    in_maps = []
    for c in range(n_cores):
        m = {}
        for n in names:
            a = inputs[n]
            if n in ("x", "mem", "positions"):
                a = np.ascontiguousarray(a[c * cfg.NS:(c + 1) * cfg.NS])
            m[n] = a
        in_maps.append(m)
    res = run_bass_kernel_spmd(nc, in_maps, core_ids=list(range(n_cores)))
    return np.concatenate([r["out"] for r in res.results], axis=0)


def kernel(**inputs):
    inputs = {k: np.asarray(v) for k, v in inputs.items()}
    cfg = Cfg(NS=2, S=2048, L=4)
    return run(inputs, cfg, 8)
```

```python
import numpy as np
from contextlib import ExitStack
import concourse.bass as bass
import concourse.mybir as mybir
from concourse.bass_utils import run_bass_kernel_spmd

F32 = mybir.dt.float32
BF16 = mybir.dt.bfloat16
I32 = mybir.dt.int32
AF = mybir.ActivationFunctionType
ALU = mybir.AluOpType
P = 128
EPS = 1e-6


def _dsize(dt):
    return 2 if dt == BF16 else 4


class Op:
    __slots__ = ("eng", "fn", "deps", "dma", "seq", "sem", "semval", "signal", "gid", "seg", "dur", "nbytes",
                 "bg", "prev", "rt", "fin", "nd", "users")


class Tile:
    _n = 0

    def __init__(self, ap, name):
        self.ap = ap
        Tile._n += 1
        self.key = (name, Tile._n)

    def __getitem__(self, idx):
        return self.ap[idx]


def _key(r):
    if isinstance(r, Tile):
        return r.key
    if isinstance(r, tuple) and isinstance(r[0], Tile):
        return (r[0].key,) + tuple(r[1:])
    return r


class Sched:
    ENGS = ("pe", "act", "dve", "pool", "sp")
    QSEMS = (("sp", 16), ("act", 4), ("pool", 4))
    HOP = 0.4
    DMA_BW = 120e3
    DMA_LAT = 1.8

    def __init__(self):
        self.all = []
        self.last_w = {}
        self.readers = {}
        self.seg = 0
        import os
        self.reorder = os.environ.get("KREORDER", "1") == "1"

    def add(self, eng, fn, r=(), w=(), dma=False, dur=0.2, nbytes=0, bg=False):
        op = Op()
        op.eng, op.fn, op.dma, op.dur, op.nbytes, op.bg = eng, fn, dma, dur, nbytes, bg
        op.signal = False
        op.seq = op.sem = op.semval = op.prev = None
        deps = set()
        for x in r:
            k = _key(x)
            for lw in self.last_w.get(k, ()):
                deps.add(lw)
        cowrite = set()
        for x in w:
            k = _key(x)
            lws = self.last_w.get(k, ())
            rds = self.readers.get(k, ())
            if dma and lws and not rds and all(o.dma for o in lws):
                cowrite.add(k)
                for lw in lws:
                    deps |= lw.deps
                continue
            for lw in lws:
                deps.add(lw)
            for rd in rds:
                deps.add(rd)
        deps.discard(op)
        op.deps = deps
        for x in r:
            self.readers.setdefault(_key(x), []).append(op)
        for x in w:
            k = _key(x)
            if k in cowrite:
                self.last_w[k] = list(self.last_w[k]) + [op]
            else:
                self.last_w[k] = [op]
            self.readers[k] = []
        op.gid = len(self.all)
        op.seg = self.seg
        self.all.append(op)
        return op

    def barrier(self):
        self.seg += 1
        self.last_w = {}
        self.readers = {}

    def pe(self, fn, r=(), w=(), **kw):
        return self.add("pe", fn, r, w, **kw)

    def act(self, fn, r=(), w=(), **kw):
        return self.add("act", fn, r, w, **kw)

    def dve(self, fn, r=(), w=(), **kw):
        return self.add("dve", fn, r, w, **kw)

    def pool(self, fn, r=(), w=(), **kw):
        return self.add("pool", fn, r, w, **kw)

    def dma(self, fn, r=(), w=(), q="sp", **kw):
        return self.add(q, fn, r, w, dma=True, **kw)

    def _schedule(self, ops):
        import heapq
        order = {e: [] for e in self.ENGS}
        import os
        lo, hi = int(os.environ.get("KR_LO", "0")), int(os.environ.get("KR_HI", "100000"))
        if not self.reorder or not (lo <= ops[0].seg <= hi):
            for op in ops:
                order[op.eng].append(op)
            return order
        for op in ops:
            op.users = []
            op.nd = 0
            op.rt = 0.0
        seg = ops[0].seg
        for op in ops:
            for d in op.deps:
                if d.seg == seg:
                    d.users.append(op)
                    op.nd += 1
        NC = 4
        heap = {e: [] for e in self.ENGS}
        front = {e: [] for e in self.ENGS}

        def push(op):
            ent = (op.gid + (10 ** 9 if op.bg else 0), op.gid, op)
            f = front[op.eng]
            if len(f) < NC:
                f.append(ent)
            else:
                m = max(f)
                if ent < m:
                    f.remove(m)
                    f.append(ent)
                    heapq.heappush(heap[op.eng], m)
                else:
                    heapq.heappush(heap[op.eng], ent)

        for op in ops:
            if op.nd == 0:
                push(op)
        t_eng = {e: 0.0 for e in self.ENGS}
        pipe = 0.0
        remaining = len(ops)
        while remaining:
            best = None
            for e in self.ENGS:
                for ent in front[e]:
                    st = max(t_eng[e], ent[2].rt)
                    key = (st, ent[0])
                    if best is None or key < best[0]:
                        best = (key, e, ent)
            (st, _), e, ent = best
            op = ent[2]
            front[e].remove(ent)
            if heap[e]:
                front[e].append(heapq.heappop(heap[e]))
            if op.dma:
                t_eng[e] = st + 0.06
                p0 = max(st, pipe)
                xfer = op.nbytes / self.DMA_BW
                pipe = p0 + xfer
                op.fin = p0 + xfer + self.DMA_LAT
            else:
                op.fin = st + op.dur
                t_eng[e] = op.fin
            order[e].append(op)
            remaining -= 1
            for u in op.users:
                u.nd -= 1
                hop = 0.0 if (u.eng == "pe" and op.eng == "pe" and not u.dma) else self.HOP
                if op.fin + hop > u.rt:
                    u.rt = op.fin + hop
                if u.nd == 0:
                    push(u)
        return order

    def emit(self, nc, es):
        ENGS = self.ENGS
        nseg = self.seg + 1
        segs = [[] for _ in range(nseg)]
        for op in self.all:
            segs[op.seg].append(op)
        final = {e: [] for e in ENGS}
        segpos = {e: [] for e in ENGS}
        for si, ops in enumerate(segs):
            if not ops:
                continue
            order = self._schedule(ops)
            for e in ENGS:
                if order[e]:
                    segpos[e].append((len(final[e]), si))
                    final[e].extend(order[e])
        qbase = {}
        nd = 0
        for q, n in self.QSEMS:
            qbase[q] = (nd, n)
            nd += n
        dma_total = [0] * nd
        for q, n in self.QSEMS:
            base = qbase[q][0]
            dl = [op for op in final[q] if op.dma]
            for i, op in enumerate(dl):
                op.sem = base + (i % n)
                op.semval = 16 * (i // n + 1)
                op.prev = dl[i - n] if i >= n else None
                dma_total[op.sem] = op.semval
        for op in self.all:
            for d in op.deps:
                if not d.dma and not (d.eng == "pe" and op.eng == "pe" and not op.dma):
                    d.signal = True
        for e in ENGS:
            for (pos, si), nxt in zip(segpos[e], segpos[e][1:] + [(len(final[e]), None)]):
                for op in reversed(final[e][pos:nxt[0]]):
                    if not op.dma:
                        op.signal = True
                        break
        for e in ENGS:
            n = 0
            for op in final[e]:
                if (not op.dma) and op.signal:
                    n += 1
                    op.seq = n
        bar_e = [dict() for _ in range(nseg + 1)]
        bar_d = [dict() for _ in range(nseg + 1)]
        cur_e, cur_d = {}, {}
        byseg_e = [dict() for _ in range(nseg)]
        byseg_d = [dict() for _ in range(nseg)]
        for e in ENGS:
            for op in final[e]:
                if op.dma:
                    if byseg_d[op.seg].get(op.sem, 0) < op.semval:
                        byseg_d[op.seg][op.sem] = op.semval
                elif op.signal:
                    if byseg_e[op.seg].get(e, 0) < op.seq:
                        byseg_e[op.seg][e] = op.seq
        for si in range(nseg):
            bar_e[si] = dict(cur_e)
            bar_d[si] = dict(cur_d)
            for k, v in byseg_e[si].items():
                cur_e[k] = max(cur_e.get(k, 0), v)
            for k, v in byseg_d[si].items():
                cur_d[k] = max(cur_d.get(k, 0), v)
        esem = {e: es.enter_context(nc.semaphore("sem_" + e)) for e in ENGS}
        dsem = [es.enter_context(nc.semaphore("dsem%d" % i)) for i in range(nd)]
        block = es.enter_context(nc.Block())
        sections = {"pe": block.tensor, "act": block.scalar, "dve": block.vector,
                    "pool": block.gpsimd, "sp": block.sync}

        def make(ename):
            def body(eng):
                seen_e = {}
                seen_d = {}

                def wait_e(en, v):
                    if seen_e.get(en, 0) < v:
                        eng.wait_ge(esem[en], v)
                        seen_e[en] = v

                def wait_d(s_, v):
                    if seen_d.get(s_, 0) < v:
                        eng.wait_ge(dsem[s_], v)
                        seen_d[s_] = v

                starts = dict(segpos[ename])
                for i, op in enumerate(final[ename]):
                    if i in starts and starts[i] > 0:
                        si = starts[i]
                        for en, v in bar_e[si].items():
                            wait_e(en, v)
                        for s_, v in bar_d[si].items():
                            wait_d(s_, v)
                    need_e = {}
                    need_d = {}
                    for d in op.deps:
                        if d.seg != op.seg:
                            continue
                        if d.dma:
                            if need_d.get(d.sem, 0) < d.semval:
                                need_d[d.sem] = d.semval
                        else:
                            if d.eng == "pe" and ename == "pe" and not op.dma:
                                continue
                            if need_e.get(d.eng, 0) < d.seq:
                                need_e[d.eng] = d.seq
                    if op.dma and op.prev is not None:
                        if need_d.get(op.prev.sem, 0) < op.prev.semval:
                            need_d[op.prev.sem] = op.prev.semval
                    for en, v in need_e.items():
                        wait_e(en, v)
                    for s_, v in need_d.items():
                        wait_d(s_, v)
                    inst = op.fn(eng)
                    if op.dma:
                        inst.then_inc(dsem[op.sem], 16)
                    elif op.signal:
                        inst.then_inc(esem[ename], 1)
                if ename == "sp":
                    for s_ in range(nd):
                        if dma_total[s_]:
                            eng.wait_ge(dsem[s_], dma_total[s_])
            return body

        for e in ENGS:
            sections[e](make(e))


class Arena:
    def __init__(self, ap, ncol):
        self.ap = ap
        self.ncol = ncol
        self.off = 0
        self.floor = 0

    def reset(self):
        self.off = self.floor

    def freeze(self):
        self.floor = self.off

    def alloc(self, name, free_shape, dt=F32, parts=P):
        n = int(np.prod(free_shape))
        words = (n * _dsize(dt) + 3) // 4
        assert self.off + words <= self.ncol, ("SBUF arena overflow", name, self.off, words, self.ncol)
        ap = self.ap[0:parts, self.off:self.off + words]
        self.off += words
        if dt != F32:
            ap = ap.bitcast(dt)[:, 0:n]
        if len(free_shape) == 2:
            ap = ap.rearrange("p (a b) -> p a b", a=free_shape[0])
        elif len(free_shape) == 3:
            ap = ap.rearrange("p (a b c) -> p a b c", a=free_shape[0], b=free_shape[1])
        return Tile(ap, name)


D = 2048
KD = 16
SSD_INNER = 1024
NHS = 16
HD = 64
NST = 128
CONV_DIM = 1536
MLA_H = 8
Q_LORA = 512
KV_LORA = 512
XH = 4
MEM = 256
FF = 5632
KF = 44
IN_COLS = 3664
C0, C1, C2, C3, C4 = 1024, 2560, 2576, 3088, 3600
TT = 512


class Cfg:
    def __init__(self, NS=2, S=2048, L=4, phases=("mix", "xattn", "ffn")):
        self.NS, self.S, self.L = NS, S, L
        self.phases = tuple(phases)


PI = float(np.pi)


class Builder:
    def __init__(self, cfg):
        self.cfg = cfg
        self.nc = bass.Bass("TRN2", target_bir_lowering=False)
        self.S = Sched()
        self.dram = {}
        self.ev_i = 0

    def din(self, name, shape, dt=F32):
        t = self.nc.dram_tensor(name, list(shape), dt, kind="ExternalInput").ap()
        self.dram[name] = t
        return t

    def dscr(self, name, shape, dt=F32):
        return self.nc.dram_tensor(name, list(shape), dt, kind="Internal").ap()

    def ps(self, b):
        return self.psum[:, b, :]

    def psb(self, b):
        return self.psum[:, b, :].bitcast(BF16)

    PSK = staticmethod(lambda b: ("psum", b))

    @staticmethod
    def _fs(ap):
        n = 1
        for d in ap.shape[1:]:
            n *= d
        return n

    def mm(self, out, lhsT, rhs, st, sp, r, w):
        dur = self._fs(out) * (4.0 if lhsT.dtype == F32 else 1.0) / 2400.0 + 0.02
        self.S.pe(lambda e: e.matmul(out, lhsT=lhsT, rhs=rhs, start=st, stop=sp), r=r, w=w, dur=dur)

    def tr(self, out, in_, ident, r, w):
        self.S.pe(lambda e: e.transpose(out=out, in_=in_, identity=ident), r=r, w=w,
                  dur=0.12 if in_.dtype == F32 else 0.07)

    def actf(self, out, in_, func, r, w, scale=None, bias=None, accum=None):
        kw = {}
        if scale is not None:
            kw["scale"] = scale
        if bias is not None:
            kw["bias"] = bias
        if accum is not None:
            kw["accum_out"] = accum
        self.S.act(lambda e: e.activation(out=out, in_=in_, func=func, **kw), r=r, w=w, dur=self._fs(out) / 960.0 + 0.22)

    def _edur(self, eng, out):
        return self._fs(out) / (480.0 if eng == "pool" else 960.0) + 0.2

    def tt(self, eng, out, in0, in1, op, r, w):
        self.S.add(eng, lambda e: e.tensor_tensor(out=out, in0=in0, in1=in1, op=op), r=r, w=w, dur=self._edur(eng, out))

    def ts(self, eng, out, in0, s1, s2, op0, op1, r, w):
        if op1 is None:
            self.S.add(eng, lambda e: e.tensor_scalar(out=out, in0=in0, scalar1=s1, scalar2=None, op0=op0), r=r, w=w,
                       dur=self._edur(eng, out))
        else:
            self.S.add(eng, lambda e: e.tensor_scalar(out=out, in0=in0, scalar1=s1, scalar2=s2, op0=op0, op1=op1), r=r, w=w,
                       dur=self._edur(eng, out))

    def stt(self, out, in0, scalar, in1, op0, op1, r, w):
        self.S.dve(lambda e: e.scalar_tensor_tensor(out=out, in0=in0, scalar=scalar, in1=in1, op0=op0, op1=op1), r=r, w=w,
                   dur=self._fs(out) / 960.0 + 0.2)

    def cp(self, eng, out, in_, r, w, bg=False):
        if eng == "act":
            self.S.act(lambda e: e.copy(out=out, in_=in_), r=r, w=w, dur=self._fs(out) / 960.0 + 0.22, bg=bg)
        else:
            self.S.add(eng, lambda e: e.tensor_copy(out=out, in_=in_), r=r, w=w, dur=self._edur(eng, out), bg=bg)

    def evac(self, out, in_, r, w):
        self.ev_i += 1
        self.cp("act" if self.ev_i % 2 else "dve", out, in_, r, w)

    def dm(self, out, in_, r=(), w=(), q="sp", bg=False):
        nb = 1
        for d in out.shape:
            nb *= d
        nb *= _dsize(out.dtype)
        self.S.dma(lambda e: e.dma_start(out=out, in_=in_), r=r, w=w, q=q, nbytes=nb, bg=bg)

    def recip(self, out, in_, r, w):
        self.S.dve(lambda e: e.reciprocal(out=out, in_=in_), r=r, w=w, dur=self._fs(out) / 960.0 + 0.2)

    def convert(self, src, K, C, CW, name):
        S = self.S
        ncb = C // CW
        dst = self.dscr(name, [ncb, P, K, CW], BF16)
        pieces = src if isinstance(src, list) else [(src, lambda st: st[:, 0:C])]
        for k in range(K):
            b = self.cv_i % 3
            self.cv_i += 1
            st, bt = self.cv_f32[b], self.cv_bf[b]
            for (ap, dfn) in pieces:
                self.dm(dfn(st), ap[k * P:(k + 1) * P], r=[], w=[st])
            eng = ("act", "dve", "pool")[self.cv_i % 3]
            self.cp(eng, bt[:, 0:C], st[:, 0:C], r=[st], w=[bt])
            d = dst[:, :, k, :].rearrange("cb p c -> p cb c")
            self.cv_pending.append((d, bt, C, CW, name))
            if len(self.cv_pending) > 2:
                self.cv_flush(1)
        return dst

    def cv_flush(self, n):
        for _ in range(n):
            if not self.cv_pending:
                return
            d, bt, C, CW, name = self.cv_pending.pop(0)
            self.dm(d, bt[:, 0:C].rearrange("p (cb c) -> p cb c", c=CW), r=[bt], w=[("dram", name)])

    def convert_jobs(self, src, K, C, CW, name):
        assert C % 512 == 0
        dst = self.dscr(name, [C // CW, P, K, CW], BF16)
        jobs = []
        for k in range(K):
            for j in range(C // 512):
                jobs.append((src, dst, k, j, CW, name))
        return dst, jobs

    def run_job(self, job, bg=True):
        src, dst, k, j, CW, name = job
        i = self.bg_i
        self.bg_i += 1
        st, bt = self.bg_f32[i % 4], self.bg_bf[i % 4]
        self.dm(st[:, :], src[k * P:(k + 1) * P, j * 512:(j + 1) * 512], w=[st], bg=bg)
        self.cp(self.bg_eng, bt[:, :], st[:, :], r=[st], w=[bt], bg=bg)
        if CW >= 512:
            c0 = j * 512
            d = dst[c0 // CW, :, k, c0 % CW:c0 % CW + 512]
            self.dm(d, bt[:, :], r=[bt], w=[("dram", name)], bg=bg)
        else:
            n = 512 // CW
            d = dst[j * n:(j + 1) * n, :, k, :].rearrange("cb p c -> p cb c")
            self.dm(d, bt[:, :].rearrange("p (cb c) -> p cb c", c=CW), r=[bt], w=[("dram", name)], bg=bg)

    def bg_take(self, key, n=None, bg=True):
        import os
        lst = self.bg_jobs.get(key)
        if not lst or (bg and os.environ.get("KBG", "1") == "0"):
            return
        if n is None or n > len(lst):
            n = len(lst)
        for job in lst[:n]:
            self.run_job(job, bg=bg)
        del lst[:n]

    def bg_need(self, key):
        if self.bg_jobs.get(key):
            self.bg_take(key, None, bg=False)
            self.S.barrier()

    def gain_T(self, dst2d, src_rows, rows, parts):
        stg = self.gstg[self.g_i % 2]
        bank = self.g_i % 4
        self.g_i += 1
        srcs = src_rows if isinstance(src_rows, list) else [(src_rows, 0, parts)]
        for ap, c0, w in srcs:
            self.dm(stg[0:rows, c0:c0 + w], ap, w=[stg])
        self.tr(self.ps(bank)[0:parts, 0:rows], stg[0:rows, 0:parts], self.ident_f[0:rows, 0:rows],
                r=[stg, self.ident_f], w=[self.PSK(bank)])
        self.cp("dve", dst2d, self.ps(bank)[0:parts, 0:rows], r=[self.PSK(bank)], w=[("gain", self.g_i)])

    def gain_fm(self, name, src, L, nk, parts=P, lo=0):
        t = self.arena.alloc(name, [L, nk], parts=parts)
        self.gain_T(t.ap.rearrange("p l k -> p (l k)"),
                    src[:, lo:lo + nk * parts].rearrange("l (k p) -> (l k) p", p=parts), L * nk, parts)
        return t

    def gain_rep(self, name, src, L, n):
        t = self.arena.alloc(name, [L, n])
        self.dm(t.ap.rearrange("p l n -> p (l n)")[:, None, :],
                src.rearrange("(a l) n -> a (l n)", a=1).partition_broadcast(P), w=[t])
        return t

    def rms_stats(self, chunks, ncols, scale, bank, rstd, sq_t):
        n = len(chunks)
        for i, (ap, key, parts) in enumerate(chunks):
            sq = sq_t[i % 2]
            self.actf(sq[0:parts, 0:ncols], ap, AF.Square, r=[key], w=[sq])
            self.mm(self.ps(bank)[:, 0:ncols], self.ones_b[0:parts, :], sq[0:parts, 0:ncols], i == 0, i == n - 1,
                    r=[sq, self.ones_b], w=[self.PSK(bank)])
        self.actf(rstd[:, 0:ncols], self.ps(bank)[:, 0:ncols], AF.Ln, r=[self.PSK(bank), self.eps_t], w=[rstd],
                  scale=scale, bias=self.eps_t[:, 0:1])
        self.actf(rstd[:, 0:ncols], rstd[:, 0:ncols], AF.Exp, r=[rstd], w=[rstd], scale=-0.5)

    def load_xt(self, xT, s, t, xt):
        tsl = slice(t * TT, (t + 1) * TT)
        for h in range(2):
            self.dm(xt[:, h * 8:(h + 1) * 8, :], xT[s, :, h * 8:(h + 1) * 8, tsl],
                    r=[("xT", s, t, h)], w=[(xt, k) for k in range(h * 8, h * 8 + 8)])

    def store_xt(self, xT, s, t, xt):
        tsl = slice(t * TT, (t + 1) * TT)
        for h in range(2):
            self.dm(xT[s, :, h * 8:(h + 1) * 8, tsl], xt[:, h * 8:(h + 1) * 8, :],
                    r=[(xt, k) for k in range(h * 8, h * 8 + 8)], w=[("xT", s, t, h)])

    def norm_h(self, xt, hT, g, l, rstd, sq_t, bank=0):
        self.rms_stats([(xt[:, k, :], (xt, k), P) for k in range(KD)], TT, 1.0 / D, bank, rstd, sq_t)
        for k in range(KD):
            self.stt(hT[:, k, :], xt[:, k, :], g[:, l, k:k + 1], rstd[:, :], ALU.mult, ALU.mult,
                     r=[(xt, k), rstd, g], w=[(hT, k)])
    def build(self):
        cfg, nc, S = self.cfg, self.nc, self.S
        NS, SQ, L = cfg.NS, cfg.S, cfg.L
        NT = SQ // TT
        ph = cfg.phases
        es = ExitStack()
        self.es = es
        es.enter_context(nc.allow_non_contiguous_dma(reason="tiny per-layer vectors / small tiles"))
        I = {}
        I["x"] = self.din("x", [NS, SQ, D])
        if "mix" in ph:
            I["positions"] = self.din("positions", [NS, SQ], I32)
            for n, sh in (("attn_norm_g", [L, D]), ("w_in", [L, D, IN_COLS]), ("conv_w", [L, 4, CONV_DIM]),
                          ("conv_b", [L, CONV_DIM]), ("dt_bias", [L, NHS]), ("a_log", [L, NHS]), ("d_skip", [L, NHS]),
                          ("ssd_norm_g", [L, SSD_INNER]), ("q_a_norm_g", [L, Q_LORA]), ("w_q_b", [L, Q_LORA, MLA_H * 192]),
                          ("kv_a_norm_g", [L, KV_LORA]), ("w_kv_b", [L, KV_LORA, MLA_H * 256]), ("mla_q_norm_g", [L, 192]),
                          ("mla_k_norm_g", [L, 192]), ("w_out", [L, D, D])):
                I[n] = self.din(n, sh)
        if "xattn" in ph:
            I["mem"] = self.din("mem", [NS, MEM, D])
            for n, sh in (("xattn_norm_g", [L, D]), ("mem_norm_g", [L, D]), ("w_xq", [L, D, 512]), ("w_xk", [L, D, 512]),
                          ("w_xv", [L, D, 512]), ("xq_norm_g", [L, 128]), ("xk_norm_g", [L, 128]), ("w_xo", [L, 512, D])):
                I[n] = self.din(n, sh)
        if "ffn" in ph:
            for n, sh in (("ffn_norm_g", [L, D]), ("w_gate", [L, D, FF]), ("w_up", [L, D, FF]), ("w_down", [L, FF, D])):
                I[n] = self.din(n, sh)
        out = nc.dram_tensor("out", [NS, SQ, D], F32, kind="ExternalOutput").ap()
        self.I = I
        xT = self.dscr("xT", [NS, P, KD, SQ])
        self.xT = xT
        ACOLS = 53000
        arena_t = es.enter_context(nc.sbuf_tensor("arena", [P, ACOLS], F32))
        self.arena = A = Arena(arena_t[:, :], ACOLS)
        self.psum = es.enter_context(nc.psum_tensor("psum", [P, 8, 512], F32))
        PSK = self.PSK

        ones_f = A.alloc("ones_f", [P])
        ident_f = A.alloc("ident_f", [P])
        mge_f = A.alloc("mge_f", [P])
        mgt_f = A.alloc("mgt_f", [P])
        ident_b = A.alloc("ident_b", [P], BF16)
        ones_b = A.alloc("ones_b", [P], BF16)
        mge_b = A.alloc("mge_b", [P], BF16)
        S.pool(lambda e: e.memset(ones_f[:, :], 1.0), w=[ones_f])
        S.pool(lambda e: e.affine_select(out=ident_f[:, :], in_=ones_f[:, :], pattern=[[1, P]],
                                         compare_op=ALU.is_equal, fill=0.0, base=0, channel_multiplier=-1),
               r=[ones_f], w=[ident_f])
        S.pool(lambda e: e.affine_select(out=mge_f[:, :], in_=ones_f[:, :], pattern=[[1, P]],
                                         compare_op=ALU.is_ge, fill=0.0, base=0, channel_multiplier=-1),
               r=[ones_f], w=[mge_f])
        S.pool(lambda e: e.affine_select(out=mgt_f[:, :], in_=ones_f[:, :], pattern=[[-1, P]],
                                         compare_op=ALU.is_ge, fill=0.0, base=-1, channel_multiplier=1),
               r=[ones_f], w=[mgt_f])
        self.cp("dve", ident_b[:, :], ident_f[:, :], r=[ident_f], w=[ident_b])
        self.cp("dve", ones_b[:, :], ones_f[:, :], r=[ones_f], w=[ones_b])
        self.cp("dve", mge_b[:, :], mge_f[:, :], r=[mge_f], w=[mge_b])
        self.ones_b, self.ident_f, self.ident_b, self.ones_f = ones_b, ident_f, ident_b, ones_f
        self.eps_t = A.alloc("eps_t", [1])
        S.pool(lambda e: e.memset(self.eps_t[:, :], EPS), w=[self.eps_t])
        self.mge_f, self.mgt_f, self.mge_b = mge_f, mgt_f, mge_b
        G = {}
        self.gstg = [A.alloc("gstg%d" % i, [P]) for i in range(2)]
        self.g_i = 0
        if "mix" in ph:
            G["attn"] = self.gain_fm("g_attn", I["attn_norm_g"], L, KD)
            cw = A.alloc("cw", [L, 4, 12])
            for l in range(L):
                self.gain_T(cw[:, l, :, :].rearrange("p j c -> p (j c)"),
                            I["conv_w"][l].rearrange("j (c p) -> (j c) p", p=P), 48, P)
            G["cw"] = cw
            G["cb"] = self.gain_fm("cb", I["conv_b"], L, 12)
            G["dtb"] = self.gain_rep("dtb", I["dt_bias"], L, NHS)
            G["dsk"] = self.gain_rep("dsk", I["d_skip"], L, NHS)
            alog = self.gain_rep("alog", I["a_log"], L, NHS)
            Aneg = A.alloc("Aneg", [L, NHS])
            self.actf(Aneg[:, :, :], alog[:, :, :], AF.Exp, r=[alog], w=[Aneg])
            self.ts("dve", Aneg[:, :, :], Aneg[:, :, :], -1.0, None, ALU.mult, None, r=[Aneg], w=[Aneg])
            G["Aneg"] = Aneg
            G["qa"] = self.gain_fm("g_qa", I["q_a_norm_g"], L, 4)
            G["kva"] = self.gain_fm("g_kva", I["kv_a_norm_g"], L, 4)
            for nm, src in (("q", I["mla_q_norm_g"]), ("k", I["mla_k_norm_g"])):
                G[nm + "n"] = self.gain_fm("g_%sn" % nm, src, L, 1)
                G[nm + "r"] = self.gain_fm("g_%sr" % nm, src, L, 1, parts=64, lo=128)
                sw = A.alloc("g_%srs" % nm, [L, 1], parts=64)
                self.gain_T(sw.ap.rearrange("p l k -> p (l k)"), [(src[:, 160:192], 0, 32), (src[:, 128:160], 32, 32)], L, 64)
                G[nm + "rs"] = sw
        if "xattn" in ph:
            G["xattn"] = self.gain_fm("g_xattn", I["xattn_norm_g"], L, KD)
            G["mem"] = self.gain_fm("g_mem", I["mem_norm_g"], L, KD)
            G["xq"] = self.gain_fm("g_xq", I["xq_norm_g"], L, 1)
            G["xk"] = self.gain_fm("g_xk", I["xk_norm_g"], L, 1)
        if "ffn" in ph:
            G["ffn"] = self.gain_fm("g_ffn", I["ffn_norm_g"], L, KD)
        self.G = G
        self.bg_f32 = [A.alloc("bgf%d" % i, [512]) for i in range(4)]
        self.bg_bf = [A.alloc("bgb%d" % i, [512], BF16) for i in range(4)]
        self.bg_i = 0
        self.bg_jobs = {}
        A.freeze()

        self.cv_i = 0
        self.cv_pending = []
        self.cv_f32 = [A.alloc("cvf%d" % i, [FF]) for i in range(3)]
        self.cv_bf = [A.alloc("cvb%d" % i, [FF], BF16) for i in range(3)]
        W = [dict() for _ in range(L)]

        def plain(l, key, grp, src, K, C, CW):
            nm = "W%s%d" % (key, l)
            if l == 0 and grp == "mx":
                W[l][key] = self.convert(src, K, C, CW, nm)
            else:
                W[l][key], jobs = self.convert_jobs(src, K, C, CW, nm)
                self.bg_jobs.setdefault((grp, l), []).extend(jobs)

        for l in range(L):
            if "mix" in ph:
                wi = I["w_in"][l]
                plain(l, "z", "mx", wi[:, 0:C0], KD, 1024, 512)
                plain(l, "xbc", "mx", wi[:, C0:C1], KD, 1536, 512)
                W[l]["sm"] = self.convert([(wi[:, C1:C2], lambda st: st[:, 0:16]),
                                           (wi[:, C4:C4 + 64], lambda st: st[:, 16:80]),
                                           (wi[:, C4 + 32:C4 + 64], lambda st: st[:, 80:112]),
                                           (wi[:, C4:C4 + 32], lambda st: st[:, 112:144])], KD, 144, 144, "Wsm%d" % l)
                plain(l, "qa", "mx", wi[:, C2:C3], KD, 512, 512)
                plain(l, "kva", "mx", wi[:, C3:C4], KD, 512, 512)
                wq3 = I["w_q_b"][l].rearrange("r (h c) -> r h c", c=192)
                v8 = lambda st, n, c: st[:, 0:n].rearrange("p (h c) -> p h c", c=c)
                W[l]["qbn"] = self.convert([(wq3[:, :, 0:128], lambda st: v8(st, 1024, 128))], 4, 1024, 1024, "Wqbn%d" % l)
                W[l]["qbr"] = self.convert([(wq3[:, :, 128:192], lambda st: v8(st, 512, 64))], 4, 512, 512, "Wqbr%d" % l)
                W[l]["qbrs"] = self.convert([(wq3[:, :, 160:192], lambda st: v8(st, 512, 64)[:, :, 0:32]),
                                             (wq3[:, :, 128:160], lambda st: v8(st, 512, 64)[:, :, 32:64])],
                                            4, 512, 512, "Wqbrs%d" % l)
                wk3 = I["w_kv_b"][l].rearrange("r (h c) -> r h c", c=256)
                W[l]["kvn"] = self.convert([(wk3[:, :, 0:128], lambda st: v8(st, 1024, 128))], 4, 1024, 1024, "Wkvn%d" % l)
                W[l]["kvv"] = self.convert([(wk3[:, :, 128:256], lambda st: v8(st, 1024, 128))], 4, 1024, 1024, "Wkvv%d" % l)
                plain(l, "out", "mx", I["w_out"][l], KD, D, 512)
            if "xattn" in ph:
                plain(l, "xq", "mx", I["w_xq"][l], KD, 512, 512)
                plain(l, "xk", "mx", I["w_xk"][l], KD, 512, 512)
                plain(l, "xv", "mx", I["w_xv"][l], KD, 512, 512)
                plain(l, "xo", "mx", I["w_xo"][l], 4, D, D)
            if "ffn" in ph:
                plain(l, "g", "ffn", I["w_gate"][l], KD, FF, 256)
                plain(l, "u", "ffn", I["w_up"][l], KD, FF, 256)
                plain(l, "d", "ffn", I["w_down"][l], KF, D, 128)
        self.cv_flush(100)
        S.barrier()
        A.reset()

        xin_t = [A.alloc("xin%d" % i, [4, D]) for i in range(2)]
        xo_t = [A.alloc("xo%d" % i, [KD, TT]) for i in range(2)]
        it = 0
        for s in range(NS):
            for t in range(NT):
                b = it % 2
                it += 1
                xi, xo = xin_t[b], xo_t[b]
                for tb in range(4):
                    self.dm(xi[:, tb, :], I["x"][s, t * TT + tb * P:t * TT + (tb + 1) * P, :], w=[(xi, tb)])
                for k in range(KD):
                    bank = k % 4
                    for tb in range(4):
                        self.tr(self.ps(bank)[:, tb * P:(tb + 1) * P], xi[:, tb, k * P:(k + 1) * P], ident_f[:, :],
                                r=[(xi, tb), ident_f], w=[PSK(bank)])
                    self.evac(xo[:, k, :], self.ps(bank), r=[PSK(bank)], w=[(xo, k)])
                self.store_xt(xT, s, t, xo)
        S.barrier()
        A.reset()

        if "mix" in ph:
            self.szt = self.dscr("szt", [NS, SQ, SSD_INNER])
            self.xbcT = self.dscr("xbcT", [NS, P, 12, SQ])
            self.dtt = self.dscr("dtt", [NS, SQ // TT, P, 4 * NHS])
            self.qnT = self.dscr("qnT", [NS, MLA_H, P, SQ], BF16)
            self.qrT = self.dscr("qrT", [NS, MLA_H, 64, SQ], BF16)
            self.knT = self.dscr("knT", [NS, MLA_H, P, SQ], BF16)
            self.krT = self.dscr("krT", [NS, MLA_H, 64, SQ], BF16)
            self.vt = self.dscr("vt", [NS, SQ, MLA_H * P], BF16)
            self.mixT = self.dscr("mixT", [NS, P, KD, SQ], BF16)
            self.phase_rope()
            S.barrier()
            A.reset()
        if "xattn" in ph:
            self.phase_mem()
            S.barrier()
            A.reset()

        import os
        KSSD = float(os.environ.get("KSSDHOST", "0"))
        HOST = {"assd": (0.15, 0.15, "pool"), "amla": (0.30, 0.30, "pool"), "ssd": (KSSD, KSSD, "act"),
                "attn": (0.18, 0.18, "dve"), "oproj": (0.10, 0.10, "pool"), "xattn": (0.27, 0.27, "pool")}
        self.bg_eng = "pool"
        for l in range(L):
            nffn = len(self.bg_jobs.get(("ffn", l), ()))
            nmx = len(self.bg_jobs.get(("mx", l + 1), ()))

            def host(nm):
                import os
                f1, f2, eng = HOST[nm]
                self.bg_eng = os.environ.get("KENG", eng)
                if f1 > 0:
                    self.bg_take(("ffn", l), int(nffn * f1) + 1)
                    self.bg_take(("mx", l + 1), int(nmx * f2) + 1)
                self.bg_eng = "pool"

            if "mix" in ph:
                self.bg_need(("mx", l))
                for f in (self.phase_assd, self.phase_amla, self.phase_ssd, self.phase_attn, self.phase_oproj):
                    nm = f.__name__[6:]
                    if any(p.startswith("only_") for p in ph) and ("only_" + nm) not in ph:
                        continue
                    host(nm)
                    f(l, W[l])
                    S.barrier()
                    A.reset()
            if "xattn" in ph and "noxattn" not in ph:
                self.bg_need(("mx", l))
                self.host_fn = lambda: host("xattn")
                self.phase_xattn(l, W[l])
                S.barrier()
                A.reset()
            if "ffn" in ph:
                self.bg_need(("ffn", l))
                if "mix" not in ph and l + 1 < L:
                    self.bg_take(("mx", l + 1))
                self.phase_ffn(l, W[l])
                S.barrier()
                A.reset()

        xi_t = [A.alloc("xfi%d" % i, [KD, TT]) for i in range(2)]
        ot_t = [A.alloc("xfo%d" % i, [4, D]) for i in range(2)]
        it = 0
        for s in range(NS):
            for t in range(NT):
                b = it % 2
                it += 1
                xi, ot = xi_t[b], ot_t[b]
                self.load_xt(xT, s, t, xi)
                n = 0
                for tb in range(4):
                    for kg in range(4):
                        bank = n % 4
                        n += 1
                        for kk in range(4):
                            k = kg * 4 + kk
                            self.tr(self.ps(bank)[:, kk * P:(kk + 1) * P], xi[:, k, tb * P:(tb + 1) * P], ident_f[:, :],
                                    r=[(xi, k), ident_f], w=[PSK(bank)])
                        self.evac(ot[:, tb, kg * 512:(kg + 1) * 512], self.ps(bank), r=[PSK(bank)], w=[(ot, tb)])
                for tb in range(4):
                    self.dm(out[s, t * TT + tb * P:t * TT + (tb + 1) * P, :], ot[:, tb, :], r=[(ot, tb)], w=[("out", s, t, tb)])
        S.emit(nc, es)
        es.close()
        return nc

    def phase_ffn(self, l, W):
        cfg, S, A = self.cfg, self.S, self.arena
        PSK = self.PSK
        NS, NT = cfg.NS, cfg.S // TT
        xT = self.xT
        Wg, Wu, Wd = W["g"], W["u"], W["d"]
        xt_t = [A.alloc("fx%d" % i, [KD, TT]) for i in range(2)]
        sq_t = [A.alloc("fsq%d" % i, [TT], BF16) for i in range(2)]
        rstd = A.alloc("frstd", [TT])
        hT = A.alloc("fh", [KD, TT], BF16)
        actT = A.alloc("fact", [KF, TT], BF16)
        wg_t = [A.alloc("fwg%d" % i, [KD, 256], BF16) for i in range(2)]
        wu_t = [A.alloc("fwu%d" % i, [KD, 256], BF16) for i in range(2)]
        wd_t = [A.alloc("fwd%d" % i, [KF, 128], BF16) for i in range(2)]
        sg_t = [A.alloc("fsg%d" % i, [TT]) for i in range(2)]
        it = 0
        wi = 0
        di = 0
        for s in range(NS):
            for t in range(NT):
                xt = xt_t[it % 2]
                it += 1
                self.load_xt(xT, s, t, xt)
                self.norm_h(xt, hT, self.G["ffn"], l, rstd, sq_t)
                for cg in range(FF // 256):
                    wg, wu = wg_t[wi % 2], wu_t[wi % 2]
                    wi += 1
                    self.dm(wg[:, :, :], Wg[cg], w=[wg])
                    self.dm(wu[:, :, :], Wu[cg], w=[wu])
                    for c in range(2):
                        j = cg * 2 + c
                        bg, bu = 1 + (j % 2), 3 + (j % 2)
                        for k in range(KD):
                            self.mm(self.ps(bg), wg[:, k, c * P:(c + 1) * P], hT[:, k, :], k == 0, k == KD - 1,
                                    r=[wg, (hT, k)], w=[PSK(bg)])
                        for k in range(KD):
                            self.mm(self.ps(bu), wu[:, k, c * P:(c + 1) * P], hT[:, k, :], k == 0, k == KD - 1,
                                    r=[wu, (hT, k)], w=[PSK(bu)])
                        sg = sg_t[j % 2]
                        self.actf(sg[:, :], self.ps(bg), AF.Silu, r=[PSK(bg)], w=[sg])
                        self.tt("dve", actT[:, j, :], sg[:, :], self.ps(bu), ALU.mult, r=[sg, PSK(bu)], w=[(actT, j)])
                for dk in range(KD):
                    wd = wd_t[di % 2]
                    bo = 5 + (di % 2)
                    di += 1
                    for h in range(2):
                        self.dm(wd[:, h * 22:(h + 1) * 22, :], Wd[dk][:, h * 22:(h + 1) * 22, :], w=[(wd, h)])
                    for j in range(KF):
                        self.mm(self.ps(bo), wd[:, j, :], actT[:, j, :], j == 0, j == KF - 1,
                                r=[(wd, j // 22), (actT, j)], w=[PSK(bo)])
                    self.tt("dve", xt[:, dk, :], self.ps(bo), xt[:, dk, :], ALU.add, r=[PSK(bo), (xt, dk)], w=[(xt, dk)])
                self.store_xt(xT, s, t, xt)
    def phase_mem(self):
        cfg, A = self.cfg, self.arena
        PSK = self.PSK
        NS = cfg.NS
        mem = self.I["mem"]
        self.memT = self.dscr("memT", [NS, P, KD, MEM], BF16)
        mt_t = [A.alloc("mt%d" % i, [D]) for i in range(2)]
        junk = A.alloc("mjunk", [D], BF16)
        ss = A.alloc("mss", [2])
        mh_t = [A.alloc("mh%d" % i, [D], BF16) for i in range(2)]
        mo_t = [A.alloc("mo%d" % i, [KD, P], BF16) for i in range(2)]
        it = 0
        for s in range(NS):
            for mb in range(MEM // P):
                b = it % 2
                it += 1
                mt, mh, mo = mt_t[b], mh_t[b], mo_t[b]
                sc = ss[:, b:b + 1]
                self.dm(mt[:, :], mem[s, mb * P:(mb + 1) * P, :], w=[mt])
                self.actf(junk[:, :], mt[:, :], AF.Square, r=[mt], w=[junk, (ss, b)], accum=sc)
                self.ts("dve", sc, sc, 1.0 / D, EPS, ALU.mult, ALU.add, r=[(ss, b)], w=[(ss, b)])
                self.actf(sc, sc, AF.Sqrt, r=[(ss, b)], w=[(ss, b)])
                self.recip(sc, sc, r=[(ss, b)], w=[(ss, b)])
                self.ts("dve", mh[:, :], mt[:, :], sc, None, ALU.mult, None, r=[mt, (ss, b)], w=[mh])
                for k in range(KD):
                    bank = 2 * b + k // 8
                    self.tr(self.psb(bank)[:, (k % 8) * P:(k % 8 + 1) * P], mh[:, k * P:(k + 1) * P], self.ident_b[:, :],
                            r=[mh, self.ident_b], w=[PSK(bank)])
                for hf in range(2):
                    bank = 2 * b + hf
                    self.evac(mo[:, hf * 8:(hf + 1) * 8, :], self.psb(bank).rearrange("p (a b) -> p a b", a=8),
                              r=[PSK(bank)], w=[(mo, hf)])
                self.dm(self.memT[s, :, :, mb * P:(mb + 1) * P], mo[:, :, :], r=[(mo, 0), (mo, 1)], w=[("memT", s, mb)])

    def phase_xattn(self, l, W):
        cfg, A, G = self.cfg, self.arena, self.G
        PSK = self.PSK
        NS, NT = cfg.NS, cfg.S // TT
        xT = self.xT
        kT = A.alloc("kT", [NS, XH, MEM], BF16)
        V = A.alloc("V", [NS, 2, 512], BF16)
        sq_t = [A.alloc("xsq%d" % i, [TT], BF16) for i in range(2)]
        rk_t = [A.alloc("xrk%d" % i, [TT]) for i in range(2)]
        mark = A.off
        Wxk = A.alloc("Wxk", [KD, 512], BF16)
        Wxv = A.alloc("Wxv", [KD, 512], BF16)
        self.dm(Wxk[:, :, :], W["xk"][0], w=[Wxk])
        self.dm(Wxv[:, :, :], W["xv"][0], w=[Wxv])
        mT_t = [A.alloc("mT%d" % i, [KD, MEM], BF16) for i in range(2)]
        n = 0
        for s in range(NS):
            mT = mT_t[s % 2]
            self.dm(mT[:, :, :], self.memT[s], r=[("memT", s)], w=[mT])
            for k in range(KD):
                self.ts("dve" if k % 2 else "pool", mT[:, k, :], mT[:, k, :], G["mem"][:, l, k:k + 1], None, ALU.mult, None,
                        r=[mT, G["mem"]], w=[(mT, k)])
            for h in range(XH):
                bq, bs_ = 1 + n % 2, 3 + n % 2
                rk = rk_t[n % 2]
                sqs = [sq_t[n % 2], sq_t[(n + 1) % 2]]
                n += 1
                for k in range(KD):
                    self.mm(self.ps(bq)[:, 0:MEM], Wxk[:, k, h * P:(h + 1) * P], mT[:, k, :], k == 0, k == KD - 1,
                            r=[Wxk, (mT, k), mT], w=[PSK(bq)])
                self.rms_stats([(self.ps(bq)[:, 0:MEM], PSK(bq), P)], MEM, 1.0 / 128, bs_, rk, sqs)
                self.stt(kT[:, s, h, :], self.ps(bq)[:, 0:MEM], G["xk"][:, l, 0:1], rk[:, 0:MEM], ALU.mult, ALU.mult,
                         r=[PSK(bq), rk, G["xk"]], w=[(kT, s, h)])
            for mb in range(2):
                bv = 5 + mb
                for k in range(KD):
                    self.mm(self.ps(bv), mT[:, k, mb * P:(mb + 1) * P], Wxv[:, k, :], k == 0, k == KD - 1,
                            r=[Wxv, (mT, k), mT], w=[PSK(bv)])
                self.evac(V[:, s, mb, :], self.ps(bv), r=[PSK(bv)], w=[(V, s, mb)])
        if "xe0" in cfg.phases:
            return
        self.S.barrier()
        A.off = mark
        self.host_fn()
        Wxq = A.alloc("Wxq", [KD, 512], BF16)
        Wxo = A.alloc("Wxo", [4, D], BF16)
        self.dm(Wxq[:, :, :], W["xq"][0], w=[Wxq])
        self.dm(Wxo[:, :, :], W["xo"][0], w=[Wxo])
        xt_t = [A.alloc("xx%d" % i, [KD, TT]) for i in range(2)]
        rstd = A.alloc("xrstd", [TT])
        hT = A.alloc("xh", [KD, TT], BF16)
        qn_t = [A.alloc("xqn%d" % i, [TT], BF16) for i in range(XH)]
        pT_t = [[A.alloc("xp%d_%d" % (i, mb), [TT], BF16) for mb in range(2)] for i in range(XH)]
        rden_t = [A.alloc("xrden%d" % i, [TT]) for i in range(2)]
        oTn = A.alloc("xo", [XH, TT], BF16)
        it = 0
        for s in range(NS):
            for t in range(NT):
                xt = xt_t[it % 2]
                it += 1
                self.load_xt(xT, s, t, xt)
                self.norm_h(xt, hT, G["xattn"], l, rstd, sq_t)
                for h in range(XH):
                    bq, bs_ = 1 + h, 5 + h % 2
                    rk, qn, pT, rden = rk_t[h % 2], qn_t[h], pT_t[h], rden_t[h % 2]
                    sqs = [sq_t[h % 2], sq_t[(h + 1) % 2]]
                    for k in range(KD):
                        self.mm(self.ps(bq), Wxq[:, k, h * P:(h + 1) * P], hT[:, k, :], k == 0, k == KD - 1,
                                r=[Wxq, (hT, k)], w=[PSK(bq)])
                    self.rms_stats([(self.ps(bq), PSK(bq), P)], TT, 1.0 / 128, bs_, rk, sqs)
                    self.stt(qn[:, :], self.ps(bq), G["xq"][:, l, 0:1], rk[:, :], ALU.mult, ALU.mult,
                             r=[PSK(bq), rk, G["xq"]], w=[qn])
                    for mb in range(2):
                        bsc = bq if mb == 0 else bs_
                        self.mm(self.ps(bsc), kT[:, s, h, mb * P:(mb + 1) * P], qn[:, :], True, True,
                                r=[(kT, s, h), qn], w=[PSK(bsc)])
                        self.actf(pT[mb][:, :], self.ps(bsc), AF.Exp, r=[PSK(bsc)], w=[pT[mb]], scale=float(128 ** -0.5))
                    for mb in range(2):
                        self.mm(self.ps(7), V[:, s, mb, h * P:(h + 1) * P], pT[mb][:, :], mb == 0, mb == 1,
                                r=[(V, s, mb), pT[mb]], w=[PSK(7)])
                    for mb in range(2):
                        self.mm(self.ps(0), self.ones_b[:, :], pT[mb][:, :], mb == 0, mb == 1,
                                r=[self.ones_b, pT[mb]], w=[PSK(0)])
                    self.actf(rden[:, :], self.ps(0), AF.Ln, r=[PSK(0)], w=[rden])
                    self.actf(rden[:, :], rden[:, :], AF.Exp, r=[rden], w=[rden], scale=-1.0)
                    self.tt("dve", oTn[:, h, :], self.ps(7), rden[:, :], ALU.mult, r=[PSK(7), rden], w=[(oTn, h)])
                for dk in range(KD):
                    bo = 1 + dk % 4
                    for h in range(XH):
                        self.mm(self.ps(bo), Wxo[:, h, dk * P:(dk + 1) * P], oTn[:, h, :], h == 0, h == XH - 1,
                                r=[Wxo, (oTn, h)], w=[PSK(bo)])
                    self.tt("dve", xt[:, dk, :], self.ps(bo), xt[:, dk, :], ALU.add, r=[PSK(bo), (xt, dk)], w=[(xt, dk)])
                self.store_xt(xT, s, t, xt)
    def phase_rope(self):
        cfg, A, S = self.cfg, self.arena, self.S
        NS, SQ = cfg.NS, cfg.S
        self.ropeC = self.dscr("ropeC", [NS, 64, SQ])
        self.ropeS = self.dscr("ropeS", [NS, 64, SQ])
        ji = A.alloc("ji", [1], I32, parts=64)
        jf = A.alloc("jf", [1], parts=64)
        invf = A.alloc("invf", [1], parts=64)
        S.pool(lambda e: e.iota(out=ji[0:32, :], pattern=[[0, 1]], base=0, channel_multiplier=1), w=[ji])
        S.pool(lambda e: e.iota(out=ji[32:64, :], pattern=[[0, 1]], base=0, channel_multiplier=1), w=[ji])
        self.cp("dve", jf[:, :], ji[:, :], r=[ji], w=[jf])
        self.actf(invf[:, :], jf[:, :], AF.Exp, r=[jf], w=[invf], scale=float(-np.log(10000.0) / 32.0))
        pos_i = A.alloc("pos_i", [SQ], I32, parts=64)
        ang = A.alloc("ang", [SQ], parts=64)
        a2 = A.alloc("a2", [SQ], parts=64)
        ni = A.alloc("ni", [SQ], I32, parts=64)
        nf = A.alloc("nf", [SQ], parts=64)
        rr = A.alloc("rr", [SQ], parts=64)
        m_ = A.alloc("m_", [SQ], parts=64)
        tabs = [A.alloc("tabS", [SQ], parts=64), A.alloc("tabC", [SQ], parts=64)]
        TWO_PI = 2.0 * PI
        for s in range(NS):
            self.dm(pos_i[:, :].rearrange("p (a s) -> p a s", a=1), self.I["positions"][s:s + 1, :].partition_broadcast(64), w=[pos_i])
            self.cp("dve", ang[:, :], pos_i[:, :], r=[pos_i], w=[ang])
            self.ts("dve", ang[:, :], ang[:, :], invf[:, 0:1], None, ALU.mult, None, r=[ang, invf], w=[ang])
            for tab, shift in ((tabs[0], 0.0), (tabs[1], PI / 2)):
                self.ts("dve", a2[:, :], ang[:, :], shift, 1.0 / TWO_PI, ALU.add, ALU.mult, r=[ang], w=[a2])
                self.cp("dve", ni[:, :], a2[:, :], r=[a2], w=[ni])
                self.cp("dve", nf[:, :], ni[:, :], r=[ni], w=[nf])
                self.stt(rr[:, :], nf[:, :], -TWO_PI, ang[:, :], ALU.mult, ALU.add, r=[nf, ang], w=[rr])
                if shift:
                    self.ts("dve", rr[:, :], rr[:, :], shift, None, ALU.add, None, r=[rr], w=[rr])
                self.ts("dve", m_[:, :], rr[:, :], PI, TWO_PI, ALU.is_gt, ALU.mult, r=[rr], w=[m_])
                self.tt("dve", rr[:, :], rr[:, :], m_[:, :], ALU.subtract, r=[rr, m_], w=[rr])
                self.ts("dve", m_[:, :], rr[:, :], -PI, TWO_PI, ALU.is_lt, ALU.mult, r=[rr], w=[m_])
                self.tt("dve", rr[:, :], rr[:, :], m_[:, :], ALU.add, r=[rr, m_], w=[rr])
                self.ts("dve", rr[:, :], rr[:, :], -3.1415925, 3.1415925, ALU.max, ALU.min, r=[rr], w=[rr])
                self.actf(tab[:, :], rr[:, :], AF.Sin, r=[rr], w=[tab])
            self.ts("dve", tabs[0][0:32, :], tabs[0][0:32, :], -1.0, None, ALU.mult, None, r=[tabs[0]], w=[tabs[0]])
            self.dm(self.ropeS[s], tabs[0][:, :], r=[tabs[0]], w=[("ropeS", s)])
            self.dm(self.ropeC[s], tabs[1][:, :], r=[tabs[1]], w=[("ropeC", s)])

    def phase_assd(self, l, W):
        cfg, A, G = self.cfg, self.arena, self.G
        PSK = self.PSK
        NS, SQ, NT = cfg.NS, cfg.S, cfg.S // TT
        xt_t = [A.alloc("ax%d" % i, [KD, TT]) for i in range(2)]
        sq_t = [A.alloc("asq%d" % i, [TT], BF16) for i in range(2)]
        rstd = A.alloc("arstd", [TT])
        hT = A.alloc("ah", [KD, TT], BF16)
        slab_t = [A.alloc("aslab%d" % i, [KD, 512], BF16) for i in range(3)]
        wsm = A.alloc("awsm", [KD, 144], BF16)
        stz = A.alloc("astz", [4, SSD_INNER])
        stx = A.alloc("astx", [12, TT])
        dts = A.alloc("adts", [4, NHS])
        self.dm(wsm[:, :, :], W["sm"][0], w=[wsm])
        it = 0
        si = 0
        n = 0
        for s in range(NS):
            for t in range(NT):
                xt = xt_t[it % 2]
                it += 1
                tsl = slice(t * TT, (t + 1) * TT)
                self.load_xt(self.xT, s, t, xt)
                self.norm_h(xt, hT, G["attn"], l, rstd, sq_t)
                import os
                skip = os.environ.get("KSKIP", "").split(",")
                for half in range(2):
                    if "z" in skip:
                        break
                    slab = slab_t[si % 3]
                    si += 1
                    self.dm(slab[:, :, :], W["z"][half], w=[slab])
                    for tb in range(4):
                        bank = 1 + n % 2
                        n += 1
                        for k in range(KD):
                            self.mm(self.ps(bank), hT[:, k, tb * P:(tb + 1) * P], slab[:, k, :], k == 0, k == KD - 1,
                                    r=[slab, (hT, k)], w=[PSK(bank)])
                        self.actf(stz[:, tb, half * 512:(half + 1) * 512], self.ps(bank), AF.Silu, r=[PSK(bank)], w=[(stz, tb)])
                for tb in range(4):
                    if "z" in skip:
                        break
                    self.dm(self.szt[s, t * TT + tb * P:t * TT + (tb + 1) * P, :], stz[:, tb, :], r=[(stz, tb)], w=[("szt", s, t, tb)])
                for tb in range(4):
                    if "dt" in skip:
                        break
                    for k in range(KD):
                        self.mm(self.ps(3)[:, tb * NHS:(tb + 1) * NHS], hT[:, k, tb * P:(tb + 1) * P], wsm[:, k, 0:NHS],
                                k == 0, k == KD - 1, r=[wsm, (hT, k)], w=[PSK(3)])
                if "dt" not in skip:
                    self.evac(dts[:, :, :], self.ps(3)[:, 0:4 * NHS].rearrange("p (a b) -> p a b", a=4), r=[PSK(3)], w=[dts])
                    if "dtdma" not in skip:
                        self.dm(self.dtt[s, t], dts.ap.rearrange("p a b -> p (a b)"), r=[dts], w=[("dtt", s, t)])
                for sl in range(3):
                    if "xbc" in skip:
                        break
                    slab = slab_t[si % 3]
                    si += 1
                    self.dm(slab[:, :, :], W["xbc"][sl], w=[slab])
                    for c in range(4):
                        ch = sl * 4 + c
                        bank = 4 + n % 2
                        n += 1
                        for k in range(KD):
                            self.mm(self.ps(bank), slab[:, k, c * P:(c + 1) * P], hT[:, k, :], k == 0, k == KD - 1,
                                    r=[slab, (hT, k)], w=[PSK(bank)])
                        self.evac(stx[:, ch, :], self.ps(bank), r=[PSK(bank)], w=[(stx, ch)])
                for h in range(2):
                    if "xbc" in skip:
                        break
                    self.dm(self.xbcT[s, :, h * 6:(h + 1) * 6, tsl], stx[:, h * 6:(h + 1) * 6, :],
                            r=[(stx, c) for c in range(h * 6, h * 6 + 6)], w=[("xbcT", s, t, h)])

    def phase_amla(self, l, W):
        cfg, A, G = self.cfg, self.arena, self.G
        PSK = self.PSK
        NS, SQ, NT = cfg.NS, cfg.S, cfg.S // TT
        xt = A.alloc("mx", [KD, TT])
        sq_t = [A.alloc("msq%d" % i, [TT], BF16) for i in range(2)]
        rstd = A.alloc("mrstd", [TT])
        rh = A.alloc("mrh", [TT])
        hT = A.alloc("mh", [KD, TT], BF16)
        slab = A.alloc("mslab", [KD, 512], BF16)
        wsm = A.alloc("mwsm", [KD, 144], BF16)
        Wqbn = A.alloc("Wqbn", [4, 1024], BF16)
        Wqbr = A.alloc("Wqbr", [4, 512], BF16)
        Wqbrs = A.alloc("Wqbrs", [4, 512], BF16)
        Wkvn = A.alloc("Wkvn", [4, 1024], BF16)
        Wkvv = A.alloc("Wkvv", [4, 1024], BF16)
        for tl, nm in ((wsm, "sm"), (Wqbn, "qbn"), (Wqbr, "qbr"), (Wqbrs, "qbrs"), (Wkvn, "kvn"), (Wkvv, "kvv")):
            self.dm(tl[:, :, :], W[nm][0], w=[tl])
        la = A.alloc("mla", [4, TT])
        lan = A.alloc("mlan", [4, TT], BF16)
        stn = A.alloc("mstn", [MLA_H, TT], BF16)
        str_ = A.alloc("mstr", [MLA_H, TT], BF16, parts=64)
        vts = A.alloc("mvts", [4, MLA_H * P], BF16)
        c2 = A.alloc("mc2", [TT], parts=64)
        s2 = A.alloc("ms2", [TT], parts=64)
        t1 = A.alloc("mt1", [TT], parts=64)
        t2 = A.alloc("mt2", [TT], parts=64)
        kr0 = A.alloc("mkr0", [TT], parts=64)
        sqk = A.alloc("msqk", [TT], BF16, parts=64)
        sqn = A.alloc("msqn", [TT], BF16)
        n = 0
        for s in range(NS):
            for t in range(NT):
                tsl = slice(t * TT, (t + 1) * TT)
                self.load_xt(self.xT, s, t, xt)
                self.norm_h(xt, hT, G["attn"], l, rstd, sq_t)
                self.dm(c2[:, :], self.ropeC[s, :, tsl], w=[c2])
                self.dm(s2[:, :], self.ropeS[s, :, tsl], w=[s2])

                def lora(wname, gname):
                    self.dm(slab[:, :, :], W[wname][0], w=[slab])
                    for c in range(4):
                        bank = 1 + c % 2
                        for k in range(KD):
                            self.mm(self.ps(bank), slab[:, k, c * P:(c + 1) * P], hT[:, k, :], k == 0, k == KD - 1,
                                    r=[slab, (hT, k)], w=[PSK(bank)])
                        self.evac(la[:, c, :], self.ps(bank), r=[PSK(bank)], w=[(la, c)])
                    self.rms_stats([(la[:, c, :], (la, c), P) for c in range(4)], TT, 1.0 / 512, 0, rstd, sq_t)
                    for c in range(4):
                        self.stt(lan[:, c, :], la[:, c, :], G[gname][:, l, c:c + 1], rstd[:, :], ALU.mult, ALU.mult,
                                 r=[(la, c), rstd, G[gname]], w=[(lan, c)])

                lora("qa", "qa")
                for h in range(MLA_H):
                    bn, br, bs = (1, 2, 3) if h % 2 == 0 else (4, 5, 6)
                    for kk in range(4):
                        self.mm(self.ps(bn), Wqbn[:, kk, h * P:(h + 1) * P], lan[:, kk, :], kk == 0, kk == 3,
                                r=[Wqbn, (lan, kk)], w=[PSK(bn)])
                    for kk in range(4):
                        self.mm(self.ps(br)[0:64, :], Wqbr[:, kk, h * 64:(h + 1) * 64], lan[:, kk, :], kk == 0, kk == 3,
                                r=[Wqbr, (lan, kk)], w=[PSK(br)])
                    for kk in range(4):
                        self.mm(self.ps(bs)[0:64, :], Wqbrs[:, kk, h * 64:(h + 1) * 64], lan[:, kk, :], kk == 0, kk == 3,
                                r=[Wqbrs, (lan, kk)], w=[PSK(bs)])
                    self.rms_stats([(self.ps(bn), PSK(bn), P), (self.ps(br)[0:64, :], PSK(br), 64)], TT, 1.0 / 192, 7, rh, sq_t)
                    self.stt(stn[:, h, :], self.ps(bn), G["qn"][:, l, 0:1], rh[:, :], ALU.mult, ALU.mult,
                             r=[PSK(bn), rh, G["qn"]], w=[(stn, h)])
                    self.stt(t1[:, :], self.ps(br)[0:64, :], G["qr"][:, l, 0:1], rh[0:64, :], ALU.mult, ALU.mult,
                             r=[PSK(br), rh, G["qr"]], w=[t1])
                    self.stt(t2[:, :], self.ps(bs)[0:64, :], G["qrs"][:, l, 0:1], rh[0:64, :], ALU.mult, ALU.mult,
                             r=[PSK(bs), rh, G["qrs"]], w=[t2])
                    self.tt("pool", t1[:, :], t1[:, :], c2[:, :], ALU.mult, r=[t1, c2], w=[t1])
                    self.tt("pool", t2[:, :], t2[:, :], s2[:, :], ALU.mult, r=[t2, s2], w=[t2])
                    self.tt("pool", str_[:, h, :], t1[:, :], t2[:, :], ALU.add, r=[t1, t2], w=[(str_, h)])
                self.dm(self.qnT[s, :, :, tsl].rearrange("h p t -> p h t"), stn[:, :, :],
                        r=[(stn, h) for h in range(MLA_H)], w=[("qnT", s, t)])
                self.dm(self.qrT[s, :, :, tsl].rearrange("h p t -> p h t"), str_[:, :, :],
                        r=[(str_, h) for h in range(MLA_H)], w=[("qrT", s, t)])
                lora("kva", "kva")
                for k in range(KD):
                    self.mm(self.ps(2)[0:64, :], wsm[:, k, 16:80], hT[:, k, :], k == 0, k == KD - 1,
                            r=[wsm, (hT, k)], w=[PSK(2)])
                for k in range(KD):
                    self.mm(self.ps(3)[0:64, :], wsm[:, k, 80:144], hT[:, k, :], k == 0, k == KD - 1,
                            r=[wsm, (hT, k)], w=[PSK(3)])
                self.actf(sqk[:, :], self.ps(2)[0:64, :], AF.Square, r=[PSK(2)], w=[sqk])
                self.ts("dve", t1[:, :], self.ps(2)[0:64, :], G["kr"][:, l, 0:1], None, ALU.mult, None, r=[PSK(2), G["kr"]], w=[t1])
                self.ts("dve", t2[:, :], self.ps(3)[0:64, :], G["krs"][:, l, 0:1], None, ALU.mult, None, r=[PSK(3), G["krs"]], w=[t2])
                self.tt("pool", t1[:, :], t1[:, :], c2[:, :], ALU.mult, r=[t1, c2], w=[t1])
                self.tt("pool", t2[:, :], t2[:, :], s2[:, :], ALU.mult, r=[t2, s2], w=[t2])
                self.tt("pool", kr0[:, :], t1[:, :], t2[:, :], ALU.add, r=[t1, t2], w=[kr0])
                for h in range(MLA_H):
                    bn = 4 + h % 2
                    for kk in range(4):
                        self.mm(self.ps(bn), Wkvn[:, kk, h * P:(h + 1) * P], lan[:, kk, :], kk == 0, kk == 3,
                                r=[Wkvn, (lan, kk)], w=[PSK(bn)])
                    self.actf(sqn[:, :], self.ps(bn), AF.Square, r=[PSK(bn)], w=[sqn])
                    self.mm(self.ps(7), self.ones_b[:, :], sqn[:, :], True, False, r=[sqn, self.ones_b], w=[PSK(7)])
                    self.mm(self.ps(7), self.ones_b[0:64, :], sqk[:, :], False, True, r=[sqk, self.ones_b], w=[PSK(7)])
                    self.actf(rh[:, :], self.ps(7), AF.Ln, r=[PSK(7), self.eps_t], w=[rh], scale=1.0 / 192, bias=self.eps_t[:, 0:1])
                    self.actf(rh[:, :], rh[:, :], AF.Exp, r=[rh], w=[rh], scale=-0.5)
                    self.stt(stn[:, h, :], self.ps(bn), G["kn"][:, l, 0:1], rh[:, :], ALU.mult, ALU.mult,
                             r=[PSK(bn), rh, G["kn"]], w=[(stn, h)])
                    self.tt("pool", str_[:, h, :], kr0[:, :], rh[0:64, :], ALU.mult, r=[kr0, rh], w=[(str_, h)])
                self.dm(self.knT[s, :, :, tsl].rearrange("h p t -> p h t"), stn[:, :, :],
                        r=[(stn, h) for h in range(MLA_H)], w=[("knT", s, t)])
                self.dm(self.krT[s, :, :, tsl].rearrange("h p t -> p h t"), str_[:, :, :],
                        r=[(str_, h) for h in range(MLA_H)], w=[("krT", s, t)])
                for tb in range(4):
                    for half in range(2):
                        bank = 1 + n % 2
                        n += 1
                        for kk in range(4):
                            self.mm(self.ps(bank), lan[:, kk, tb * P:(tb + 1) * P], Wkvv[:, kk, half * 512:(half + 1) * 512],
                                    kk == 0, kk == 3, r=[Wkvv, (lan, kk)], w=[PSK(bank)])
                        self.evac(vts[:, tb, half * 512:(half + 1) * 512], self.ps(bank), r=[PSK(bank)], w=[(vts, tb)])
                self.dm(self.vt[s, tsl, :].rearrange("(tb p) c -> p tb c", p=P), vts[:, :, :],
                        r=[(vts, tb) for tb in range(4)], w=[("vt", s, t)])

    def phase_attn(self, l, W):
        cfg, A = self.cfg, self.arena
        PSK = self.PSK
        NS, SQ, NT = cfg.NS, cfg.S, cfg.S // TT
        NJ = SQ // P
        kn_t = [A.alloc("ckn%d" % i, [SQ], BF16) for i in range(2)]
        kr_t = [A.alloc("ckr%d" % i, [SQ], BF16, parts=64) for i in range(2)]
        v_all = A.alloc("cvall", [NJ, MLA_H * P], BF16)
        qn_t = [A.alloc("cqn%d" % i, [TT], BF16) for i in range(2)]
        qr_t = [A.alloc("cqr%d" % i, [TT], BF16, parts=64) for i in range(2)]
        p_t = [A.alloc("cp%d" % i, [TT], BF16) for i in range(3)]
        rden = A.alloc("crden", [TT])
        o_t = [A.alloc("co%d" % i, [TT], BF16) for i in range(2)]
        scale = float(192 ** -0.5)
        hi = 0
        qi = 0
        pj = 0
        for s in range(NS):
            vsrc = self.vt[s].rearrange("(j p) c -> p j c", p=P)
            for hf in range(2):
                self.dm(v_all[:, hf * (NJ // 2):(hf + 1) * (NJ // 2), :], vsrc[:, hf * (NJ // 2):(hf + 1) * (NJ // 2), :], w=[v_all])
            for h in range(MLA_H):
                kn, kr = kn_t[hi % 2], kr_t[hi % 2]
                hi += 1
                self.dm(kn[:, :], self.knT[s, h], w=[kn])
                self.dm(kr[:, :], self.krT[s, h], w=[kr])
                for Q in range(NT):
                    qn, qr, ot = qn_t[qi % 2], qr_t[qi % 2], o_t[qi % 2]
                    bo, bd = (4, 5) if qi % 2 == 0 else (6, 7)
                    qi += 1
                    qsl = slice(Q * TT, (Q + 1) * TT)
                    self.dm(qn[:, :], self.qnT[s, h, :, qsl], w=[qn])
                    self.dm(qr[:, :], self.qrT[s, h, :, qsl], w=[qr])
                    nj = 4 * Q + 4
                    for j in range(nj):
                        r_ = j - 4 * Q
                        q0 = P * max(r_, 0)
                        bs_ = pj % 3
                        p = p_t[pj % 3]
                        pj += 1
                        self.mm(self.ps(bs_)[:, q0:TT], kn[:, j * P:(j + 1) * P], qn[:, q0:TT], True, False,
                                r=[kn, qn], w=[PSK(bs_)])
                        self.mm(self.ps(bs_)[:, q0:TT], kr[:, j * P:(j + 1) * P], qr[:, q0:TT], False, True,
                                r=[kr, qr], w=[PSK(bs_)])
                        self.actf(p[:, q0:TT], self.ps(bs_)[:, q0:TT], AF.Exp, r=[PSK(bs_)], w=[p], scale=scale)
                        if r_ >= 0:
                            self.tt("pool", p[:, q0:q0 + P], p[:, q0:q0 + P], self.mge_b[:, :], ALU.mult, r=[p, self.mge_b], w=[p])
                        self.mm(self.ps(bo)[:, q0:TT], v_all[:, j, h * P:(h + 1) * P], p[:, q0:TT], j == 0, j == nj - 1,
                                r=[v_all, p], w=[PSK(bo)])
                        self.mm(self.ps(bd)[:, q0:TT], self.ones_b[:, :], p[:, q0:TT], j == 0, j == nj - 1,
                                r=[self.ones_b, p], w=[PSK(bd)])
                    self.actf(rden[:, :], self.ps(bd), AF.Ln, r=[PSK(bd)], w=[rden])
                    self.actf(rden[:, :], rden[:, :], AF.Exp, r=[rden], w=[rden], scale=-1.0)
                    self.tt("dve", ot[:, :], self.ps(bo), rden[:, :], ALU.mult, r=[PSK(bo), rden], w=[ot])
                    self.dm(self.mixT[s, :, 8 + h, qsl], ot[:, :], r=[ot], w=[("mixT", s, h, Q)])

    def phase_oproj(self, l, W):
        cfg, A = self.cfg, self.arena
        PSK = self.PSK
        NS, NT = cfg.NS, cfg.S // TT
        xt_t = [A.alloc("ox%d" % i, [KD, TT]) for i in range(2)]
        mx_t = [A.alloc("om%d" % i, [KD, TT], BF16) for i in range(2)]
        slab_t = [A.alloc("oslab%d" % i, [KD, 512], BF16) for i in range(2)]
        it = 0
        si = 0
        for s in range(NS):
            for t in range(NT):
                xt, mx = xt_t[it % 2], mx_t[it % 2]
                it += 1
                tsl = slice(t * TT, (t + 1) * TT)
                self.load_xt(self.xT, s, t, xt)
                for h in range(2):
                    self.dm(mx[:, h * 8:(h + 1) * 8, :], self.mixT[s, :, h * 8:(h + 1) * 8, tsl], w=[(mx, h)])
                for sl in range(4):
                    slab = slab_t[si % 2]
                    si += 1
                    self.dm(slab[:, :, :], W["out"][sl], w=[slab])
                    for c in range(4):
                        dk = sl * 4 + c
                        bank = 1 + dk % 2
                        for k in range(KD):
                            self.mm(self.ps(bank), slab[:, k, c * P:(c + 1) * P], mx[:, k, :], k == 0, k == KD - 1,
                                    r=[slab, (mx, k // 8)], w=[PSK(bank)])
                        self.tt("dve", xt[:, dk, :], self.ps(bank), xt[:, dk, :], ALU.add, r=[PSK(bank), (xt, dk)], w=[(xt, dk)])
                self.store_xt(self.xT, s, t, xt)
    def phase_ssd(self, l, W):
        cfg, A, G, S = self.cfg, self.arena, self.G, self.S
        PSK = self.PSK
        NS, SQ, NT = cfg.NS, cfg.S, cfg.S // TT
        ident_b, mge_f, mgt_f, ones_f = self.ident_b, self.mge_f, self.mgt_f, self.ones_f
        gn = A.alloc("sgn", [SSD_INNER])
        self.dm(gn[:, :].rearrange("p (a c) -> p a c", a=1), self.I["ssd_norm_g"][l:l + 1, :].partition_broadcast(P), w=[gn])
        xb = A.alloc("sxb", [12, TT + 3])
        acc_t = [A.alloc("sacc%d" % i, [TT]) for i in range(2)]
        cv = A.alloc("scv", [12, TT], BF16)
        xs_tok = A.alloc("sxs", [4, SSD_INNER], BF16)
        B_tok = A.alloc("sBt", [4, 256], BF16)
        sz = A.alloc("ssz", [4, SSD_INNER])
        sm = {}
        for nm in ("dtr", "dtv", "av", "acs", "tot", "E", "Wd", "dec", "dtw"):
            sm[nm] = A.alloc("s" + nm, [4, NHS])
        flat = lambda tl: tl.ap.rearrange("p a b -> p (a b)")
        xdt = A.alloc("sxdt", [4, SSD_INNER], BF16)
        xdtw = A.alloc("sxdtw", [4, SSD_INNER], BF16)
        st_f = A.alloc("sstf", [SSD_INNER])
        st_b = A.alloc("sstb", [SSD_INNER], BF16)
        cbm_t = [A.alloc("scbm%d" % i, [P], BF16) for i in range(2)]
        ra_t = [A.alloc("sra%d" % i, [8, P]) for i in range(2)]
        ed_t = [A.alloc("sed%d" % i, [TT]) for i in range(2)]
        MT_t = [A.alloc("sMT%d" % i, [4, P], BF16) for i in range(4)]
        y_t = [A.alloc("sy%d" % i, [TT]) for i in range(2)]
        xd_t = [A.alloc("sxd%d" % i, [TT]) for i in range(2)]
        junk = A.alloc("sjunk", [TT], BF16)
        ssq = A.alloc("sssq", [8])
        yn_t = [A.alloc("syn%d" % i, [TT], BF16) for i in range(2)]
        mixs = A.alloc("smixs", [8, TT], BF16)
        v3 = lambda ap: ap.rearrange("p (e d) -> p e d", d=HD)
        bc = lambda ap, n: ap[:, :, None].broadcast_to([P, ap.shape[1], n])
        ci = 0
        mi = 0
        for s in range(NS):
            S.pool(lambda e: e.memset(st_f[:, :], 0.0), w=[(st_f, 0), (st_f, 1)])
            S.pool(lambda e: e.memset(st_b[:, :], 0.0), w=[(st_b, 0), (st_b, 1)])
            for t in range(NT):
                tsl = slice(t * TT, (t + 1) * TT)
                if t == 0:
                    S.pool(lambda e: e.memset(xb[:, :, 0:3], 0.0), w=[xb])
                    for h in range(2):
                        self.dm(xb[:, h * 6:(h + 1) * 6, 3:TT + 3], self.xbcT[s, :, h * 6:(h + 1) * 6, 0:TT], w=[xb])
                else:
                    for h in range(2):
                        self.dm(xb[:, h * 6:(h + 1) * 6, :], self.xbcT[s, :, h * 6:(h + 1) * 6, t * TT - 3:(t + 1) * TT], w=[xb])
                self.dm(sz[:, :, :], self.szt[s, tsl, :].rearrange("(tb p) c -> p tb c", p=P), w=[sz])
                self.dm(flat(sm["dtr"]), self.dtt[s, t], w=[sm["dtr"]])
                for c in range(12):
                    acc = acc_t[c % 2]
                    self.ts("dve", acc[:, :], xb[:, c, 0:TT], G["cw"][:, l, 0, c:c + 1], G["cb"][:, l, c:c + 1], ALU.mult, ALU.add,
                            r=[xb, G["cw"], G["cb"]], w=[acc])
                    for j in range(1, 4):
                        self.stt(acc[:, :], xb[:, c, j:j + TT], G["cw"][:, l, j, c:c + 1], acc[:, :], ALU.mult, ALU.add,
                                 r=[xb, acc, G["cw"]], w=[acc])
                    self.actf(cv[:, c, :], acc[:, :], AF.Silu, r=[acc], w=[(cv, c)])
                for tb in range(4):
                    bank = 4 + tb % 2
                    for c in range(8):
                        self.tr(self.psb(bank)[:, c * P:(c + 1) * P], cv[:, c, tb * P:(tb + 1) * P], ident_b[:, :],
                                r=[(cv, c), ident_b], w=[PSK(bank)])
                    self.evac(xs_tok[:, tb, :], self.psb(bank), r=[PSK(bank)], w=[(xs_tok, tb)])
                    for g in range(2):
                        self.tr(self.psb(6)[:, (tb * 2 + g) * P:(tb * 2 + g + 1) * P], cv[:, 8 + g, tb * P:(tb + 1) * P], ident_b[:, :],
                                r=[(cv, 8 + g), ident_b], w=[PSK(6)])
                self.evac(B_tok.ap.rearrange("p a b -> p (a b)"), self.psb(6), r=[PSK(6)], w=[B_tok])
                dtv, av = sm["dtv"], sm["av"]
                self.tt("dve", dtv[:, :, :], sm["dtr"][:, :, :], G["dtb"][:, l:l + 1, :].broadcast_to([P, 4, NHS]), ALU.add,
                        r=[sm["dtr"], G["dtb"]], w=[dtv])
                self.actf(dtv[:, :, :], dtv[:, :, :], AF.Exp, r=[dtv], w=[dtv])
                self.ts("dve", dtv[:, :, :], dtv[:, :, :], 1.0, None, ALU.add, None, r=[dtv], w=[dtv])
                self.actf(dtv[:, :, :], dtv[:, :, :], AF.Ln, r=[dtv], w=[dtv])
                self.tt("dve", av[:, :, :], dtv[:, :, :], G["Aneg"][:, l:l + 1, :].broadcast_to([P, 4, NHS]), ALU.mult,
                        r=[dtv, G["Aneg"]], w=[av])
                self.mm(self.ps(3)[:, 0:64], mge_f[:, :], flat(av), True, True, r=[av, mge_f], w=[PSK(3)])
                self.mm(self.ps(3)[:, 64:128], ones_f[:, :], flat(av), True, True, r=[av, ones_f], w=[PSK(3)])
                self.cp("dve", flat(sm["acs"]), self.ps(3)[:, 0:64], r=[PSK(3)], w=[sm["acs"]])
                self.cp("dve", flat(sm["tot"]), self.ps(3)[:, 64:128], r=[PSK(3)], w=[sm["tot"]])
                self.actf(sm["E"][:, :, :], sm["acs"][:, :, :], AF.Exp, r=[sm["acs"]], w=[sm["E"]])
                self.tt("dve", sm["Wd"][:, :, :], sm["tot"][:, :, :], sm["acs"][:, :, :], ALU.subtract, r=[sm["tot"], sm["acs"]], w=[sm["Wd"]])
                self.actf(sm["Wd"][:, :, :], sm["Wd"][:, :, :], AF.Exp, r=[sm["Wd"]], w=[sm["Wd"]])
                self.actf(sm["dec"][:, :, :], sm["tot"][:, :, :], AF.Exp, r=[sm["tot"]], w=[sm["dec"]])
                self.tt("dve", sm["dtw"][:, :, :], dtv[:, :, :], sm["Wd"][:, :, :], ALU.mult, r=[dtv, sm["Wd"]], w=[sm["dtw"]])
                for tb in range(4):
                    self.tt("pool", v3(xdt[:, tb, :]), v3(xs_tok[:, tb, :]), bc(dtv[:, tb, :], HD), ALU.mult,
                            r=[(xs_tok, tb), dtv], w=[(xdt, tb)])
                    self.tt("pool", v3(xdtw[:, tb, :]), v3(xs_tok[:, tb, :]), bc(sm["dtw"][:, tb, :], HD), ALU.mult,
                            r=[(xs_tok, tb), sm["dtw"]], w=[(xdtw, tb)])
                for tb in range(4):
                    csl = slice(tb * P, (tb + 1) * P)
                    for g in range(2):
                        gs = slice(g * 512, (g + 1) * 512)
                        es_ = slice(g * 8, (g + 1) * 8)
                        cbm, ra, yt, xd, yn = cbm_t[ci % 2], ra_t[ci % 2], y_t[ci % 2], xd_t[ci % 2], yn_t[ci % 2]
                        sc = ssq[:, ci % 8:ci % 8 + 1]
                        sck = (ssq, ci % 8)
                        ci += 1
                        Bt, Ct = cv[:, 8 + g, csl], cv[:, 10 + g, csl]
                        self.mm(self.ps(0)[:, 0:P], Bt, Ct, True, True, r=[(cv, 8 + g), (cv, 10 + g)], w=[PSK(0)])
                        self.tt("dve", cbm[:, :], self.ps(0)[:, 0:P], mge_f[:, :], ALU.mult, r=[PSK(0), mge_f], w=[cbm])
                        self.tt("pool", ra[:, :, :], mge_f[:, None, :].broadcast_to([P, 8, P]), bc(av[:, tb, es_], P), ALU.mult,
                                r=[mge_f, av], w=[ra])
                        MTs = []
                        for hh in range(2):
                            ed = ed_t[hh]
                            MT = MT_t[mi % 4]
                            mi += 1
                            MTs.append(MT)
                            self.mm(self.ps(1 + hh), mgt_f[:, :], ra[:, hh * 4:(hh + 1) * 4, :].rearrange("p a b -> p (a b)"), True, True,
                                    r=[ra, mgt_f], w=[PSK(1 + hh)])
                            self.actf(ed[:, :], self.ps(1 + hh), AF.Exp, r=[PSK(1 + hh)], w=[ed])
                            self.tt("dve", MT[:, :, :], ed[:, :].rearrange("p (a b) -> p a b", a=4),
                                    cbm[:, None, :].broadcast_to([P, 4, P]), ALU.mult, r=[ed, cbm], w=[MT])
                        for e in range(8):
                            col = (g * 8 + e) * HD
                            self.mm(self.ps(4)[:, e * HD:(e + 1) * HD], MTs[e // 4][:, e % 4, :], xdt[:, tb, col:col + HD], True, True,
                                    r=[MTs[e // 4], (xdt, tb)], w=[PSK(4)])
                        self.mm(self.ps(5), Ct, st_b[:, gs], True, True, r=[(cv, 10 + g), (st_b, g)], w=[PSK(5)])
                        self.mm(self.ps(6), B_tok[:, tb, g * P:(g + 1) * P], xdtw[:, tb, gs], True, True,
                                r=[B_tok, (xdtw, tb)], w=[PSK(6)])
                        self.tt("dve", v3(yt[:, :]), v3(self.ps(5)), bc(sm["E"][:, tb, es_], HD), ALU.mult,
                                r=[PSK(5), sm["E"]], w=[yt])
                        self.tt("dve", yt[:, :], self.ps(4), yt[:, :], ALU.add, r=[PSK(4), yt], w=[yt])
                        self.tt("pool", v3(xd[:, :]), v3(xs_tok[:, tb, gs]), bc(G["dsk"][:, l, es_], HD), ALU.mult,
                                r=[(xs_tok, tb), G["dsk"]], w=[xd])
                        self.tt("pool", yt[:, :], yt[:, :], xd[:, :], ALU.add, r=[yt, xd], w=[yt])
                        self.tt("pool", yt[:, :], yt[:, :], sz[:, tb, gs], ALU.mult, r=[yt, sz], w=[yt])
                        self.actf(junk[:, :], yt[:, :], AF.Square, r=[yt], w=[junk, sck], accum=sc)
                        self.actf(sc, sc, AF.Ln, r=[sck, self.eps_t], w=[sck], scale=1.0 / 512, bias=self.eps_t[:, 0:1])
                        self.actf(sc, sc, AF.Exp, r=[sck], w=[sck], scale=-0.5)
                        self.stt(yn[:, :], yt[:, :], sc, gn[:, gs], ALU.mult, ALU.mult, r=[yt, sck, gn], w=[yn])
                        for c4 in range(4):
                            self.tr(self.psb(7)[:, c4 * P:(c4 + 1) * P], yn[:, c4 * P:(c4 + 1) * P], ident_b[:, :],
                                    r=[yn, ident_b], w=[PSK(7)])
                        self.evac(mixs[:, g * 4:(g + 1) * 4, csl], self.psb(7)[:, 0:512].rearrange("p (a b) -> p a b", a=4),
                                  r=[PSK(7)], w=[(mixs, g, tb)])
                        self.tt("pool", v3(st_f[:, gs]), v3(st_f[:, gs]), bc(sm["dec"][:, tb, es_], HD), ALU.mult,
                                r=[(st_f, g), sm["dec"]], w=[(st_f, g)])
                        self.tt("dve", st_f[:, gs], self.ps(6), st_f[:, gs], ALU.add, r=[PSK(6), (st_f, g)], w=[(st_f, g)])
                        self.cp("act", st_b[:, gs], st_f[:, gs], r=[(st_f, g)], w=[(st_b, g)])
                self.dm(self.mixT[s, :, 0:8, tsl], mixs[:, :, :],
                        r=[(mixs, g, tb) for g in range(2) for tb in range(4)], w=[("mixT", s, "ssd", t)])


def run(inputs, cfg, n_cores):
    b = Builder(cfg)
    nc = b.build()
    names = [n for n in b.dram if n in inputs]
    in_maps = []
    for c in range(n_cores):
        m = {}
        for n in names:
            a = inputs[n]
            if n in ("x", "mem", "positions"):
                a = np.ascontiguousarray(a[c * cfg.NS:(c + 1) * cfg.NS])
            m[n] = a
        in_maps.append(m)
    res = run_bass_kernel_spmd(nc, in_maps, core_ids=list(range(n_cores)))
    return np.concatenate([r["out"] for r in res.results], axis=0)


def kernel(**inputs):
    inputs = {k: np.asarray(v) for k, v in inputs.items()}
    cfg = Cfg(NS=2, S=2048, L=4)
    return run(inputs, cfg, 8)
```

```python
import numpy as np
from contextlib import ExitStack
import concourse.bass as bass
import concourse.mybir as mybir
from concourse.bass_utils import run_bass_kernel_spmd

F32 = mybir.dt.float32
BF16 = mybir.dt.bfloat16
I32 = mybir.dt.int32
AF = mybir.ActivationFunctionType
ALU = mybir.AluOpType
P = 128
EPS = 1e-6


def _dsize(dt):
    return 2 if dt == BF16 else 4


class Op:
    __slots__ = ("eng", "fn", "deps", "dma", "seq", "sem", "semval", "signal", "gid", "seg", "dur", "nbytes",
                 "bg", "prev", "rt", "fin", "nd", "users")


class Tile:
    _n = 0

    def __init__(self, ap, name):
        self.ap = ap
        Tile._n += 1
        self.key = (name, Tile._n)

    def __getitem__(self, idx):
        return self.ap[idx]


def _key(r):
    if isinstance(r, Tile):
        return r.key
    if isinstance(r, tuple) and isinstance(r[0], Tile):
        return (r[0].key,) + tuple(r[1:])
    return r


class Sched:
    ENGS = ("pe", "act", "dve", "pool", "sp")
    QSEMS = (("sp", 16), ("act", 4), ("pool", 4))
    HOP = 0.4
    DMA_BW = 120e3
    DMA_LAT = 1.8

    def __init__(self):
        self.all = []
        self.last_w = {}
        self.readers = {}
        self.seg = 0
        import os
        self.reorder = os.environ.get("KREORDER", "1") == "1"

    def add(self, eng, fn, r=(), w=(), dma=False, dur=0.2, nbytes=0, bg=False):
        op = Op()
        op.eng, op.fn, op.dma, op.dur, op.nbytes, op.bg = eng, fn, dma, dur, nbytes, bg
        op.signal = False
        op.seq = op.sem = op.semval = op.prev = None
        deps = set()
        for x in r:
            k = _key(x)
            for lw in self.last_w.get(k, ()):
                deps.add(lw)
        cowrite = set()
        for x in w:
            k = _key(x)
            lws = self.last_w.get(k, ())
            rds = self.readers.get(k, ())
            if dma and lws and not rds and all(o.dma for o in lws):
                cowrite.add(k)
                for lw in lws:
                    deps |= lw.deps
                continue
            for lw in lws:
                deps.add(lw)
            for rd in rds:
                deps.add(rd)
        deps.discard(op)
        op.deps = deps
        for x in r:
            self.readers.setdefault(_key(x), []).append(op)
        for x in w:
            k = _key(x)
            if k in cowrite:
                self.last_w[k] = list(self.last_w[k]) + [op]
            else:
                self.last_w[k] = [op]
            self.readers[k] = []
        op.gid = len(self.all)
        op.seg = self.seg
        self.all.append(op)
        return op

    def barrier(self):
        self.seg += 1
        self.last_w = {}
        self.readers = {}

    def pe(self, fn, r=(), w=(), **kw):
        return self.add("pe", fn, r, w, **kw)

    def act(self, fn, r=(), w=(), **kw):
        return self.add("act", fn, r, w, **kw)

    def dve(self, fn, r=(), w=(), **kw):
        return self.add("dve", fn, r, w, **kw)

    def pool(self, fn, r=(), w=(), **kw):
        return self.add("pool", fn, r, w, **kw)

    def dma(self, fn, r=(), w=(), q="sp", **kw):
        return self.add(q, fn, r, w, dma=True, **kw)

    def _schedule(self, ops):
        import heapq
        order = {e: [] for e in self.ENGS}
        import os
        lo, hi = int(os.environ.get("KR_LO", "0")), int(os.environ.get("KR_HI", "100000"))
        if not self.reorder or not (lo <= ops[0].seg <= hi):
            for op in ops:
                order[op.eng].append(op)
            return order
        for op in ops:
            op.users = []
            op.nd = 0
            op.rt = 0.0
        seg = ops[0].seg
        for op in ops:
            for d in op.deps:
                if d.seg == seg:
                    d.users.append(op)
                    op.nd += 1
        NC = int(os.environ.get("KNC", "4"))
        heap = {e: [] for e in self.ENGS}
        front = {e: [] for e in self.ENGS}

        def push(op):
            ent = (op.gid + (10 ** 9 if op.bg else 0), op.gid, op)
            f = front[op.eng]
            if len(f) < NC:
                f.append(ent)
            else:
                m = max(f)
                if ent < m:
                    f.remove(m)
                    f.append(ent)
                    heapq.heappush(heap[op.eng], m)
                else:
                    heapq.heappush(heap[op.eng], ent)

        for op in ops:
            if op.nd == 0:
                push(op)
        t_eng = {e: 0.0 for e in self.ENGS}
        pipe = 0.0
        remaining = len(ops)
        while remaining:
            best = None
            for e in self.ENGS:
                for ent in front[e]:
                    st = max(t_eng[e], ent[2].rt)
                    key = (st, ent[0])
                    if best is None or key < best[0]:
                        best = (key, e, ent)
            (st, _), e, ent = best
            op = ent[2]
            front[e].remove(ent)
            if heap[e]:
                front[e].append(heapq.heappop(heap[e]))
            if op.dma:
                t_eng[e] = st + 0.06
                p0 = max(st, pipe)
                xfer = op.nbytes / self.DMA_BW
                pipe = p0 + xfer
                op.fin = p0 + xfer + self.DMA_LAT
            else:
                op.fin = st + op.dur
                t_eng[e] = op.fin
            order[e].append(op)
            remaining -= 1
            for u in op.users:
                u.nd -= 1
                hop = 0.0 if (u.eng == "pe" and op.eng == "pe" and not u.dma) else self.HOP
                if op.fin + hop > u.rt:
                    u.rt = op.fin + hop
                if u.nd == 0:
                    push(u)
        return order

    def emit(self, nc, es):
        ENGS = self.ENGS
        nseg = self.seg + 1
        segs = [[] for _ in range(nseg)]
        for op in self.all:
            segs[op.seg].append(op)
        final = {e: [] for e in ENGS}
        segpos = {e: [] for e in ENGS}
        for si, ops in enumerate(segs):
            if not ops:
                continue
            order = self._schedule(ops)
            for e in ENGS:
                if order[e]:
                    segpos[e].append((len(final[e]), si))
                    final[e].extend(order[e])
        qbase = {}
        nd = 0
        for q, n in self.QSEMS:
            qbase[q] = (nd, n)
            nd += n
        dma_total = [0] * nd
        for q, n in self.QSEMS:
            base = qbase[q][0]
            dl = [op for op in final[q] if op.dma]
            for i, op in enumerate(dl):
                op.sem = base + (i % n)
                op.semval = 16 * (i // n + 1)
                op.prev = dl[i - n] if i >= n else None
                dma_total[op.sem] = op.semval
        for op in self.all:
            for d in op.deps:
                if not d.dma and not (d.eng == "pe" and op.eng == "pe" and not op.dma):
                    d.signal = True
        for e in ENGS:
            for (pos, si), nxt in zip(segpos[e], segpos[e][1:] + [(len(final[e]), None)]):
                for op in reversed(final[e][pos:nxt[0]]):
                    if not op.dma:
                        op.signal = True
                        break
        for e in ENGS:
            n = 0
            for op in final[e]:
                if (not op.dma) and op.signal:
                    n += 1
                    op.seq = n
        bar_e = [dict() for _ in range(nseg + 1)]
        bar_d = [dict() for _ in range(nseg + 1)]
        cur_e, cur_d = {}, {}
        byseg_e = [dict() for _ in range(nseg)]
        byseg_d = [dict() for _ in range(nseg)]
        for e in ENGS:
            for op in final[e]:
                if op.dma:
                    if byseg_d[op.seg].get(op.sem, 0) < op.semval:
                        byseg_d[op.seg][op.sem] = op.semval
                elif op.signal:
                    if byseg_e[op.seg].get(e, 0) < op.seq:
                        byseg_e[op.seg][e] = op.seq
        for si in range(nseg):
            bar_e[si] = dict(cur_e)
            bar_d[si] = dict(cur_d)
            for k, v in byseg_e[si].items():
                cur_e[k] = max(cur_e.get(k, 0), v)
            for k, v in byseg_d[si].items():
                cur_d[k] = max(cur_d.get(k, 0), v)
        esem = {e: es.enter_context(nc.semaphore("sem_" + e)) for e in ENGS}
        dsem = [es.enter_context(nc.semaphore("dsem%d" % i)) for i in range(nd)]
        block = es.enter_context(nc.Block())
        sections = {"pe": block.tensor, "act": block.scalar, "dve": block.vector,
                    "pool": block.gpsimd, "sp": block.sync}

        def make(ename):
            def body(eng):
                seen_e = {}
                seen_d = {}

                def wait_e(en, v):
                    if seen_e.get(en, 0) < v:
                        eng.wait_ge(esem[en], v)
                        seen_e[en] = v

                def wait_d(s_, v):
                    if seen_d.get(s_, 0) < v:
                        eng.wait_ge(dsem[s_], v)
                        seen_d[s_] = v

                starts = dict(segpos[ename])
                for i, op in enumerate(final[ename]):
                    if i in starts and starts[i] > 0:
                        si = starts[i]
                        for en, v in bar_e[si].items():
                            wait_e(en, v)
                        for s_, v in bar_d[si].items():
                            wait_d(s_, v)
                    need_e = {}
                    need_d = {}
                    for d in op.deps:
                        if d.seg != op.seg:
                            continue
                        if d.dma:
                            if need_d.get(d.sem, 0) < d.semval:
                                need_d[d.sem] = d.semval
                        else:
                            if d.eng == "pe" and ename == "pe" and not op.dma:
                                continue
                            if need_e.get(d.eng, 0) < d.seq:
                                need_e[d.eng] = d.seq
                    if op.dma and op.prev is not None:
                        if need_d.get(op.prev.sem, 0) < op.prev.semval:
                            need_d[op.prev.sem] = op.prev.semval
                    for en, v in need_e.items():
                        wait_e(en, v)
                    for s_, v in need_d.items():
                        wait_d(s_, v)
                    inst = op.fn(eng)
                    if op.dma:
                        inst.then_inc(dsem[op.sem], 16)
                    elif op.signal:
                        inst.then_inc(esem[ename], 1)
                if ename == "sp":
                    for s_ in range(nd):
                        if dma_total[s_]:
                            eng.wait_ge(dsem[s_], dma_total[s_])
            return body

        for e in ENGS:
            sections[e](make(e))


class Arena:
    def __init__(self, ap, ncol):
        self.ap = ap
        self.ncol = ncol
        self.off = 0
        self.floor = 0

    def reset(self):
        self.off = self.floor

    def freeze(self):
        self.floor = self.off

    def alloc(self, name, free_shape, dt=F32, parts=P):
        n = int(np.prod(free_shape))
        words = (n * _dsize(dt) + 3) // 4
        assert self.off + words <= self.ncol, ("SBUF arena overflow", name, self.off, words, self.ncol)
        ap = self.ap[0:parts, self.off:self.off + words]
        self.off += words
        if dt != F32:
            ap = ap.bitcast(dt)[:, 0:n]
        if len(free_shape) == 2:
            ap = ap.rearrange("p (a b) -> p a b", a=free_shape[0])
        elif len(free_shape) == 3:
            ap = ap.rearrange("p (a b c) -> p a b c", a=free_shape[0], b=free_shape[1])
        return Tile(ap, name)


D = 2048
KD = 16
SSD_INNER = 1024
NHS = 16
HD = 64
NST = 128
CONV_DIM = 1536
MLA_H = 8
Q_LORA = 512
KV_LORA = 512
XH = 4
MEM = 256
FF = 5632
KF = 44
IN_COLS = 3664
C0, C1, C2, C3, C4 = 1024, 2560, 2576, 3088, 3600
TT = 512


class Cfg:
    def __init__(self, NS=2, S=2048, L=4, phases=("mix", "xattn", "ffn")):
        self.NS, self.S, self.L = NS, S, L
        self.phases = tuple(phases)


PI = float(np.pi)


class Builder:
    def __init__(self, cfg):
        self.cfg = cfg
        self.nc = bass.Bass("TRN2", target_bir_lowering=False)
        self.S = Sched()
        self.dram = {}
        self.ev_i = 0

    def din(self, name, shape, dt=F32):
        t = self.nc.dram_tensor(name, list(shape), dt, kind="ExternalInput").ap()
        self.dram[name] = t
        return t

    def dscr(self, name, shape, dt=F32):
        return self.nc.dram_tensor(name, list(shape), dt, kind="Internal").ap()

    def ps(self, b):
        return self.psum[:, b, :]

    def psb(self, b):
        return self.psum[:, b, :].bitcast(BF16)

    PSK = staticmethod(lambda b: ("psum", b))

    @staticmethod
    def _fs(ap):
        n = 1
        for d in ap.shape[1:]:
            n *= d
        return n

    def mm(self, out, lhsT, rhs, st, sp, r, w):
        dur = self._fs(out) * (4.0 if lhsT.dtype == F32 else 1.0) / 2400.0 + 0.02
        self.S.pe(lambda e: e.matmul(out, lhsT=lhsT, rhs=rhs, start=st, stop=sp), r=r, w=w, dur=dur)

    def tr(self, out, in_, ident, r, w):
        self.S.pe(lambda e: e.transpose(out=out, in_=in_, identity=ident), r=r, w=w,
                  dur=0.12 if in_.dtype == F32 else 0.07)

    def actf(self, out, in_, func, r, w, scale=None, bias=None, accum=None):
        kw = {}
        if scale is not None:
            kw["scale"] = scale
        if bias is not None:
            kw["bias"] = bias
        if accum is not None:
            kw["accum_out"] = accum
        self.S.act(lambda e: e.activation(out=out, in_=in_, func=func, **kw), r=r, w=w, dur=self._fs(out) / 960.0 + 0.22)

    def _edur(self, eng, out):
        return self._fs(out) / (480.0 if eng == "pool" else 960.0) + 0.2

    def tt(self, eng, out, in0, in1, op, r, w):
        self.S.add(eng, lambda e: e.tensor_tensor(out=out, in0=in0, in1=in1, op=op), r=r, w=w, dur=self._edur(eng, out))

    def ts(self, eng, out, in0, s1, s2, op0, op1, r, w):
        if op1 is None:
            self.S.add(eng, lambda e: e.tensor_scalar(out=out, in0=in0, scalar1=s1, scalar2=None, op0=op0), r=r, w=w,
                       dur=self._edur(eng, out))
        else:
            self.S.add(eng, lambda e: e.tensor_scalar(out=out, in0=in0, scalar1=s1, scalar2=s2, op0=op0, op1=op1), r=r, w=w,
                       dur=self._edur(eng, out))

    def stt(self, out, in0, scalar, in1, op0, op1, r, w):
        self.S.dve(lambda e: e.scalar_tensor_tensor(out=out, in0=in0, scalar=scalar, in1=in1, op0=op0, op1=op1), r=r, w=w,
                   dur=self._fs(out) / 960.0 + 0.2)

    def cp(self, eng, out, in_, r, w, bg=False):
        if eng == "act":
            self.S.act(lambda e: e.copy(out=out, in_=in_), r=r, w=w, dur=self._fs(out) / 960.0 + 0.22, bg=bg)
        else:
            self.S.add(eng, lambda e: e.tensor_copy(out=out, in_=in_), r=r, w=w, dur=self._edur(eng, out), bg=bg)

    def evac(self, out, in_, r, w):
        self.ev_i += 1
        self.cp("act" if self.ev_i % 2 else "dve", out, in_, r, w)

    def dm(self, out, in_, r=(), w=(), q="sp", bg=False):
        nb = 1
        for d in out.shape:
            nb *= d
        nb *= _dsize(out.dtype)
        self.S.dma(lambda e: e.dma_start(out=out, in_=in_), r=r, w=w, q=q, nbytes=nb, bg=bg)

    def recip(self, out, in_, r, w):
        self.S.dve(lambda e: e.reciprocal(out=out, in_=in_), r=r, w=w, dur=self._fs(out) / 960.0 + 0.2)

    def convert(self, src, K, C, CW, name):
        S = self.S
        ncb = C // CW
        dst = self.dscr(name, [ncb, P, K, CW], BF16)
        pieces = src if isinstance(src, list) else [(src, lambda st: st[:, 0:C])]
        for k in range(K):
            b = self.cv_i % 3
            self.cv_i += 1
            st, bt = self.cv_f32[b], self.cv_bf[b]
            for (ap, dfn) in pieces:
                self.dm(dfn(st), ap[k * P:(k + 1) * P], r=[], w=[st])
            eng = ("act", "dve", "pool")[self.cv_i % 3]
            self.cp(eng, bt[:, 0:C], st[:, 0:C], r=[st], w=[bt])
            d = dst[:, :, k, :].rearrange("cb p c -> p cb c")
            self.cv_pending.append((d, bt, C, CW, name))
            if len(self.cv_pending) > 2:
                self.cv_flush(1)
        return dst

    def cv_flush(self, n):
        for _ in range(n):
            if not self.cv_pending:
                return
            d, bt, C, CW, name = self.cv_pending.pop(0)
            self.dm(d, bt[:, 0:C].rearrange("p (cb c) -> p cb c", c=CW), r=[bt], w=[("dram", name)])

    def convert_jobs(self, src, K, C, CW, name):
        assert C % 512 == 0
        dst = self.dscr(name, [C // CW, P, K, CW], BF16)
        jobs = []
        for k in range(K):
            for j in range(C // 512):
                jobs.append((src, dst, k, j, CW, name))
        return dst, jobs

    def run_job(self, job, bg=True):
        src, dst, k, j, CW, name = job
        i = self.bg_i
        self.bg_i += 1
        st, bt = self.bg_f32[i % 4], self.bg_bf[i % 4]
        self.dm(st[:, :], src[k * P:(k + 1) * P, j * 512:(j + 1) * 512], w=[st], bg=bg)
        self.cp(self.bg_eng, bt[:, :], st[:, :], r=[st], w=[bt], bg=bg)
        if CW >= 512:
            c0 = j * 512
            d = dst[c0 // CW, :, k, c0 % CW:c0 % CW + 512]
            self.dm(d, bt[:, :], r=[bt], w=[("dram", name)], bg=bg)
        else:
            n = 512 // CW
            d = dst[j * n:(j + 1) * n, :, k, :].rearrange("cb p c -> p cb c")
            self.dm(d, bt[:, :].rearrange("p (cb c) -> p cb c", c=CW), r=[bt], w=[("dram", name)], bg=bg)

    def bg_take(self, key, n=None, bg=True):
        import os
        lst = self.bg_jobs.get(key)
        if not lst or (bg and os.environ.get("KBG", "1") == "0"):
            return
        if n is None or n > len(lst):
            n = len(lst)
        for job in lst[:n]:
            self.run_job(job, bg=bg)
        del lst[:n]

    def bg_need(self, key):
        if self.bg_jobs.get(key):
            self.bg_take(key, None, bg=False)
            self.S.barrier()

    def gain_T(self, dst2d, src_rows, rows, parts):
        stg = self.gstg[self.g_i % 2]
        bank = self.g_i % 4
        self.g_i += 1
        srcs = src_rows if isinstance(src_rows, list) else [(src_rows, 0, parts)]
        for ap, c0, w in srcs:
            self.dm(stg[0:rows, c0:c0 + w], ap, w=[stg])
        self.tr(self.ps(bank)[0:parts, 0:rows], stg[0:rows, 0:parts], self.ident_f[0:rows, 0:rows],
                r=[stg, self.ident_f], w=[self.PSK(bank)])
        self.cp("dve", dst2d, self.ps(bank)[0:parts, 0:rows], r=[self.PSK(bank)], w=[("gain", self.g_i)])

    def gain_fm(self, name, src, L, nk, parts=P, lo=0):
        t = self.arena.alloc(name, [L, nk], parts=parts)
        self.gain_T(t.ap.rearrange("p l k -> p (l k)"),
                    src[:, lo:lo + nk * parts].rearrange("l (k p) -> (l k) p", p=parts), L * nk, parts)
        return t

    def gain_rep(self, name, src, L, n):
        t = self.arena.alloc(name, [L, n])
        self.dm(t.ap.rearrange("p l n -> p (l n)")[:, None, :],
                src.rearrange("(a l) n -> a (l n)", a=1).partition_broadcast(P), w=[t])
        return t

    def rms_stats(self, chunks, ncols, scale, bank, rstd, sq_t):
        n = len(chunks)
        for i, (ap, key, parts) in enumerate(chunks):
            sq = sq_t[i % 2]
            self.actf(sq[0:parts, 0:ncols], ap, AF.Square, r=[key], w=[sq])
            self.mm(self.ps(bank)[:, 0:ncols], self.ones_b[0:parts, :], sq[0:parts, 0:ncols], i == 0, i == n - 1,
                    r=[sq, self.ones_b], w=[self.PSK(bank)])
        self.actf(rstd[:, 0:ncols], self.ps(bank)[:, 0:ncols], AF.Ln, r=[self.PSK(bank), self.eps_t], w=[rstd],
                  scale=scale, bias=self.eps_t[:, 0:1])
        self.actf(rstd[:, 0:ncols], rstd[:, 0:ncols], AF.Exp, r=[rstd], w=[rstd], scale=-0.5)

    def load_xt(self, xT, s, t, xt):
        tsl = slice(t * TT, (t + 1) * TT)
        for h in range(2):
            self.dm(xt[:, h * 8:(h + 1) * 8, :], xT[s, :, h * 8:(h + 1) * 8, tsl],
                    r=[("xT", s, t, h)], w=[(xt, k) for k in range(h * 8, h * 8 + 8)])

    def store_xt(self, xT, s, t, xt):
        tsl = slice(t * TT, (t + 1) * TT)
        for h in range(2):
            self.dm(xT[s, :, h * 8:(h + 1) * 8, tsl], xt[:, h * 8:(h + 1) * 8, :],
                    r=[(xt, k) for k in range(h * 8, h * 8 + 8)], w=[("xT", s, t, h)])

    def norm_h(self, xt, hT, g, l, rstd, sq_t, bank=0):
        self.rms_stats([(xt[:, k, :], (xt, k), P) for k in range(KD)], TT, 1.0 / D, bank, rstd, sq_t)
        for k in range(KD):
            self.stt(hT[:, k, :], xt[:, k, :], g[:, l, k:k + 1], rstd[:, :], ALU.mult, ALU.mult,
                     r=[(xt, k), rstd, g], w=[(hT, k)])
    def build(self):
        cfg, nc, S = self.cfg, self.nc, self.S
        NS, SQ, L = cfg.NS, cfg.S, cfg.L
        NT = SQ // TT
        ph = cfg.phases
        es = ExitStack()
        self.es = es
        es.enter_context(nc.allow_non_contiguous_dma(reason="tiny per-layer vectors / small tiles"))
        I = {}
        I["x"] = self.din("x", [NS, SQ, D])
        if "mix" in ph:
            I["positions"] = self.din("positions", [NS, SQ], I32)
            for n, sh in (("attn_norm_g", [L, D]), ("w_in", [L, D, IN_COLS]), ("conv_w", [L, 4, CONV_DIM]),
                          ("conv_b", [L, CONV_DIM]), ("dt_bias", [L, NHS]), ("a_log", [L, NHS]), ("d_skip", [L, NHS]),
                          ("ssd_norm_g", [L, SSD_INNER]), ("q_a_norm_g", [L, Q_LORA]), ("w_q_b", [L, Q_LORA, MLA_H * 192]),
                          ("kv_a_norm_g", [L, KV_LORA]), ("w_kv_b", [L, KV_LORA, MLA_H * 256]), ("mla_q_norm_g", [L, 192]),
                          ("mla_k_norm_g", [L, 192]), ("w_out", [L, D, D])):
                I[n] = self.din(n, sh)
        if "xattn" in ph:
            I["mem"] = self.din("mem", [NS, MEM, D])
            for n, sh in (("xattn_norm_g", [L, D]), ("mem_norm_g", [L, D]), ("w_xq", [L, D, 512]), ("w_xk", [L, D, 512]),
                          ("w_xv", [L, D, 512]), ("xq_norm_g", [L, 128]), ("xk_norm_g", [L, 128]), ("w_xo", [L, 512, D])):
                I[n] = self.din(n, sh)
        if "ffn" in ph:
            for n, sh in (("ffn_norm_g", [L, D]), ("w_gate", [L, D, FF]), ("w_up", [L, D, FF]), ("w_down", [L, FF, D])):
                I[n] = self.din(n, sh)
        out = nc.dram_tensor("out", [NS, SQ, D], F32, kind="ExternalOutput").ap()
        self.I = I
        xT = self.dscr("xT", [NS, P, KD, SQ])
        self.xT = xT
        ACOLS = 53000
        arena_t = es.enter_context(nc.sbuf_tensor("arena", [P, ACOLS], F32))
        self.arena = A = Arena(arena_t[:, :], ACOLS)
        self.psum = es.enter_context(nc.psum_tensor("psum", [P, 8, 512], F32))
        PSK = self.PSK

        ones_f = A.alloc("ones_f", [P])
        ident_f = A.alloc("ident_f", [P])
        mge_f = A.alloc("mge_f", [P])
        mgt_f = A.alloc("mgt_f", [P])
        ident_b = A.alloc("ident_b", [P], BF16)
        ones_b = A.alloc("ones_b", [P], BF16)
        mge_b = A.alloc("mge_b", [P], BF16)
        S.pool(lambda e: e.memset(ones_f[:, :], 1.0), w=[ones_f])
        S.pool(lambda e: e.affine_select(out=ident_f[:, :], in_=ones_f[:, :], pattern=[[1, P]],
                                         compare_op=ALU.is_equal, fill=0.0, base=0, channel_multiplier=-1),
               r=[ones_f], w=[ident_f])
        S.pool(lambda e: e.affine_select(out=mge_f[:, :], in_=ones_f[:, :], pattern=[[1, P]],
                                         compare_op=ALU.is_ge, fill=0.0, base=0, channel_multiplier=-1),
               r=[ones_f], w=[mge_f])
        S.pool(lambda e: e.affine_select(out=mgt_f[:, :], in_=ones_f[:, :], pattern=[[-1, P]],
                                         compare_op=ALU.is_ge, fill=0.0, base=-1, channel_multiplier=1),
               r=[ones_f], w=[mgt_f])
        self.cp("dve", ident_b[:, :], ident_f[:, :], r=[ident_f], w=[ident_b])
        self.cp("dve", ones_b[:, :], ones_f[:, :], r=[ones_f], w=[ones_b])
        self.cp("dve", mge_b[:, :], mge_f[:, :], r=[mge_f], w=[mge_b])
        self.ones_b, self.ident_f, self.ident_b, self.ones_f = ones_b, ident_f, ident_b, ones_f
        self.eps_t = A.alloc("eps_t", [1])
        S.pool(lambda e: e.memset(self.eps_t[:, :], EPS), w=[self.eps_t])
        self.mge_f, self.mgt_f, self.mge_b = mge_f, mgt_f, mge_b
        G = {}
        self.gstg = [A.alloc("gstg%d" % i, [P]) for i in range(2)]
        self.g_i = 0
        if "mix" in ph:
            G["attn"] = self.gain_fm("g_attn", I["attn_norm_g"], L, KD)
            cw = A.alloc("cw", [L, 4, 12])
            for l in range(L):
                self.gain_T(cw[:, l, :, :].rearrange("p j c -> p (j c)"),
                            I["conv_w"][l].rearrange("j (c p) -> (j c) p", p=P), 48, P)
            G["cw"] = cw
            G["cb"] = self.gain_fm("cb", I["conv_b"], L, 12)
            G["dtb"] = self.gain_rep("dtb", I["dt_bias"], L, NHS)
            G["dsk"] = self.gain_rep("dsk", I["d_skip"], L, NHS)
            alog = self.gain_rep("alog", I["a_log"], L, NHS)
            Aneg = A.alloc("Aneg", [L, NHS])
            self.actf(Aneg[:, :, :], alog[:, :, :], AF.Exp, r=[alog], w=[Aneg])
            self.ts("dve", Aneg[:, :, :], Aneg[:, :, :], -1.0, None, ALU.mult, None, r=[Aneg], w=[Aneg])
            G["Aneg"] = Aneg
            G["qa"] = self.gain_fm("g_qa", I["q_a_norm_g"], L, 4)
            G["kva"] = self.gain_fm("g_kva", I["kv_a_norm_g"], L, 4)
            for nm, src in (("q", I["mla_q_norm_g"]), ("k", I["mla_k_norm_g"])):
                G[nm + "n"] = self.gain_fm("g_%sn" % nm, src, L, 1)
                G[nm + "r"] = self.gain_fm("g_%sr" % nm, src, L, 1, parts=64, lo=128)
                sw = A.alloc("g_%srs" % nm, [L, 1], parts=64)
                self.gain_T(sw.ap.rearrange("p l k -> p (l k)"), [(src[:, 160:192], 0, 32), (src[:, 128:160], 32, 32)], L, 64)
                G[nm + "rs"] = sw
        if "xattn" in ph:
            G["xattn"] = self.gain_fm("g_xattn", I["xattn_norm_g"], L, KD)
            G["mem"] = self.gain_fm("g_mem", I["mem_norm_g"], L, KD)
            G["xq"] = self.gain_fm("g_xq", I["xq_norm_g"], L, 1)
            G["xk"] = self.gain_fm("g_xk", I["xk_norm_g"], L, 1)
        if "ffn" in ph:
            G["ffn"] = self.gain_fm("g_ffn", I["ffn_norm_g"], L, KD)
        self.G = G
        self.bg_f32 = [A.alloc("bgf%d" % i, [512]) for i in range(4)]
        self.bg_bf = [A.alloc("bgb%d" % i, [512], BF16) for i in range(4)]
        self.bg_i = 0
        self.bg_jobs = {}
        A.freeze()

        self.cv_i = 0
        self.cv_pending = []
        self.cv_f32 = [A.alloc("cvf%d" % i, [FF]) for i in range(3)]
        self.cv_bf = [A.alloc("cvb%d" % i, [FF], BF16) for i in range(3)]
        W = [dict() for _ in range(L)]

        def plain(l, key, grp, src, K, C, CW):
            nm = "W%s%d" % (key, l)
            if l == 0 and grp == "mx":
                W[l][key] = self.convert(src, K, C, CW, nm)
            else:
                W[l][key], jobs = self.convert_jobs(src, K, C, CW, nm)
                self.bg_jobs.setdefault((grp, l), []).extend(jobs)

        for l in range(L):
            if "mix" in ph:
                wi = I["w_in"][l]
                plain(l, "z", "mx", wi[:, 0:C0], KD, 1024, 512)
                plain(l, "xbc", "mx", wi[:, C0:C1], KD, 1536, 512)
                W[l]["sm"] = self.convert([(wi[:, C1:C2], lambda st: st[:, 0:16]),
                                           (wi[:, C4:C4 + 64], lambda st: st[:, 16:80]),
                                           (wi[:, C4 + 32:C4 + 64], lambda st: st[:, 80:112]),
                                           (wi[:, C4:C4 + 32], lambda st: st[:, 112:144])], KD, 144, 144, "Wsm%d" % l)
                plain(l, "qa", "mx", wi[:, C2:C3], KD, 512, 512)
                plain(l, "kva", "mx", wi[:, C3:C4], KD, 512, 512)
                wq3 = I["w_q_b"][l].rearrange("r (h c) -> r h c", c=192)
                v8 = lambda st, n, c: st[:, 0:n].rearrange("p (h c) -> p h c", c=c)
                W[l]["qbn"] = self.convert([(wq3[:, :, 0:128], lambda st: v8(st, 1024, 128))], 4, 1024, 1024, "Wqbn%d" % l)
                W[l]["qbr"] = self.convert([(wq3[:, :, 128:192], lambda st: v8(st, 512, 64))], 4, 512, 512, "Wqbr%d" % l)
                W[l]["qbrs"] = self.convert([(wq3[:, :, 160:192], lambda st: v8(st, 512, 64)[:, :, 0:32]),
                                             (wq3[:, :, 128:160], lambda st: v8(st, 512, 64)[:, :, 32:64])],
                                            4, 512, 512, "Wqbrs%d" % l)
                wk3 = I["w_kv_b"][l].rearrange("r (h c) -> r h c", c=256)
                W[l]["kvn"] = self.convert([(wk3[:, :, 0:128], lambda st: v8(st, 1024, 128))], 4, 1024, 1024, "Wkvn%d" % l)
                W[l]["kvv"] = self.convert([(wk3[:, :, 128:256], lambda st: v8(st, 1024, 128))], 4, 1024, 1024, "Wkvv%d" % l)
                plain(l, "out", "mx", I["w_out"][l], KD, D, 512)
            if "xattn" in ph:
                plain(l, "xq", "mx", I["w_xq"][l], KD, 512, 512)
                plain(l, "xk", "mx", I["w_xk"][l], KD, 512, 512)
                plain(l, "xv", "mx", I["w_xv"][l], KD, 512, 512)
                plain(l, "xo", "mx", I["w_xo"][l], 4, D, D)
            if "ffn" in ph:
                plain(l, "g", "ffn", I["w_gate"][l], KD, FF, 256)
                plain(l, "u", "ffn", I["w_up"][l], KD, FF, 256)
                plain(l, "d", "ffn", I["w_down"][l], KF, D, 128)
        self.cv_flush(100)
        S.barrier()
        A.reset()

        xin_t = [A.alloc("xin%d" % i, [4, D]) for i in range(2)]
        xo_t = [A.alloc("xo%d" % i, [KD, TT]) for i in range(2)]
        it = 0
        for s in range(NS):
            for t in range(NT):
                b = it % 2
                it += 1
                xi, xo = xin_t[b], xo_t[b]
                for tb in range(4):
                    self.dm(xi[:, tb, :], I["x"][s, t * TT + tb * P:t * TT + (tb + 1) * P, :], w=[(xi, tb)])
                for k in range(KD):
                    bank = k % 4
                    for tb in range(4):
                        self.tr(self.ps(bank)[:, tb * P:(tb + 1) * P], xi[:, tb, k * P:(k + 1) * P], ident_f[:, :],
                                r=[(xi, tb), ident_f], w=[PSK(bank)])
                    self.evac(xo[:, k, :], self.ps(bank), r=[PSK(bank)], w=[(xo, k)])
                self.store_xt(xT, s, t, xo)
        S.barrier()
        A.reset()

        if "mix" in ph:
            self.szt = self.dscr("szt", [NS, SQ, SSD_INNER])
            self.xbcT = self.dscr("xbcT", [NS, P, 12, SQ])
            self.dtt = self.dscr("dtt", [NS, SQ // TT, P, 4 * NHS])
            self.qnT = self.dscr("qnT", [NS, MLA_H, P, SQ], BF16)
            self.qrT = self.dscr("qrT", [NS, MLA_H, 64, SQ], BF16)
            self.knT = self.dscr("knT", [NS, MLA_H, P, SQ], BF16)
            self.krT = self.dscr("krT", [NS, MLA_H, 64, SQ], BF16)
            self.vt = self.dscr("vt", [NS, SQ, MLA_H * P], BF16)
            self.mixT = self.dscr("mixT", [NS, P, KD, SQ], BF16)
            self.phase_rope()
            S.barrier()
            A.reset()
        if "xattn" in ph:
            self.phase_mem()
            S.barrier()
            A.reset()

        import os
        KSSD = float(os.environ.get("KSSDHOST", "0"))
        HOST = {"assd": (0.15, 0.15, "pool"), "amla": (0.30, 0.30, "pool"), "ssd": (KSSD, KSSD, "act"),
                "attn": (0.18, 0.18, "dve"), "oproj": (0.10, 0.10, "pool"), "xattn": (0.27, 0.27, "pool")}
        self.bg_eng = "pool"
        for l in range(L):
            nffn = len(self.bg_jobs.get(("ffn", l), ()))
            nmx = len(self.bg_jobs.get(("mx", l + 1), ()))

            def host(nm):
                import os
                f1, f2, eng = HOST[nm]
                self.bg_eng = os.environ.get("KENG", eng)
                if f1 > 0:
                    self.bg_take(("ffn", l), int(nffn * f1) + 1)
                    self.bg_take(("mx", l + 1), int(nmx * f2) + 1)
                self.bg_eng = "pool"

            if "mix" in ph:
                self.bg_need(("mx", l))
                for f in (self.phase_assd, self.phase_amla, self.phase_ssd, self.phase_attn, self.phase_oproj):
                    nm = f.__name__[6:]
                    if any(p.startswith("only_") for p in ph) and ("only_" + nm) not in ph:
                        continue
                    host(nm)
                    f(l, W[l])
                    S.barrier()
                    A.reset()
            if "xattn" in ph and "noxattn" not in ph:
                self.bg_need(("mx", l))
                self.host_fn = lambda: host("xattn")
                self.phase_xattn(l, W[l])
                S.barrier()
                A.reset()
            if "ffn" in ph:
                self.bg_need(("ffn", l))
                if "mix" not in ph and l + 1 < L:
                    self.bg_take(("mx", l + 1))
                self.phase_ffn(l, W[l])
                S.barrier()
                A.reset()

        xi_t = [A.alloc("xfi%d" % i, [KD, TT]) for i in range(2)]
        ot_t = [A.alloc("xfo%d" % i, [4, D]) for i in range(2)]
        it = 0
        for s in range(NS):
            for t in range(NT):
                b = it % 2
                it += 1
                xi, ot = xi_t[b], ot_t[b]
                self.load_xt(xT, s, t, xi)
                n = 0
                for tb in range(4):
                    for kg in range(4):
                        bank = n % 4
                        n += 1
                        for kk in range(4):
                            k = kg * 4 + kk
                            self.tr(self.ps(bank)[:, kk * P:(kk + 1) * P], xi[:, k, tb * P:(tb + 1) * P], ident_f[:, :],
                                    r=[(xi, k), ident_f], w=[PSK(bank)])
                        self.evac(ot[:, tb, kg * 512:(kg + 1) * 512], self.ps(bank), r=[PSK(bank)], w=[(ot, tb)])
                for tb in range(4):
                    self.dm(out[s, t * TT + tb * P:t * TT + (tb + 1) * P, :], ot[:, tb, :], r=[(ot, tb)], w=[("out", s, t, tb)])
        S.emit(nc, es)
        es.close()
        return nc

    def phase_ffn(self, l, W):
        cfg, S, A = self.cfg, self.S, self.arena
        PSK = self.PSK
        NS, NT = cfg.NS, cfg.S // TT
        xT = self.xT
        Wg, Wu, Wd = W["g"], W["u"], W["d"]
        xt_t = [A.alloc("fx%d" % i, [KD, TT]) for i in range(2)]
        sq_t = [A.alloc("fsq%d" % i, [TT], BF16) for i in range(2)]
        rstd = A.alloc("frstd", [TT])
        hT = A.alloc("fh", [KD, TT], BF16)
        actT = A.alloc("fact", [KF, TT], BF16)
        wg_t = [A.alloc("fwg%d" % i, [KD, 256], BF16) for i in range(2)]
        wu_t = [A.alloc("fwu%d" % i, [KD, 256], BF16) for i in range(2)]
        wd_t = [A.alloc("fwd%d" % i, [KF, 128], BF16) for i in range(2)]
        sg_t = [A.alloc("fsg%d" % i, [TT]) for i in range(2)]
        it = 0
        wi = 0
        di = 0
        for s in range(NS):
            for t in range(NT):
                xt = xt_t[it % 2]
                it += 1
                self.load_xt(xT, s, t, xt)
                self.norm_h(xt, hT, self.G["ffn"], l, rstd, sq_t)
                for cg in range(FF // 256):
                    wg, wu = wg_t[wi % 2], wu_t[wi % 2]
                    wi += 1
                    self.dm(wg[:, :, :], Wg[cg], w=[wg])
                    self.dm(wu[:, :, :], Wu[cg], w=[wu])
                    for c in range(2):
                        j = cg * 2 + c
                        bg, bu = 1 + (j % 2), 3 + (j % 2)
                        for k in range(KD):
                            self.mm(self.ps(bg), wg[:, k, c * P:(c + 1) * P], hT[:, k, :], k == 0, k == KD - 1,
                                    r=[wg, (hT, k)], w=[PSK(bg)])
                        for k in range(KD):
                            self.mm(self.ps(bu), wu[:, k, c * P:(c + 1) * P], hT[:, k, :], k == 0, k == KD - 1,
                                    r=[wu, (hT, k)], w=[PSK(bu)])
                        sg = sg_t[j % 2]
                        self.actf(sg[:, :], self.ps(bg), AF.Silu, r=[PSK(bg)], w=[sg])
                        self.tt("dve", actT[:, j, :], sg[:, :], self.ps(bu), ALU.mult, r=[sg, PSK(bu)], w=[(actT, j)])
                for dk in range(KD):
                    wd = wd_t[di % 2]
                    bo = 5 + (di % 2)
                    di += 1
                    for h in range(2):
                        self.dm(wd[:, h * 22:(h + 1) * 22, :], Wd[dk][:, h * 22:(h + 1) * 22, :], w=[(wd, h)])
                    for j in range(KF):
                        self.mm(self.ps(bo), wd[:, j, :], actT[:, j, :], j == 0, j == KF - 1,
                                r=[(wd, j // 22), (actT, j)], w=[PSK(bo)])
                    self.tt("dve", xt[:, dk, :], self.ps(bo), xt[:, dk, :], ALU.add, r=[PSK(bo), (xt, dk)], w=[(xt, dk)])
                self.store_xt(xT, s, t, xt)
    def phase_mem(self):
        cfg, A = self.cfg, self.arena
        PSK = self.PSK
        NS = cfg.NS
        mem = self.I["mem"]
        self.memT = self.dscr("memT", [NS, P, KD, MEM], BF16)
        mt_t = [A.alloc("mt%d" % i, [D]) for i in range(2)]
        junk = A.alloc("mjunk", [D], BF16)
        ss = A.alloc("mss", [2])
        mh_t = [A.alloc("mh%d" % i, [D], BF16) for i in range(2)]
        mo_t = [A.alloc("mo%d" % i, [KD, P], BF16) for i in range(2)]
        it = 0
        for s in range(NS):
            for mb in range(MEM // P):
                b = it % 2
                it += 1
                mt, mh, mo = mt_t[b], mh_t[b], mo_t[b]
                sc = ss[:, b:b + 1]
                self.dm(mt[:, :], mem[s, mb * P:(mb + 1) * P, :], w=[mt])
                self.actf(junk[:, :], mt[:, :], AF.Square, r=[mt], w=[junk, (ss, b)], accum=sc)
                self.ts("dve", sc, sc, 1.0 / D, EPS, ALU.mult, ALU.add, r=[(ss, b)], w=[(ss, b)])
                self.actf(sc, sc, AF.Sqrt, r=[(ss, b)], w=[(ss, b)])
                self.recip(sc, sc, r=[(ss, b)], w=[(ss, b)])
                self.ts("dve", mh[:, :], mt[:, :], sc, None, ALU.mult, None, r=[mt, (ss, b)], w=[mh])
                for k in range(KD):
                    bank = 2 * b + k // 8
                    self.tr(self.psb(bank)[:, (k % 8) * P:(k % 8 + 1) * P], mh[:, k * P:(k + 1) * P], self.ident_b[:, :],
                            r=[mh, self.ident_b], w=[PSK(bank)])
                for hf in range(2):
                    bank = 2 * b + hf
                    self.evac(mo[:, hf * 8:(hf + 1) * 8, :], self.psb(bank).rearrange("p (a b) -> p a b", a=8),
                              r=[PSK(bank)], w=[(mo, hf)])
                self.dm(self.memT[s, :, :, mb * P:(mb + 1) * P], mo[:, :, :], r=[(mo, 0), (mo, 1)], w=[("memT", s, mb)])

    def phase_xattn(self, l, W):
        cfg, A, G = self.cfg, self.arena, self.G
        PSK = self.PSK
        NS, NT = cfg.NS, cfg.S // TT
        xT = self.xT
        kT = A.alloc("kT", [NS, XH, MEM], BF16)
        V = A.alloc("V", [NS, 2, 512], BF16)
        sq_t = [A.alloc("xsq%d" % i, [TT], BF16) for i in range(2)]
        rk_t = [A.alloc("xrk%d" % i, [TT]) for i in range(2)]
        mark = A.off
        Wxk = A.alloc("Wxk", [KD, 512], BF16)
        Wxv = A.alloc("Wxv", [KD, 512], BF16)
        self.dm(Wxk[:, :, :], W["xk"][0], w=[Wxk])
        self.dm(Wxv[:, :, :], W["xv"][0], w=[Wxv])
        mT_t = [A.alloc("mT%d" % i, [KD, MEM], BF16) for i in range(2)]
        n = 0
        for s in range(NS):
            mT = mT_t[s % 2]
            self.dm(mT[:, :, :], self.memT[s], r=[("memT", s)], w=[mT])
            for k in range(KD):
                self.ts("dve" if k % 2 else "pool", mT[:, k, :], mT[:, k, :], G["mem"][:, l, k:k + 1], None, ALU.mult, None,
                        r=[mT, G["mem"]], w=[(mT, k)])
            for h in range(XH):
                bq, bs_ = 1 + n % 2, 3 + n % 2
                rk = rk_t[n % 2]
                sqs = [sq_t[n % 2], sq_t[(n + 1) % 2]]
                n += 1
                for k in range(KD):
                    self.mm(self.ps(bq)[:, 0:MEM], Wxk[:, k, h * P:(h + 1) * P], mT[:, k, :], k == 0, k == KD - 1,
                            r=[Wxk, (mT, k), mT], w=[PSK(bq)])
                self.rms_stats([(self.ps(bq)[:, 0:MEM], PSK(bq), P)], MEM, 1.0 / 128, bs_, rk, sqs)
                self.stt(kT[:, s, h, :], self.ps(bq)[:, 0:MEM], G["xk"][:, l, 0:1], rk[:, 0:MEM], ALU.mult, ALU.mult,
                         r=[PSK(bq), rk, G["xk"]], w=[(kT, s, h)])
            for mb in range(2):
                bv = 5 + mb
                for k in range(KD):
                    self.mm(self.ps(bv), mT[:, k, mb * P:(mb + 1) * P], Wxv[:, k, :], k == 0, k == KD - 1,
                            r=[Wxv, (mT, k), mT], w=[PSK(bv)])
                self.evac(V[:, s, mb, :], self.ps(bv), r=[PSK(bv)], w=[(V, s, mb)])
        if "xe0" in cfg.phases:
            return
        self.S.barrier()
        A.off = mark
        self.host_fn()
        Wxq = A.alloc("Wxq", [KD, 512], BF16)
        Wxo = A.alloc("Wxo", [4, D], BF16)
        self.dm(Wxq[:, :, :], W["xq"][0], w=[Wxq])
        self.dm(Wxo[:, :, :], W["xo"][0], w=[Wxo])
        xt_t = [A.alloc("xx%d" % i, [KD, TT]) for i in range(2)]
        rstd = A.alloc("xrstd", [TT])
        hT_t = [A.alloc("xh%d" % i, [KD, TT], BF16) for i in range(2)]
        qn_t = [A.alloc("xqn%d" % i, [TT], BF16) for i in range(XH)]
        pT_t = [[A.alloc("xp%d_%d" % (i, mb), [TT], BF16) for mb in range(2)] for i in range(XH)]
        rden_t = [A.alloc("xrden%d" % i, [TT]) for i in range(2)]
        oTn = A.alloc("xo", [XH, TT], BF16)
        it = 0
        for s in range(NS):
            for t in range(NT):
                xt, hT = xt_t[it % 2], hT_t[it % 2]
                it += 1
                self.load_xt(xT, s, t, xt)
                self.norm_h(xt, hT, G["xattn"], l, rstd, sq_t)
                for h in range(XH):
                    bq, bs_ = 1 + h, 5 + h % 2
                    rk, qn, pT, rden = rk_t[h % 2], qn_t[h], pT_t[h], rden_t[h % 2]
                    sqs = [sq_t[h % 2], sq_t[(h + 1) % 2]]
                    for k in range(KD):
                        self.mm(self.ps(bq), Wxq[:, k, h * P:(h + 1) * P], hT[:, k, :], k == 0, k == KD - 1,
                                r=[Wxq, (hT, k)], w=[PSK(bq)])
                    self.rms_stats([(self.ps(bq), PSK(bq), P)], TT, 1.0 / 128, bs_, rk, sqs)
                    self.stt(qn[:, :], self.ps(bq), G["xq"][:, l, 0:1], rk[:, :], ALU.mult, ALU.mult,
                             r=[PSK(bq), rk, G["xq"]], w=[qn])
                    for mb in range(2):
                        bsc = bq if mb == 0 else bs_
                        self.mm(self.ps(bsc), kT[:, s, h, mb * P:(mb + 1) * P], qn[:, :], True, True,
                                r=[(kT, s, h), qn], w=[PSK(bsc)])
                        self.actf(pT[mb][:, :], self.ps(bsc), AF.Exp, r=[PSK(bsc)], w=[pT[mb]], scale=float(128 ** -0.5))
                    for mb in range(2):
                        self.mm(self.ps(7), V[:, s, mb, h * P:(h + 1) * P], pT[mb][:, :], mb == 0, mb == 1,
                                r=[(V, s, mb), pT[mb]], w=[PSK(7)])
                    for mb in range(2):
                        self.mm(self.ps(0), self.ones_b[:, :], pT[mb][:, :], mb == 0, mb == 1,
                                r=[self.ones_b, pT[mb]], w=[PSK(0)])
                    self.actf(rden[:, :], self.ps(0), AF.Ln, r=[PSK(0)], w=[rden])
                    self.actf(rden[:, :], rden[:, :], AF.Exp, r=[rden], w=[rden], scale=-1.0)
                    self.tt("dve", oTn[:, h, :], self.ps(7), rden[:, :], ALU.mult, r=[PSK(7), rden], w=[(oTn, h)])
                for dk in range(KD):
                    bo = 1 + dk % 4
                    for h in range(XH):
                        self.mm(self.ps(bo), Wxo[:, h, dk * P:(dk + 1) * P], oTn[:, h, :], h == 0, h == XH - 1,
                                r=[Wxo, (oTn, h)], w=[PSK(bo)])
                    self.tt("dve", xt[:, dk, :], self.ps(bo), xt[:, dk, :], ALU.add, r=[PSK(bo), (xt, dk)], w=[(xt, dk)])
                self.store_xt(xT, s, t, xt)
    def phase_rope(self):
        cfg, A, S = self.cfg, self.arena, self.S
        NS, SQ = cfg.NS, cfg.S
        self.ropeC = self.dscr("ropeC", [NS, 64, SQ])
        self.ropeS = self.dscr("ropeS", [NS, 64, SQ])
        ji = A.alloc("ji", [1], I32, parts=64)
        jf = A.alloc("jf", [1], parts=64)
        invf = A.alloc("invf", [1], parts=64)
        S.pool(lambda e: e.iota(out=ji[0:32, :], pattern=[[0, 1]], base=0, channel_multiplier=1), w=[ji])
        S.pool(lambda e: e.iota(out=ji[32:64, :], pattern=[[0, 1]], base=0, channel_multiplier=1), w=[ji])
        self.cp("dve", jf[:, :], ji[:, :], r=[ji], w=[jf])
        self.actf(invf[:, :], jf[:, :], AF.Exp, r=[jf], w=[invf], scale=float(-np.log(10000.0) / 32.0))
        pos_i = A.alloc("pos_i", [SQ], I32, parts=64)
        ang = A.alloc("ang", [SQ], parts=64)
        a2 = A.alloc("a2", [SQ], parts=64)
        ni = A.alloc("ni", [SQ], I32, parts=64)
        nf = A.alloc("nf", [SQ], parts=64)
        rr = A.alloc("rr", [SQ], parts=64)
        m_ = A.alloc("m_", [SQ], parts=64)
        tabs = [A.alloc("tabS", [SQ], parts=64), A.alloc("tabC", [SQ], parts=64)]
        TWO_PI = 2.0 * PI
        for s in range(NS):
            self.dm(pos_i[:, :].rearrange("p (a s) -> p a s", a=1), self.I["positions"][s:s + 1, :].partition_broadcast(64), w=[pos_i])
            self.cp("dve", ang[:, :], pos_i[:, :], r=[pos_i], w=[ang])
            self.ts("dve", ang[:, :], ang[:, :], invf[:, 0:1], None, ALU.mult, None, r=[ang, invf], w=[ang])
            for tab, shift in ((tabs[0], 0.0), (tabs[1], PI / 2)):
                self.ts("dve", a2[:, :], ang[:, :], shift, 1.0 / TWO_PI, ALU.add, ALU.mult, r=[ang], w=[a2])
                self.cp("dve", ni[:, :], a2[:, :], r=[a2], w=[ni])
                self.cp("dve", nf[:, :], ni[:, :], r=[ni], w=[nf])
                self.stt(rr[:, :], nf[:, :], -TWO_PI, ang[:, :], ALU.mult, ALU.add, r=[nf, ang], w=[rr])
                if shift:
                    self.ts("dve", rr[:, :], rr[:, :], shift, None, ALU.add, None, r=[rr], w=[rr])
                self.ts("dve", m_[:, :], rr[:, :], PI, TWO_PI, ALU.is_gt, ALU.mult, r=[rr], w=[m_])
                self.tt("dve", rr[:, :], rr[:, :], m_[:, :], ALU.subtract, r=[rr, m_], w=[rr])
                self.ts("dve", m_[:, :], rr[:, :], -PI, TWO_PI, ALU.is_lt, ALU.mult, r=[rr], w=[m_])
                self.tt("dve", rr[:, :], rr[:, :], m_[:, :], ALU.add, r=[rr, m_], w=[rr])
                self.ts("dve", rr[:, :], rr[:, :], -3.1415925, 3.1415925, ALU.max, ALU.min, r=[rr], w=[rr])
                self.actf(tab[:, :], rr[:, :], AF.Sin, r=[rr], w=[tab])
            self.ts("dve", tabs[0][0:32, :], tabs[0][0:32, :], -1.0, None, ALU.mult, None, r=[tabs[0]], w=[tabs[0]])
            self.dm(self.ropeS[s], tabs[0][:, :], r=[tabs[0]], w=[("ropeS", s)])
            self.dm(self.ropeC[s], tabs[1][:, :], r=[tabs[1]], w=[("ropeC", s)])

    def phase_assd(self, l, W):
        cfg, A, G = self.cfg, self.arena, self.G
        PSK = self.PSK
        NS, SQ, NT = cfg.NS, cfg.S, cfg.S // TT
        xt_t = [A.alloc("ax%d" % i, [KD, TT]) for i in range(2)]
        sq_t = [A.alloc("asq%d" % i, [TT], BF16) for i in range(2)]
        rstd = A.alloc("arstd", [TT])
        hT = A.alloc("ah", [KD, TT], BF16)
        slab_t = [A.alloc("aslab%d" % i, [KD, 512], BF16) for i in range(3)]
        wsm = A.alloc("awsm", [KD, 144], BF16)
        stz = A.alloc("astz", [4, SSD_INNER])
        stx = A.alloc("astx", [12, TT])
        dts = A.alloc("adts", [4, NHS])
        self.dm(wsm[:, :, :], W["sm"][0], w=[wsm])
        it = 0
        si = 0
        n = 0
        for s in range(NS):
            for t in range(NT):
                xt = xt_t[it % 2]
                it += 1
                tsl = slice(t * TT, (t + 1) * TT)
                self.load_xt(self.xT, s, t, xt)
                self.norm_h(xt, hT, G["attn"], l, rstd, sq_t)
                import os
                skip = os.environ.get("KSKIP", "").split(",")
                for half in range(2):
                    if "z" in skip:
                        break
                    slab = slab_t[si % 3]
                    si += 1
                    self.dm(slab[:, :, :], W["z"][half], w=[slab])
                    for tb in range(4):
                        bank = 1 + n % 2
                        n += 1
                        for k in range(KD):
                            self.mm(self.ps(bank), hT[:, k, tb * P:(tb + 1) * P], slab[:, k, :], k == 0, k == KD - 1,
                                    r=[slab, (hT, k)], w=[PSK(bank)])
                        self.actf(stz[:, tb, half * 512:(half + 1) * 512], self.ps(bank), AF.Silu, r=[PSK(bank)], w=[(stz, tb)])
                for tb in range(4):
                    if "z" in skip:
                        break
                    self.dm(self.szt[s, t * TT + tb * P:t * TT + (tb + 1) * P, :], stz[:, tb, :], r=[(stz, tb)], w=[("szt", s, t, tb)])
                for tb in range(4):
                    if "dt" in skip:
                        break
                    for k in range(KD):
                        self.mm(self.ps(3)[:, tb * NHS:(tb + 1) * NHS], hT[:, k, tb * P:(tb + 1) * P], wsm[:, k, 0:NHS],
                                k == 0, k == KD - 1, r=[wsm, (hT, k)], w=[PSK(3)])
                if "dt" not in skip:
                    self.evac(dts[:, :, :], self.ps(3)[:, 0:4 * NHS].rearrange("p (a b) -> p a b", a=4), r=[PSK(3)], w=[dts])
                    if "dtdma" not in skip:
                        self.dm(self.dtt[s, t], dts.ap.rearrange("p a b -> p (a b)"), r=[dts], w=[("dtt", s, t)])
                for sl in range(3):
                    if "xbc" in skip:
                        break
                    slab = slab_t[si % 3]
                    si += 1
                    self.dm(slab[:, :, :], W["xbc"][sl], w=[slab])
                    for c in range(4):
                        ch = sl * 4 + c
                        bank = 4 + n % 2
                        n += 1
                        for k in range(KD):
                            self.mm(self.ps(bank), slab[:, k, c * P:(c + 1) * P], hT[:, k, :], k == 0, k == KD - 1,
                                    r=[slab, (hT, k)], w=[PSK(bank)])
                        self.evac(stx[:, ch, :], self.ps(bank), r=[PSK(bank)], w=[(stx, ch)])
                for h in range(2):
                    if "xbc" in skip:
                        break
                    self.dm(self.xbcT[s, :, h * 6:(h + 1) * 6, tsl], stx[:, h * 6:(h + 1) * 6, :],
                            r=[(stx, c) for c in range(h * 6, h * 6 + 6)], w=[("xbcT", s, t, h)])

    def phase_amla(self, l, W):
        cfg, A, G = self.cfg, self.arena, self.G
        PSK = self.PSK
        NS, SQ, NT = cfg.NS, cfg.S, cfg.S // TT
        xt = A.alloc("mx", [KD, TT])
        sq_t = [A.alloc("msq%d" % i, [TT], BF16) for i in range(2)]
        rstd = A.alloc("mrstd", [TT])
        rh = A.alloc("mrh", [TT])
        hT = A.alloc("mh", [KD, TT], BF16)
        slab = A.alloc("mslab", [KD, 512], BF16)
        wsm = A.alloc("mwsm", [KD, 144], BF16)
        Wqbn = A.alloc("Wqbn", [4, 1024], BF16)
        Wqbr = A.alloc("Wqbr", [4, 512], BF16)
        Wqbrs = A.alloc("Wqbrs", [4, 512], BF16)
        Wkvn = A.alloc("Wkvn", [4, 1024], BF16)
        Wkvv = A.alloc("Wkvv", [4, 1024], BF16)
        for tl, nm in ((wsm, "sm"), (Wqbn, "qbn"), (Wqbr, "qbr"), (Wqbrs, "qbrs"), (Wkvn, "kvn"), (Wkvv, "kvv")):
            self.dm(tl[:, :, :], W[nm][0], w=[tl])
        la = A.alloc("mla", [4, TT])
        lan = A.alloc("mlan", [4, TT], BF16)
        stn = A.alloc("mstn", [MLA_H, TT], BF16)
        str_ = A.alloc("mstr", [MLA_H, TT], BF16, parts=64)
        vts = A.alloc("mvts", [4, MLA_H * P], BF16)
        c2 = A.alloc("mc2", [TT], parts=64)
        s2 = A.alloc("ms2", [TT], parts=64)
        t1 = A.alloc("mt1", [TT], parts=64)
        t2 = A.alloc("mt2", [TT], parts=64)
        kr0 = A.alloc("mkr0", [TT], parts=64)
        sqk = A.alloc("msqk", [TT], BF16, parts=64)
        sqn = A.alloc("msqn", [TT], BF16)
        n = 0
        for s in range(NS):
            for t in range(NT):
                tsl = slice(t * TT, (t + 1) * TT)
                self.load_xt(self.xT, s, t, xt)
                self.norm_h(xt, hT, G["attn"], l, rstd, sq_t)
                self.dm(c2[:, :], self.ropeC[s, :, tsl], w=[c2])
                self.dm(s2[:, :], self.ropeS[s, :, tsl], w=[s2])

                def lora(wname, gname):
                    self.dm(slab[:, :, :], W[wname][0], w=[slab])
                    for c in range(4):
                        bank = 1 + c % 2
                        for k in range(KD):
                            self.mm(self.ps(bank), slab[:, k, c * P:(c + 1) * P], hT[:, k, :], k == 0, k == KD - 1,
                                    r=[slab, (hT, k)], w=[PSK(bank)])
                        self.evac(la[:, c, :], self.ps(bank), r=[PSK(bank)], w=[(la, c)])
                    self.rms_stats([(la[:, c, :], (la, c), P) for c in range(4)], TT, 1.0 / 512, 0, rstd, sq_t)
                    for c in range(4):
                        self.stt(lan[:, c, :], la[:, c, :], G[gname][:, l, c:c + 1], rstd[:, :], ALU.mult, ALU.mult,
                                 r=[(la, c), rstd, G[gname]], w=[(lan, c)])

                lora("qa", "qa")
                for h in range(MLA_H):
                    bn, br, bs = (1, 2, 3) if h % 2 == 0 else (4, 5, 6)
                    for kk in range(4):
                        self.mm(self.ps(bn), Wqbn[:, kk, h * P:(h + 1) * P], lan[:, kk, :], kk == 0, kk == 3,
                                r=[Wqbn, (lan, kk)], w=[PSK(bn)])
                    for kk in range(4):
                        self.mm(self.ps(br)[0:64, :], Wqbr[:, kk, h * 64:(h + 1) * 64], lan[:, kk, :], kk == 0, kk == 3,
                                r=[Wqbr, (lan, kk)], w=[PSK(br)])
                    for kk in range(4):
                        self.mm(self.ps(bs)[0:64, :], Wqbrs[:, kk, h * 64:(h + 1) * 64], lan[:, kk, :], kk == 0, kk == 3,
                                r=[Wqbrs, (lan, kk)], w=[PSK(bs)])
                    self.rms_stats([(self.ps(bn), PSK(bn), P), (self.ps(br)[0:64, :], PSK(br), 64)], TT, 1.0 / 192, 7, rh, sq_t)
                    self.stt(stn[:, h, :], self.ps(bn), G["qn"][:, l, 0:1], rh[:, :], ALU.mult, ALU.mult,
                             r=[PSK(bn), rh, G["qn"]], w=[(stn, h)])
                    self.stt(t1[:, :], self.ps(br)[0:64, :], G["qr"][:, l, 0:1], rh[0:64, :], ALU.mult, ALU.mult,
                             r=[PSK(br), rh, G["qr"]], w=[t1])
                    self.stt(t2[:, :], self.ps(bs)[0:64, :], G["qrs"][:, l, 0:1], rh[0:64, :], ALU.mult, ALU.mult,
                             r=[PSK(bs), rh, G["qrs"]], w=[t2])
                    self.tt("pool", t1[:, :], t1[:, :], c2[:, :], ALU.mult, r=[t1, c2], w=[t1])
                    self.tt("pool", t2[:, :], t2[:, :], s2[:, :], ALU.mult, r=[t2, s2], w=[t2])
                    self.tt("pool", str_[:, h, :], t1[:, :], t2[:, :], ALU.add, r=[t1, t2], w=[(str_, h)])
                self.dm(self.qnT[s, :, :, tsl].rearrange("h p t -> p h t"), stn[:, :, :],
                        r=[(stn, h) for h in range(MLA_H)], w=[("qnT", s, t)])
                self.dm(self.qrT[s, :, :, tsl].rearrange("h p t -> p h t"), str_[:, :, :],
                        r=[(str_, h) for h in range(MLA_H)], w=[("qrT", s, t)])
                lora("kva", "kva")
                for k in range(KD):
                    self.mm(self.ps(2)[0:64, :], wsm[:, k, 16:80], hT[:, k, :], k == 0, k == KD - 1,
                            r=[wsm, (hT, k)], w=[PSK(2)])
                for k in range(KD):
                    self.mm(self.ps(3)[0:64, :], wsm[:, k, 80:144], hT[:, k, :], k == 0, k == KD - 1,
                            r=[wsm, (hT, k)], w=[PSK(3)])
                self.actf(sqk[:, :], self.ps(2)[0:64, :], AF.Square, r=[PSK(2)], w=[sqk])
                self.ts("dve", t1[:, :], self.ps(2)[0:64, :], G["kr"][:, l, 0:1], None, ALU.mult, None, r=[PSK(2), G["kr"]], w=[t1])
                self.ts("dve", t2[:, :], self.ps(3)[0:64, :], G["krs"][:, l, 0:1], None, ALU.mult, None, r=[PSK(3), G["krs"]], w=[t2])
                self.tt("pool", t1[:, :], t1[:, :], c2[:, :], ALU.mult, r=[t1, c2], w=[t1])
                self.tt("pool", t2[:, :], t2[:, :], s2[:, :], ALU.mult, r=[t2, s2], w=[t2])
                self.tt("pool", kr0[:, :], t1[:, :], t2[:, :], ALU.add, r=[t1, t2], w=[kr0])
                for h in range(MLA_H):
                    bn = 4 + h % 2
                    for kk in range(4):
                        self.mm(self.ps(bn), Wkvn[:, kk, h * P:(h + 1) * P], lan[:, kk, :], kk == 0, kk == 3,
                                r=[Wkvn, (lan, kk)], w=[PSK(bn)])
                    self.actf(sqn[:, :], self.ps(bn), AF.Square, r=[PSK(bn)], w=[sqn])
                    self.mm(self.ps(7), self.ones_b[:, :], sqn[:, :], True, False, r=[sqn, self.ones_b], w=[PSK(7)])
                    self.mm(self.ps(7), self.ones_b[0:64, :], sqk[:, :], False, True, r=[sqk, self.ones_b], w=[PSK(7)])
                    self.actf(rh[:, :], self.ps(7), AF.Ln, r=[PSK(7), self.eps_t], w=[rh], scale=1.0 / 192, bias=self.eps_t[:, 0:1])
                    self.actf(rh[:, :], rh[:, :], AF.Exp, r=[rh], w=[rh], scale=-0.5)
                    self.stt(stn[:, h, :], self.ps(bn), G["kn"][:, l, 0:1], rh[:, :], ALU.mult, ALU.mult,
                             r=[PSK(bn), rh, G["kn"]], w=[(stn, h)])
                    self.tt("pool", str_[:, h, :], kr0[:, :], rh[0:64, :], ALU.mult, r=[kr0, rh], w=[(str_, h)])
                self.dm(self.knT[s, :, :, tsl].rearrange("h p t -> p h t"), stn[:, :, :],
                        r=[(stn, h) for h in range(MLA_H)], w=[("knT", s, t)])
                self.dm(self.krT[s, :, :, tsl].rearrange("h p t -> p h t"), str_[:, :, :],
                        r=[(str_, h) for h in range(MLA_H)], w=[("krT", s, t)])
                for tb in range(4):
                    for half in range(2):
                        bank = 1 + n % 2
                        n += 1
                        for kk in range(4):
                            self.mm(self.ps(bank), lan[:, kk, tb * P:(tb + 1) * P], Wkvv[:, kk, half * 512:(half + 1) * 512],
                                    kk == 0, kk == 3, r=[Wkvv, (lan, kk)], w=[PSK(bank)])
                        self.evac(vts[:, tb, half * 512:(half + 1) * 512], self.ps(bank), r=[PSK(bank)], w=[(vts, tb)])
                self.dm(self.vt[s, tsl, :].rearrange("(tb p) c -> p tb c", p=P), vts[:, :, :],
                        r=[(vts, tb) for tb in range(4)], w=[("vt", s, t)])

    def phase_attn(self, l, W):
        cfg, A = self.cfg, self.arena
        PSK = self.PSK
        NS, SQ, NT = cfg.NS, cfg.S, cfg.S // TT
        NJ = SQ // P
        kn_t = [A.alloc("ckn%d" % i, [SQ], BF16) for i in range(2)]
        kr_t = [A.alloc("ckr%d" % i, [SQ], BF16, parts=64) for i in range(2)]
        v_all = A.alloc("cvall", [NJ, MLA_H * P], BF16)
        qn_t = [A.alloc("cqn%d" % i, [TT], BF16) for i in range(2)]
        qr_t = [A.alloc("cqr%d" % i, [TT], BF16, parts=64) for i in range(2)]
        p_t = [A.alloc("cp%d" % i, [TT], BF16) for i in range(3)]
        rden = A.alloc("crden", [TT])
        o_t = [A.alloc("co%d" % i, [TT], BF16) for i in range(2)]
        scale = float(192 ** -0.5)
        hi = 0
        qi = 0
        pj = 0
        for s in range(NS):
            vsrc = self.vt[s].rearrange("(j p) c -> p j c", p=P)
            for hf in range(2):
                self.dm(v_all[:, hf * (NJ // 2):(hf + 1) * (NJ // 2), :], vsrc[:, hf * (NJ // 2):(hf + 1) * (NJ // 2), :], w=[v_all])
            for h in range(MLA_H):
                kn, kr = kn_t[hi % 2], kr_t[hi % 2]
                hi += 1
                self.dm(kn[:, :], self.knT[s, h], w=[kn])
                self.dm(kr[:, :], self.krT[s, h], w=[kr])
                for Q in range(NT):
                    qn, qr, ot = qn_t[qi % 2], qr_t[qi % 2], o_t[qi % 2]
                    bo, bd = (4, 5) if qi % 2 == 0 else (6, 7)
                    qi += 1
                    qsl = slice(Q * TT, (Q + 1) * TT)
                    self.dm(qn[:, :], self.qnT[s, h, :, qsl], w=[qn])
                    self.dm(qr[:, :], self.qrT[s, h, :, qsl], w=[qr])
                    nj = 4 * Q + 4
                    for j in range(nj):
                        r_ = j - 4 * Q
                        q0 = P * max(r_, 0)
                        bs_ = pj % 3
                        p = p_t[pj % 3]
                        pj += 1
                        self.mm(self.ps(bs_)[:, q0:TT], kn[:, j * P:(j + 1) * P], qn[:, q0:TT], True, False,
                                r=[kn, qn], w=[PSK(bs_)])
                        self.mm(self.ps(bs_)[:, q0:TT], kr[:, j * P:(j + 1) * P], qr[:, q0:TT], False, True,
                                r=[kr, qr], w=[PSK(bs_)])
                        self.actf(p[:, q0:TT], self.ps(bs_)[:, q0:TT], AF.Exp, r=[PSK(bs_)], w=[p], scale=scale)
                        if r_ >= 0:
                            self.tt("pool", p[:, q0:q0 + P], p[:, q0:q0 + P], self.mge_b[:, :], ALU.mult, r=[p, self.mge_b], w=[p])
                        self.mm(self.ps(bo)[:, q0:TT], v_all[:, j, h * P:(h + 1) * P], p[:, q0:TT], j == 0, j == nj - 1,
                                r=[v_all, p], w=[PSK(bo)])
                        self.mm(self.ps(bd)[:, q0:TT], self.ones_b[:, :], p[:, q0:TT], j == 0, j == nj - 1,
                                r=[self.ones_b, p], w=[PSK(bd)])
                    self.actf(rden[:, :], self.ps(bd), AF.Ln, r=[PSK(bd)], w=[rden])
                    self.actf(rden[:, :], rden[:, :], AF.Exp, r=[rden], w=[rden], scale=-1.0)
                    self.tt("dve", ot[:, :], self.ps(bo), rden[:, :], ALU.mult, r=[PSK(bo), rden], w=[ot])
                    self.dm(self.mixT[s, :, 8 + h, qsl], ot[:, :], r=[ot], w=[("mixT", s, h, Q)])

    def phase_oproj(self, l, W):
        cfg, A = self.cfg, self.arena
        PSK = self.PSK
        NS, NT = cfg.NS, cfg.S // TT
        xt_t = [A.alloc("ox%d" % i, [KD, TT]) for i in range(2)]
        mx_t = [A.alloc("om%d" % i, [KD, TT], BF16) for i in range(2)]
        slab_t = [A.alloc("oslab%d" % i, [KD, 512], BF16) for i in range(2)]
        it = 0
        si = 0
        for s in range(NS):
            for t in range(NT):
                xt, mx = xt_t[it % 2], mx_t[it % 2]
                it += 1
                tsl = slice(t * TT, (t + 1) * TT)
                self.load_xt(self.xT, s, t, xt)
                for h in range(2):
                    self.dm(mx[:, h * 8:(h + 1) * 8, :], self.mixT[s, :, h * 8:(h + 1) * 8, tsl], w=[(mx, h)])
                for sl in range(4):
                    slab = slab_t[si % 2]
                    si += 1
                    self.dm(slab[:, :, :], W["out"][sl], w=[slab])
                    for c in range(4):
                        dk = sl * 4 + c
                        bank = 1 + dk % 2
                        for k in range(KD):
                            self.mm(self.ps(bank), slab[:, k, c * P:(c + 1) * P], mx[:, k, :], k == 0, k == KD - 1,
                                    r=[slab, (mx, k // 8)], w=[PSK(bank)])
                        self.tt("dve", xt[:, dk, :], self.ps(bank), xt[:, dk, :], ALU.add, r=[PSK(bank), (xt, dk)], w=[(xt, dk)])
                self.store_xt(self.xT, s, t, xt)
    def phase_ssd(self, l, W):
        cfg, A, G, S = self.cfg, self.arena, self.G, self.S
        PSK = self.PSK
        NS, SQ, NT = cfg.NS, cfg.S, cfg.S // TT
        ident_b, mge_f, mgt_f, ones_f = self.ident_b, self.mge_f, self.mgt_f, self.ones_f
        gn = A.alloc("sgn", [SSD_INNER])
        self.dm(gn[:, :].rearrange("p (a c) -> p a c", a=1), self.I["ssd_norm_g"][l:l + 1, :].partition_broadcast(P), w=[gn])
        xb = A.alloc("sxb", [12, TT + 3])
        acc_t = [A.alloc("sacc%d" % i, [TT]) for i in range(2)]
        cv = A.alloc("scv", [12, TT], BF16)
        xs_tok = A.alloc("sxs", [4, SSD_INNER], BF16)
        B_tok = A.alloc("sBt", [4, 256], BF16)
        sz = A.alloc("ssz", [4, SSD_INNER])
        sm = {}
        for nm in ("dtr", "dtv", "av", "acs", "tot", "E", "Wd", "dec", "dtw"):
            sm[nm] = A.alloc("s" + nm, [4, NHS])
        flat = lambda tl: tl.ap.rearrange("p a b -> p (a b)")
        xdt = A.alloc("sxdt", [4, SSD_INNER], BF16)
        xdtw = A.alloc("sxdtw", [4, SSD_INNER], BF16)
        st_f = A.alloc("sstf", [SSD_INNER])
        st_b = A.alloc("sstb", [SSD_INNER], BF16)
        cbm_t = [A.alloc("scbm%d" % i, [P], BF16) for i in range(2)]
        ra_t = [A.alloc("sra%d" % i, [8, P]) for i in range(2)]
        ed_t = [A.alloc("sed%d" % i, [TT]) for i in range(2)]
        MT_t = [A.alloc("sMT%d" % i, [4, P], BF16) for i in range(4)]
        y_t = [A.alloc("sy%d" % i, [TT]) for i in range(2)]
        xd_t = [A.alloc("sxd%d" % i, [TT]) for i in range(2)]
        junk = A.alloc("sjunk", [TT], BF16)
        ssq = A.alloc("sssq", [8])
        yn_t = [A.alloc("syn%d" % i, [TT], BF16) for i in range(2)]
        mixs = A.alloc("smixs", [8, TT], BF16)
        v3 = lambda ap: ap.rearrange("p (e d) -> p e d", d=HD)
        bc = lambda ap, n: ap[:, :, None].broadcast_to([P, ap.shape[1], n])
        ci = 0
        mi = 0
        for s in range(NS):
            S.pool(lambda e: e.memset(st_f[:, :], 0.0), w=[(st_f, 0), (st_f, 1)])
            S.pool(lambda e: e.memset(st_b[:, :], 0.0), w=[(st_b, 0), (st_b, 1)])
            for t in range(NT):
                tsl = slice(t * TT, (t + 1) * TT)
                if t == 0:
                    S.pool(lambda e: e.memset(xb[:, :, 0:3], 0.0), w=[xb])
                    for h in range(2):
                        self.dm(xb[:, h * 6:(h + 1) * 6, 3:TT + 3], self.xbcT[s, :, h * 6:(h + 1) * 6, 0:TT], w=[xb])
                else:
                    for h in range(2):
                        self.dm(xb[:, h * 6:(h + 1) * 6, :], self.xbcT[s, :, h * 6:(h + 1) * 6, t * TT - 3:(t + 1) * TT], w=[xb])
                self.dm(sz[:, :, :], self.szt[s, tsl, :].rearrange("(tb p) c -> p tb c", p=P), w=[sz])
                self.dm(flat(sm["dtr"]), self.dtt[s, t], w=[sm["dtr"]])
                for c in range(12):
                    acc = acc_t[c % 2]
                    self.actf(acc[:, :], xb[:, c, 0:TT], AF.Identity, r=[xb, G["cw"], G["cb"]], w=[acc],
                              scale=G["cw"][:, l, 0, c:c + 1], bias=G["cb"][:, l, c:c + 1])
                    for j in range(1, 4):
                        self.stt(acc[:, :], xb[:, c, j:j + TT], G["cw"][:, l, j, c:c + 1], acc[:, :], ALU.mult, ALU.add,
                                 r=[xb, acc, G["cw"]], w=[acc])
                    self.actf(cv[:, c, :], acc[:, :], AF.Silu, r=[acc], w=[(cv, c)])
                for tb in range(4):
                    bank = 4 + tb % 2
                    for c in range(8):
                        self.tr(self.psb(bank)[:, c * P:(c + 1) * P], cv[:, c, tb * P:(tb + 1) * P], ident_b[:, :],
                                r=[(cv, c), ident_b], w=[PSK(bank)])
                    self.cp("act", xs_tok[:, tb, :], self.psb(bank), r=[PSK(bank)], w=[(xs_tok, tb)])
                    for g in range(2):
                        self.tr(self.psb(6)[:, (tb * 2 + g) * P:(tb * 2 + g + 1) * P], cv[:, 8 + g, tb * P:(tb + 1) * P], ident_b[:, :],
                                r=[(cv, 8 + g), ident_b], w=[PSK(6)])
                self.cp("act", B_tok.ap.rearrange("p a b -> p (a b)"), self.psb(6), r=[PSK(6)], w=[B_tok])
                dtv, av = sm["dtv"], sm["av"]
                self.tt("dve", dtv[:, :, :], sm["dtr"][:, :, :], G["dtb"][:, l:l + 1, :].broadcast_to([P, 4, NHS]), ALU.add,
                        r=[sm["dtr"], G["dtb"]], w=[dtv])
                self.actf(dtv[:, :, :], dtv[:, :, :], AF.Exp, r=[dtv], w=[dtv])
                self.ts("dve", dtv[:, :, :], dtv[:, :, :], 1.0, None, ALU.add, None, r=[dtv], w=[dtv])
                self.actf(dtv[:, :, :], dtv[:, :, :], AF.Ln, r=[dtv], w=[dtv])
                self.tt("dve", av[:, :, :], dtv[:, :, :], G["Aneg"][:, l:l + 1, :].broadcast_to([P, 4, NHS]), ALU.mult,
                        r=[dtv, G["Aneg"]], w=[av])
                self.mm(self.ps(3)[:, 0:64], mge_f[:, :], flat(av), True, True, r=[av, mge_f], w=[PSK(3)])
                self.mm(self.ps(3)[:, 64:128], ones_f[:, :], flat(av), True, True, r=[av, ones_f], w=[PSK(3)])
                self.cp("dve", flat(sm["acs"]), self.ps(3)[:, 0:64], r=[PSK(3)], w=[sm["acs"]])
                self.cp("dve", flat(sm["tot"]), self.ps(3)[:, 64:128], r=[PSK(3)], w=[sm["tot"]])
                self.actf(sm["E"][:, :, :], sm["acs"][:, :, :], AF.Exp, r=[sm["acs"]], w=[sm["E"]])
                self.tt("dve", sm["Wd"][:, :, :], sm["tot"][:, :, :], sm["acs"][:, :, :], ALU.subtract, r=[sm["tot"], sm["acs"]], w=[sm["Wd"]])
                self.actf(sm["Wd"][:, :, :], sm["Wd"][:, :, :], AF.Exp, r=[sm["Wd"]], w=[sm["Wd"]])
                self.actf(sm["dec"][:, :, :], sm["tot"][:, :, :], AF.Exp, r=[sm["tot"]], w=[sm["dec"]])
                self.tt("dve", sm["dtw"][:, :, :], dtv[:, :, :], sm["Wd"][:, :, :], ALU.mult, r=[dtv, sm["Wd"]], w=[sm["dtw"]])
                for tb in range(4):
                    self.tt("pool", v3(xdt[:, tb, :]), v3(xs_tok[:, tb, :]), bc(dtv[:, tb, :], HD), ALU.mult,
                            r=[(xs_tok, tb), dtv], w=[(xdt, tb)])
                    self.tt("pool", v3(xdtw[:, tb, :]), v3(xs_tok[:, tb, :]), bc(sm["dtw"][:, tb, :], HD), ALU.mult,
                            r=[(xs_tok, tb), sm["dtw"]], w=[(xdtw, tb)])
                for tb in range(4):
                    csl = slice(tb * P, (tb + 1) * P)
                    for g in range(2):
                        gs = slice(g * 512, (g + 1) * 512)
                        es_ = slice(g * 8, (g + 1) * 8)
                        cbm, ra, yt, xd, yn = cbm_t[ci % 2], ra_t[ci % 2], y_t[ci % 2], xd_t[ci % 2], yn_t[ci % 2]
                        sc = ssq[:, ci % 8:ci % 8 + 1]
                        sck = (ssq, ci % 8)
                        ci += 1
                        Bt, Ct = cv[:, 8 + g, csl], cv[:, 10 + g, csl]
                        self.mm(self.ps(0)[:, 0:P], Bt, Ct, True, True, r=[(cv, 8 + g), (cv, 10 + g)], w=[PSK(0)])
                        self.tt("dve", cbm[:, :], self.ps(0)[:, 0:P], mge_f[:, :], ALU.mult, r=[PSK(0), mge_f], w=[cbm])
                        self.tt("pool", ra[:, :, :], mge_f[:, None, :].broadcast_to([P, 8, P]), bc(av[:, tb, es_], P), ALU.mult,
                                r=[mge_f, av], w=[ra])
                        MTs = []
                        for hh in range(2):
                            ed = ed_t[hh]
                            MT = MT_t[mi % 4]
                            mi += 1
                            MTs.append(MT)
                            self.mm(self.ps(1 + hh), mgt_f[:, :], ra[:, hh * 4:(hh + 1) * 4, :].rearrange("p a b -> p (a b)"), True, True,
                                    r=[ra, mgt_f], w=[PSK(1 + hh)])
                            self.actf(ed[:, :], self.ps(1 + hh), AF.Exp, r=[PSK(1 + hh)], w=[ed])
                            self.tt("dve", MT[:, :, :], ed[:, :].rearrange("p (a b) -> p a b", a=4),
                                    cbm[:, None, :].broadcast_to([P, 4, P]), ALU.mult, r=[ed, cbm], w=[MT])
                        for e in range(8):
                            col = (g * 8 + e) * HD
                            self.mm(self.ps(4)[:, e * HD:(e + 1) * HD], MTs[e // 4][:, e % 4, :], xdt[:, tb, col:col + HD], True, True,
                                    r=[MTs[e // 4], (xdt, tb)], w=[PSK(4)])
                        self.mm(self.ps(5), Ct, st_b[:, gs], True, True, r=[(cv, 10 + g), (st_b, g)], w=[PSK(5)])
                        self.mm(self.ps(6), B_tok[:, tb, g * P:(g + 1) * P], xdtw[:, tb, gs], True, True,
                                r=[B_tok, (xdtw, tb)], w=[PSK(6)])
                        self.tt("dve", v3(yt[:, :]), v3(self.ps(5)), bc(sm["E"][:, tb, es_], HD), ALU.mult,
                                r=[PSK(5), sm["E"]], w=[yt])
                        self.tt("dve", yt[:, :], self.ps(4), yt[:, :], ALU.add, r=[PSK(4), yt], w=[yt])
                        self.tt("pool", v3(xd[:, :]), v3(xs_tok[:, tb, gs]), bc(G["dsk"][:, l, es_], HD), ALU.mult,
                                r=[(xs_tok, tb), G["dsk"]], w=[xd])
                        self.tt("pool", yt[:, :], yt[:, :], xd[:, :], ALU.add, r=[yt, xd], w=[yt])
                        self.tt("pool", yt[:, :], yt[:, :], sz[:, tb, gs], ALU.mult, r=[yt, sz], w=[yt])
                        self.actf(junk[:, :], yt[:, :], AF.Square, r=[yt], w=[junk, sck], accum=sc)
                        self.actf(sc, sc, AF.Ln, r=[sck, self.eps_t], w=[sck], scale=1.0 / 512, bias=self.eps_t[:, 0:1])
                        self.actf(sc, sc, AF.Exp, r=[sck], w=[sck], scale=-0.5)
                        self.stt(yn[:, :], yt[:, :], sc, gn[:, gs], ALU.mult, ALU.mult, r=[yt, sck, gn], w=[yn])
                        for c4 in range(4):
                            self.tr(self.psb(7)[:, c4 * P:(c4 + 1) * P], yn[:, c4 * P:(c4 + 1) * P], ident_b[:, :],
                                    r=[yn, ident_b], w=[PSK(7)])
                        self.cp("act", mixs[:, g * 4:(g + 1) * 4, csl], self.psb(7)[:, 0:512].rearrange("p (a b) -> p a b", a=4),
                                r=[PSK(7)], w=[(mixs, g, tb)])
                        self.tt("pool", v3(st_f[:, gs]), v3(st_f[:, gs]), bc(sm["dec"][:, tb, es_], HD), ALU.mult,
                                r=[(st_f, g), sm["dec"]], w=[(st_f, g)])
                        self.tt("dve", st_f[:, gs], self.ps(6), st_f[:, gs], ALU.add, r=[PSK(6), (st_f, g)], w=[(st_f, g)])
                        self.cp("act", st_b[:, gs], st_f[:, gs], r=[(st_f, g)], w=[(st_b, g)])
                self.dm(self.mixT[s, :, 0:8, tsl], mixs[:, :, :],
                        r=[(mixs, g, tb) for g in range(2) for tb in range(4)], w=[("mixT", s, "ssd", t)])


def run(inputs, cfg, n_cores):
    b = Builder(cfg)
    nc = b.build()
    names = [n for n in b.dram if n in inputs]
    in_maps = []
    for c in range(n_cores):
        m = {}
        for n in names:
            a = inputs[n]
            if n in ("x", "mem", "positions"):
                a = np.ascontiguousarray(a[c * cfg.NS:(c + 1) * cfg.NS])
            m[n] = a
        in_maps.append(m)
    res = run_bass_kernel_spmd(nc, in_maps, core_ids=list(range(n_cores)))
    return np.concatenate([r["out"] for r in res.results], axis=0)


def kernel(**inputs):
    inputs = {k: np.asarray(v) for k, v in inputs.items()}
    cfg = Cfg(NS=2, S=2048, L=4)
    return run(inputs, cfg, 8)
```
